# Optimizing a Trainium2 kernel written in Bass

```python
import math
import jax
import jax.numpy as jnp
from jax import lax
import numpy as np

D_MODEL = 1024
BATCH = 16
SEQ = 2048
DEPTH = 4

F32 = jnp.float32
N_MIXERS = 4
GRID_W = 64
MEM_LEN = 256
NORM_EPS = 1e-6

ATT_HEADS = 16
ATT_KV_HEADS = 4
ATT_HEAD_DIM = D_MODEL // ATT_HEADS
ATT_Q_W = ATT_HEADS * ATT_HEAD_DIM
ATT_KV_W = ATT_KV_HEADS * ATT_HEAD_DIM
ROPE_THETA = 10000.0
WINDOW = 128
ATT_BLOCK = 128

NA_ROWS_MAX = 8
NA_COLS = 16
NA_QCOLS = 16
NA_KCOLS = NA_QCOLS + NA_COLS

X_HEADS = 4
X_HEAD_DIM = 128
X_W = X_HEADS * X_HEAD_DIM

HG_KEY_DIM = 128
HG_HEADS = D_MODEL // HG_KEY_DIM
HG_VAL_DIM = D_MODEL // HG_HEADS
HG_W = HG_HEADS * HG_KEY_DIM
HG_CHUNK = 16

SSM_INNER = 2 * D_MODEL
SSM_HEAD_DIM = 64
SSM_HEADS = SSM_INNER // SSM_HEAD_DIM
SSM_GROUPS = 8
SSM_HEADS_PER_GROUP = SSM_HEADS // SSM_GROUPS
SSM_STATE = 128
SSM_CONV = 5
SSM_CHUNK = 64
SSM_CONV_DIM = SSM_INNER + 2 * SSM_GROUPS * SSM_STATE

D_FF = 2816
FFN_CONV = 3

A_IN_W = ATT_Q_W + 2 * ATT_KV_W + X_W
B_IN_W = 5 * HG_W + X_W
C_IN_W = SSM_INNER + SSM_CONV_DIM + 2 * SSM_HEADS + X_W
D_IN_W = ATT_Q_W + 2 * ATT_KV_W + X_W

kernel_name = 'hybrid_bidir_interleaved_encoder'


def _layers_of(m):
    return (DEPTH - m + N_MIXERS - 1) // N_MIXERS


def rmsnorm(x, g):
    xf = x.astype(F32)
    y = xf * lax.rsqrt(jnp.mean(xf * xf, axis=-1, keepdims=True) + NORM_EPS)
    return (y * g.astype(F32)).astype(x.dtype)


def flip_t(a):
    return jnp.flip(a, axis=1)


def rope(x, pos):
    half = x.shape[-1] // 2
    inv = ROPE_THETA ** (-jnp.arange(half, dtype=F32) / half)
    ang = pos.astype(F32)[:, None] * inv[None, :]
    cos = jnp.cos(ang)[None, :, None, :]
    sin = jnp.sin(ang)[None, :, None, :]
    xf = x.astype(F32)
    x1, x2 = xf[..., :half], xf[..., half:]
    return jnp.concatenate([x1 * cos - x2 * sin, x2 * cos + x1 * sin], axis=-1).astype(x.dtype)


def dwconv_centred(x, w, b):
    k, c = w.shape
    y = lax.conv_general_dilated(x, w[:, None, :].astype(x.dtype), window_strides=(1,),
                                 padding=[(k // 2, k // 2)],
                                 dimension_numbers=('NWC', 'WIO', 'NWC'),
                                 feature_group_count=c)
    return y + b.astype(x.dtype)


def memory_cross_attn(xq, mem_kv):
    bsz, t, _ = xq.shape
    q = xq.reshape(bsz, t, X_HEADS, X_HEAD_DIM)
    k, v = jnp.split(mem_kv, 2, axis=-1)
    k = k.reshape(bsz, -1, X_HEADS, X_HEAD_DIM)
    v = v.reshape(bsz, -1, X_HEADS, X_HEAD_DIM)
    s = jnp.einsum('bthd,bmhd->bhtm', q, k, preferred_element_type=F32) * (X_HEAD_DIM ** -0.5)
    p = jax.nn.softmax(s, axis=-1)
    o = jnp.einsum('bhtm,bmhd->bthd', p.astype(v.dtype), v)
    return o.reshape(bsz, t, X_W)


def window_gqa_sink(q, k, v, sink):
    bsz, t, hq, d = q.shape
    hkv = k.shape[2]
    grp = hq // hkv
    nb = t // ATT_BLOCK
    band = ATT_BLOCK + 2 * WINDOW
    pad = ((0, 0), (WINDOW, WINDOW), (0, 0), (0, 0))
    idx = np.arange(nb)[:, None] * ATT_BLOCK + np.arange(band)[None, :]
    kb = jnp.pad(k, pad)[:, idx]
    vb = jnp.pad(v, pad)[:, idx]
    qb = q.reshape(bsz, nb, ATT_BLOCK, hkv, grp, d)
    s = jnp.einsum('bnqhgd,bnkhd->bnhgqk', qb, kb, preferred_element_type=F32) * (d ** -0.5)
    qpos = np.arange(nb)[:, None] * ATT_BLOCK + np.arange(ATT_BLOCK)[None, :]
    kpos = idx - WINDOW
    valid = ((np.abs(qpos[:, :, None] - kpos[:, None, :]) <= WINDOW)
             & (kpos >= 0)[:, None, :] & (kpos < t)[:, None, :])
    s = jnp.where(valid[None, :, None, None], s, -jnp.inf)
    sk = sink.astype(F32).reshape(hkv, grp)[None, None, :, :, None, None]
    m = jnp.maximum(jnp.max(s, axis=-1, keepdims=True), sk)
    p = jnp.exp(s - m)
    p = p / (jnp.sum(p, axis=-1, keepdims=True) + jnp.exp(sk - m))
    o = jnp.einsum('bnhgqk,bnkhd->bnqhgd', p.astype(v.dtype), vb)
    return o.reshape(bsz, t, hq, d)


def neighbourhood_attn(q, k, v, rpb):
    bsz, t, hq, d = q.shape
    hkv = k.shape[2]
    grp = hq // hkv
    rows = t // GRID_W
    kr = min(NA_ROWS_MAX, rows)
    ncb = GRID_W // NA_QCOLS
    r = np.arange(rows)
    key_rows = np.clip(r - kr // 2, 0, rows - kr)[:, None] + np.arange(kr)[None, :]
    key_cols = (np.clip(np.arange(ncb) * NA_QCOLS - NA_COLS // 2, 0, GRID_W - NA_KCOLS)[:, None]
                + np.arange(NA_KCOLS)[None, :])
    key_idx = (key_rows[:, None, :, None] * GRID_W + key_cols[None, :, None, :]).reshape(rows, ncb, kr * NA_KCOLS)
    kg = k[:, key_idx]
    vg = v[:, key_idx]
    qb = q.reshape(bsz, rows, ncb, NA_QCOLS, hkv, grp, d)
    s = jnp.einsum('brjqhgd,brjkhd->brjhgqk', qb, kg, preferred_element_type=F32) * (d ** -0.5)
    qcol = np.arange(ncb)[:, None] * NA_QCOLS + np.arange(NA_QCOLS)[None, :]
    col_start = np.clip(qcol - NA_COLS // 2, 0, GRID_W - NA_COLS)
    kcol = np.repeat(key_cols[:, None, :], kr, axis=1).reshape(ncb, -1)
    krow = np.repeat(key_rows[:, :, None], NA_KCOLS, axis=2).reshape(rows, -1)
    in_win = ((kcol[:, None, :] >= col_start[..., None])
              & (kcol[:, None, :] < col_start[..., None] + NA_COLS))
    dr = krow - r[:, None] + NA_ROWS_MAX - 1
    dc = np.clip(kcol[:, None, :] - qcol[..., None] + NA_COLS - 1, 0, 2 * NA_COLS - 2)
    bias = rpb.astype(F32)[:, dr[:, None, None, :], dc[None]]
    bias = jnp.where(in_win[None, None], bias, -jnp.inf)
    bias = bias.reshape(hkv, grp, rows, ncb, NA_QCOLS, -1).transpose(2, 3, 0, 1, 4, 5)
    p = jax.nn.softmax(s + bias[None], axis=-1)
    o = jnp.einsum('brjhgqk,brjkhd->brjqhgd', p.astype(v.dtype), vg)
    return o.reshape(bsz, t, hq, d)


def gla_chunked(q, k, v, log_f):
    bsz, t, h, dk = q.shape
    dv = v.shape[-1]
    n = t // HG_CHUNK
    q, k, log_f = (a.reshape(bsz, n, HG_CHUNK, h, dk) for a in (q, k, log_f))
    v = v.reshape(bsz, n, HG_CHUNK, h, dv)
    b = jnp.cumsum(log_f, axis=2)
    b_end = b[:, :, -1:]
    q_dec = q * jnp.exp(b)
    k_dec = k * jnp.exp(-b)
    k_end = k * jnp.exp(b_end - b)
    causal = np.tril(np.ones((HG_CHUNK, HG_CHUNK), dtype=bool))
    s = jnp.where(causal, jnp.einsum('bcthk,bcshk->bchts', q_dec, k_dec), 0.0)
    o_intra = jnp.einsum('bchts,bcshv->bcthv', s, v)

    def step(state, inp):
        qd, ke, vc, dec = inp
        o = jnp.einsum('bthk,bhkv->bthv', qd, state)
        state = dec[..., None] * state + jnp.einsum('bshk,bshv->bhkv', ke, vc)
        return state, o

    xs = tuple(jnp.moveaxis(a, 1, 0) for a in (q_dec, k_end, v, jnp.exp(b_end[:, :, 0])))
    _, o_inter = lax.scan(step, jnp.zeros((bsz, h, dk, dv), F32), xs)
    return (o_intra + jnp.moveaxis(o_inter, 0, 1)).reshape(bsz, t, h, dv)


def ssd_chunked(x, dt, a, bm, cm):
    bsz, t, g, j, p = x.shape
    nst = bm.shape[-1]
    n = t // SSM_CHUNK
    x = x.reshape(bsz, n, SSM_CHUNK, g, j, p)
    dt = dt.reshape(bsz, n, SSM_CHUNK, g, j)
    bm = bm.reshape(bsz, n, SSM_CHUNK, g, nst)
    cm = cm.reshape(bsz, n, SSM_CHUNK, g, nst)
    cs = jnp.cumsum(dt * a, axis=2)
    xdt = x * dt[..., None]
    cs_t = jnp.moveaxis(cs, 2, -1)
    causal = np.tril(np.ones((SSM_CHUNK, SSM_CHUNK), dtype=bool))
    decay = jnp.exp(jnp.where(causal, cs_t[..., :, None] - cs_t[..., None, :], -jnp.inf))
    cb = jnp.einsum('bctgn,bcsgn->bcgts', cm, bm)
    y_diag = jnp.einsum('bcgjts,bcsgjp->bctgjp', cb[:, :, :, None] * decay, xdt)

    def step(state, inp):
        cq, bk, xd, c = inp
        c_end = c[:, -1]
        y_off = jnp.einsum('btgn,bgjpn->btgjp', cq, state) * jnp.exp(c)[..., None]
        w = jnp.exp(c_end[:, None] - c)[..., None]
        state = jnp.exp(c_end)[..., None, None] * state + jnp.einsum('bsgn,bsgjp->bgjpn', bk, xd * w)
        return state, y_off

    xs = tuple(jnp.moveaxis(arr, 1, 0) for arr in (cm, bm, xdt, cs))
    _, y_off = lax.scan(step, jnp.zeros((bsz, g, j, p, nst), F32), xs)
    return (y_diag + jnp.moveaxis(y_off, 0, 1)).reshape(bsz, t, g, j, p)


def hgrn_lower_bounds(logits):
    p = jax.nn.softmax(logits.astype(F32), axis=1)
    return jnp.cumsum(p, axis=1) - p[:, :1]


def mixer_window_attn(u, mem_kv, w_in, sink, w_out, pos):
    bsz, t, _ = u.shape
    q, k, v, xq = jnp.split(u @ w_in, [ATT_Q_W, ATT_Q_W + ATT_KV_W, ATT_Q_W + 2 * ATT_KV_W], axis=-1)
    q = rope(q.reshape(bsz, t, ATT_HEADS, ATT_HEAD_DIM), pos)
    k = rope(k.reshape(bsz, t, ATT_KV_HEADS, ATT_HEAD_DIM), pos)
    v = v.reshape(bsz, t, ATT_KV_HEADS, ATT_HEAD_DIM)
    o = window_gqa_sink(q, k, v, sink).reshape(bsz, t, ATT_Q_W)
    return jnp.concatenate([o, memory_cross_attn(xq, mem_kv)], axis=-1) @ w_out


def mixer_hgrn2(u, mem_kv, w_in, lb_fwd, lb_bwd, norm_g, w_out):
    bsz, t, _ = u.shape
    q, i, zf, zb, gate, xq = jnp.split(u @ w_in, [HG_W, 2 * HG_W, 3 * HG_W, 4 * HG_W, 5 * HG_W], axis=-1)
    shp = (bsz, t, HG_HEADS, HG_KEY_DIM)
    q = jax.nn.silu(q.astype(F32)).reshape(shp)
    v = i.astype(F32).reshape(bsz, t, HG_HEADS, HG_VAL_DIM)

    def forget(z, lb):
        f = lb + (1.0 - lb) * jax.nn.sigmoid(z.astype(F32))
        return jnp.log(f).reshape(shp), (1.0 - f).reshape(shp)

    lf_f, k_f = forget(zf, lb_fwd)
    lf_b, k_b = forget(zb, lb_bwd)
    o = (gla_chunked(q, k_f, v, lf_f)
         + flip_t(gla_chunked(flip_t(q), flip_t(k_b), flip_t(v), flip_t(lf_b))))
    o = rmsnorm(o, norm_g.reshape(HG_HEADS, HG_VAL_DIM)).reshape(bsz, t, HG_W).astype(u.dtype)
    o = o * jax.nn.silu(gate)
    return jnp.concatenate([o, memory_cross_attn(xq, mem_kv)], axis=-1) @ w_out


def mixer_mamba2(u, mem_kv, w_in, conv_w, conv_b, dt_bias, a_log, d_skip, norm_g, w_out):
    bsz, t, _ = u.shape
    z, xbc, dt_raw, xq = jnp.split(
        u @ w_in, [SSM_INNER, SSM_INNER + SSM_CONV_DIM, SSM_INNER + SSM_CONV_DIM + 2 * SSM_HEADS], axis=-1)
    xbc = jax.nn.silu(dwconv_centred(xbc, conv_w, conv_b)).astype(F32)
    xs, bm, cm = jnp.split(xbc, [SSM_INNER, SSM_INNER + SSM_GROUPS * SSM_STATE], axis=-1)
    xs = xs.reshape(bsz, t, SSM_GROUPS, SSM_HEADS_PER_GROUP, SSM_HEAD_DIM)
    bm = bm.reshape(bsz, t, SSM_GROUPS, SSM_STATE)
    cm = cm.reshape(bsz, t, SSM_GROUPS, SSM_STATE)
    dt = jax.nn.softplus(dt_raw.astype(F32).reshape(bsz, t, 2, SSM_HEADS) + dt_bias.astype(F32))
    dt = dt.reshape(bsz, t, 2, SSM_GROUPS, SSM_HEADS_PER_GROUP)
    a = -jnp.exp(a_log.astype(F32)).reshape(2, SSM_GROUPS, SSM_HEADS_PER_GROUP)
    y = (ssd_chunked(xs, dt[:, :, 0], a[0], bm, cm)
         + flip_t(ssd_chunked(flip_t(xs), flip_t(dt[:, :, 1]), a[1], flip_t(bm), flip_t(cm))))
    y = y + d_skip.astype(F32).reshape(SSM_GROUPS, SSM_HEADS_PER_GROUP, 1) * xs
    y = y.reshape(bsz, t, SSM_INNER) * jax.nn.silu(z.astype(F32))
    y = rmsnorm(y.reshape(bsz, t, SSM_GROUPS, -1), norm_g.reshape(SSM_GROUPS, -1))
    y = y.reshape(bsz, t, SSM_INNER).astype(u.dtype)
    return jnp.concatenate([y, memory_cross_attn(xq, mem_kv)], axis=-1) @ w_out


def mixer_neighbourhood(u, mem_kv, w_in, rpb, w_out):
    bsz, t, _ = u.shape
    q, k, v, xq = jnp.split(u @ w_in, [ATT_Q_W, ATT_Q_W + ATT_KV_W, ATT_Q_W + 2 * ATT_KV_W], axis=-1)
    q = q.reshape(bsz, t, ATT_HEADS, ATT_HEAD_DIM)
    k = k.reshape(bsz, t, ATT_KV_HEADS, ATT_HEAD_DIM)
    v = v.reshape(bsz, t, ATT_KV_HEADS, ATT_HEAD_DIM)
    o = neighbourhood_attn(q, k, v, rpb).reshape(bsz, t, ATT_Q_W)
    return jnp.concatenate([o, memory_cross_attn(xq, mem_kv)], axis=-1) @ w_out


def conv_ffn(v, w_in, conv_w, conv_b, w_out):
    gate, up = jnp.split(v @ w_in, 2, axis=-1)
    hid = jax.nn.gelu(dwconv_centred(gate, conv_w, conv_b)) * up
    return hid @ w_out


def setup_inputs(seed: int = 0) -> dict:
    key = jax.random.key(seed)
    ks = jax.random.split(key, 32)
    n_a, n_b, n_c, n_d = (_layers_of(m) for m in range(N_MIXERS))

    def nrm(k, shape, scale):
        return jax.random.normal(k, shape, F32) * scale

    def gain(k, shape):
        return 1.0 + 0.05 * jax.random.normal(k, shape, F32)

    dt0 = jnp.exp(jax.random.uniform(ks[16], (n_c, 2, SSM_HEADS), F32, math.log(1e-3), math.log(1e-1)))
    return {
        'x': nrm(ks[0], (BATCH, SEQ, D_MODEL), 1.0),
        'mem': nrm(ks[1], (BATCH, MEM_LEN, D_MODEL), 1.0),
        'norm_g': gain(ks[2], (DEPTH, 4, D_MODEL)),
        'mem_norm_g': gain(ks[3], (D_MODEL,)),
        'w_mem_kv': nrm(ks[4], (DEPTH, D_MODEL, 2 * X_W), D_MODEL ** -0.5),
        'a_w_in': nrm(ks[5], (n_a, D_MODEL, A_IN_W), D_MODEL ** -0.5),
        'a_sink': nrm(ks[6], (n_a, ATT_HEADS), 0.5),
        'a_w_out': nrm(ks[7], (n_a, ATT_Q_W + X_W, D_MODEL), (ATT_Q_W + X_W) ** -0.5),
        'b_w_in': nrm(ks[8], (n_b, D_MODEL, B_IN_W), D_MODEL ** -0.5),
        'b_lb_logits': nrm(ks[9], (2, DEPTH, HG_W), 0.1),
        'b_norm_g': gain(ks[10], (n_b, HG_W)),
        'b_w_out': nrm(ks[11], (n_b, HG_W + X_W, D_MODEL), (HG_W + X_W) ** -0.5),
        'c_w_in': nrm(ks[12], (n_c, D_MODEL, C_IN_W), D_MODEL ** -0.5),
        'c_conv_w': nrm(ks[13], (n_c, SSM_CONV, SSM_CONV_DIM), SSM_CONV ** -0.5),
        'c_conv_b': nrm(ks[14], (n_c, SSM_CONV_DIM), 0.02),
        'c_dt_bias': dt0 + jnp.log(-jnp.expm1(-dt0)),
        'c_a_log': jnp.log(jax.random.uniform(ks[15], (n_c, 2, SSM_HEADS), F32, 1.0, 16.0)),
        'c_d': 1.0 + 0.1 * jax.random.normal(ks[17], (n_c, SSM_HEADS), F32),
        'c_norm_g': gain(ks[18], (n_c, SSM_INNER)),
        'c_w_out': nrm(ks[19], (n_c, SSM_INNER + X_W, D_MODEL), (SSM_INNER + X_W) ** -0.5),
        'd_w_in': nrm(ks[20], (n_d, D_MODEL, D_IN_W), D_MODEL ** -0.5),
        'd_rpb': nrm(ks[21], (n_d, ATT_HEADS, 2 * NA_ROWS_MAX - 1, 2 * NA_COLS - 1), 0.1),
        'd_w_out': nrm(ks[22], (n_d, ATT_Q_W + X_W, D_MODEL), (ATT_Q_W + X_W) ** -0.5),
        'ffn_w_in': nrm(ks[23], (DEPTH, D_MODEL, 2 * D_FF), D_MODEL ** -0.5),
        'ffn_conv_w': nrm(ks[24], (DEPTH, FFN_CONV, D_FF), FFN_CONV ** -0.5),
        'ffn_conv_b': nrm(ks[25], (DEPTH, D_FF), 0.02),
        'ffn_w_out': nrm(ks[26], (DEPTH, D_FF, D_MODEL), D_FF ** -0.5),
    }


def reference(x, mem, norm_g, mem_norm_g, w_mem_kv, a_w_in, a_sink, a_w_out,
              b_w_in, b_lb_logits, b_norm_g, b_w_out,
              c_w_in, c_conv_w, c_conv_b, c_dt_bias, c_a_log, c_d, c_norm_g, c_w_out,
              d_w_in, d_rpb, d_w_out, ffn_w_in, ffn_conv_w, ffn_conv_b, ffn_w_out):
    t = x.shape[1]
    pos = jnp.arange(t)
    mem_n = rmsnorm(mem, mem_norm_g)
    lb = hgrn_lower_bounds(b_lb_logits)
    h = x
    for layer in range(DEPTH):
        kind, slot = layer % N_MIXERS, layer // N_MIXERS
        u = rmsnorm(h, norm_g[layer, 0])
        mem_kv = mem_n @ w_mem_kv[layer]
        if kind == 0:
            y = mixer_window_attn(u, mem_kv, a_w_in[slot], a_sink[slot], a_w_out[slot], pos)
        elif kind == 1:
            y = mixer_hgrn2(u, mem_kv, b_w_in[slot], lb[0, layer], lb[1, layer], b_norm_g[slot], b_w_out[slot])
        elif kind == 2:
            y = mixer_mamba2(u, mem_kv, c_w_in[slot], c_conv_w[slot], c_conv_b[slot], c_dt_bias[slot],
                             c_a_log[slot], c_d[slot], c_norm_g[slot], c_w_out[slot])
        else:
            y = mixer_neighbourhood(u, mem_kv, d_w_in[slot], d_rpb[slot], d_w_out[slot])
        h = h + rmsnorm(y, norm_g[layer, 1])
        f = conv_ffn(rmsnorm(h, norm_g[layer, 2]), ffn_w_in[layer], ffn_conv_w[layer],
                     ffn_conv_b[layer], ffn_w_out[layer])
        h = h + rmsnorm(f, norm_g[layer, 3])
    return h
```

```python
import contextlib
import numpy as np
import concourse.bass as bass
import concourse.mybir as mybir
from concourse.bass_utils import run_bass_kernel_spmd

F32 = mybir.dt.float32
BF16 = mybir.dt.bfloat16
AF = mybir.ActivationFunctionType
ALU = mybir.AluOpType
AX = mybir.AxisListType

ENGS = ("pe", "act", "dve", "pool", "sp")
SEM_EPOCH = 30000
N_DMA_SEMS = 24


class Reg:
    __slots__ = ("name", "lw", "rd", "rd_dma")

    def __init__(self, name=""):
        self.name = name
        self.lw = None
        self.rd = {}
        self.rd_dma = []


class Op:
    __slots__ = ("eng", "fn", "deps", "signal", "sem", "val", "is_dma", "epoch", "prev")

    def __init__(self, eng, fn, is_dma, epoch):
        self.eng = eng
        self.fn = fn
        self.is_dma = is_dma
        self.epoch = epoch
        self.deps = []
        self.signal = False
        self.sem = None
        self.val = 0
        self.prev = 0


class Prog:
    def __init__(self, nc):
        self.nc = nc
        self.ops = {e: [] for e in ENGS}
        self.last = {e: None for e in ENGS}
        self.dma_since = []
        self.epoch = 0
        self.nops = 0

    def add(self, eng, fn, r=(), w=(), dma=False, extra_deps=()):
        op = Op(eng, fn, dma, self.epoch)
        deps = []
        ep = self.epoch

        def need(d, kind):
            if d is None or d.epoch < ep:
                return
            if d.is_dma or dma:
                deps.append(d)
                return
            if d.eng == eng and eng == "pe":
                return
            deps.append(d)

        for g in r:
            need(g.lw, "raw")
        for g in w:
            need(g.lw, "waw")
            for o in g.rd.values():
                need(o, "war")
            for o in g.rd_dma:
                need(o, "war")
        deps.extend(extra_deps)
        seen = set()
        for d in deps:
            if d is op or id(d) in seen:
                continue
            seen.add(id(d))
            op.deps.append(d)
            d.signal = True
        for g in w:
            g.lw = op
            g.rd = {}
            g.rd_dma = []
        for g in r:
            if dma:
                g.rd_dma.append(op)
            else:
                g.rd[eng] = op
        self.ops[eng].append(op)
        if dma:
            self.dma_since.append(op)
        elif fn is not None:
            self.last[eng] = op
        self.nops += 1
        return op

    def barrier(self):
        deps = [self.last[e] for e in ("pe", "act", "dve", "pool")
                if self.last[e] is not None and self.last[e].epoch == self.epoch]
        deps += self.dma_since
        b = self.add("sp", lambda e: e.nop(), extra_deps=deps)
        b.signal = True
        self.epoch += 1
        self.dma_since = []
        b.epoch = self.epoch
        for e in ("pe", "act", "dve", "pool"):
            self.add(e, None, extra_deps=[b])

    def finish(self):
        deps = [self.last[e] for e in ("pe", "act", "dve", "pool") if self.last[e] is not None]
        deps += self.dma_since
        self.add("sp", None, extra_deps=deps)

    def emit(self, stack):
        nc = self.nc
        for e in ENGS:
            pool = None
            cnts = None
            rr = 0
            cur = None
            cnt = 0
            for op in self.ops[e]:
                if op.is_dma:
                    if pool is None:
                        pool = [stack.enter_context(nc.semaphore(f"dq_{e}_{i}")) for i in range(N_DMA_SEMS)]
                        cnts = [0] * N_DMA_SEMS
                    i = rr % N_DMA_SEMS
                    rr += 1
                    op.sem = pool[i]
                    op.prev = cnts[i]
                    cnts[i] += 16
                    op.val = cnts[i]
                elif op.signal:
                    if cur is None or cnt >= SEM_EPOCH:
                        cur = stack.enter_context(nc.semaphore(f"s_{e}_{len(self.ops[e])}_{cnt}_{id(op) % 9973}"))
                        cnt = 0
                    cnt += 1
                    op.sem = cur
                    op.val = cnt

        def run(ename, eh):
            known = {}
            for op in self.ops[ename]:
                waits = {}
                for d in op.deps:
                    if known.get(d.sem, 0) >= d.val:
                        continue
                    if waits.get(d.sem, 0) < d.val:
                        waits[d.sem] = d.val
                if op.is_dma and op.prev > 0 and known.get(op.sem, 0) < op.prev:
                    if waits.get(op.sem, 0) < op.prev:
                        waits[op.sem] = op.prev
                for sem, val in waits.items():
                    eh.wait_ge(sem, val)
                    known[sem] = val
                if op.fn is not None:
                    ins = op.fn(eh)
                    if op.is_dma:
                        ins.then_inc(op.sem, 16)
                    elif op.signal:
                        ins.then_inc(op.sem, 1)

        with nc.Block() as block:
            @block.sync
            def _(e):
                run("sp", e)

            @block.tensor
            def _(e):
                run("pe", e)

            @block.scalar
            def _(e):
                run("act", e)

            @block.vector
            def _(e):
                run("dve", e)

            @block.gpsimd
            def _(e):
                run("pool", e)


D = 1024
T = 2048
KD = D // 128
NSEQ = 2
MEM = 256
DFF = 2816
NF = DFF // 128
EPS = 1e-6


class K:
    def __init__(self, nc, stack, dram):
        self.nc = nc
        self.p = Prog(nc)
        self.stack = stack
        self.dram = dram
        p = self.p
        self.ps = []
        self.ps_reg = []
        for i in range(8):
            t = stack.enter_context(nc.psum_tensor(f"ps{i}", [128, 512], F32))
            self.ps.append(t)
            self.ps_reg.append(Reg(f"ps{i}"))
        self.ps_rr = 0
        self.ARENA_W = 53200
        self.arena = stack.enter_context(nc.sbuf_tensor("arena", [128, self.ARENA_W], F32))
        self.arena_bf = self.arena.bitcast(BF16)
        self.bump = 0
        self.marks = []

    def alloc(self, n, dt=F32, name=""):
        words = (n + 1) // 2 if dt == BF16 else n
        words = (words + 15) // 16 * 16
        off = self.bump
        self.bump += words
        assert self.bump <= self.ARENA_W, f"arena overflow at {name}: {self.bump}"
        if dt == BF16:
            ap = self.arena_bf[:, 2 * off: 2 * off + n]
        else:
            ap = self.arena[:, off: off + n]
        return ap, Reg(name)

    def mark(self):
        self.marks.append(self.bump)

    def release(self):
        self.bump = self.marks.pop()

    def psum(self):
        i = self.ps_rr % 8
        self.ps_rr += 1
        return self.ps[i], self.ps_reg[i]

    def dma(self, out, in_, r=(), w=(), eng="sp"):
        return self.p.add(eng, lambda e: e.dma_start(out=out, in_=in_), r=r, w=w, dma=True)

    def mm(self, out, lhsT, rhs, start, stop, r=(), w=()):
        return self.p.add("pe", lambda e: e.matmul(out, lhsT, rhs, start=start, stop=stop), r=r, w=w)

    def tr(self, out, in_, ident, r=(), w=()):
        return self.p.add("pe", lambda e: e.transpose(out, in_, ident), r=r, w=w)

    def act(self, out, in_, func, r=(), w=(), bias=None, scale=None, eng="act"):
        kw = {}
        if bias is not None:
            kw["bias"] = bias
        if scale is not None:
            kw["scale"] = scale
        return self.p.add("act", lambda e: e.activation(out, in_, func, **kw), r=r, w=w)

    def v(self, eng, fn, r=(), w=()):
        return self.p.add(eng, fn, r=r, w=w)

    def ts(self, eng, out, in0, s1, s2, op0, op1, r=(), w=()):
        if s2 is None:
            return self.p.add(eng, lambda e: e.tensor_scalar(out, in0, s1, None, op0), r=r, w=w)
        return self.p.add(eng, lambda e: e.tensor_scalar(out, in0, s1, s2, op0, op1), r=r, w=w)

    def stt(self, out, in0, scalar, in1, op0, op1, r=(), w=()):
        return self.p.add("dve", lambda e: e.scalar_tensor_tensor(out=out, in0=in0, scalar=scalar, in1=in1,
                                                                  op0=op0, op1=op1), r=r, w=w)

    def tt(self, eng, out, in0, in1, op, r=(), w=()):
        return self.p.add(eng, lambda e: e.tensor_tensor(out=out, in0=in0, in1=in1, op=op), r=r, w=w)

    def cp(self, eng, out, in_, r=(), w=()):
        if eng == "act":
            return self.p.add("act", lambda e: e.copy(out, in_), r=r, w=w)
        return self.p.add(eng, lambda e: e.tensor_copy(out, in_), r=r, w=w)

    def ms(self, eng, out, val, w=()):
        return self.p.add(eng, lambda e: e.memset(out, val), w=w)

    def rcp(self, out, in_, r=(), w=()):
        return self.p.add("dve", lambda e: e.reciprocal(out, in_), r=r, w=w)


def _v3(ap, a):
    return ap.rearrange("p (a b) -> p a b", a=a)


class Ring:
    def __init__(self, items):
        self.items = items
        self.i = 0

    def next(self):
        it = self.items[self.i % len(self.items)]
        self.i += 1
        return it


def k_setup(self):
    dram = self.dram
    self.identf, self.identf_r = self.alloc(128, F32, "identf")
    self.identb, self.identb_r = self.alloc(128, BF16, "identb")
    self.onesm, self.onesm_r = self.alloc(128, BF16, "onesm")
    self.onesb, self.onesb_r = self.alloc(128, BF16, "onesb")
    self.dma(self.identf, dram["c_ident"], w=[self.identf_r])
    self.cp("dve", self.identb, self.identf, r=[self.identf_r], w=[self.identb_r])
    self.ms("pool", self.onesm, 1.0 / 1024.0, w=[self.onesm_r])
    self.ms("pool", self.onesb, 1.0, w=[self.onesb_r])
    self.epsc, self.epsc_r = self.alloc(16, F32, "epsc")
    self.ms("pool", self.epsc, EPS, w=[self.epsc_r])
    self.onec, self.onec_r = self.alloc(16, F32, "onec")
    self.ms("pool", self.onec, 1.0, w=[self.onec_r])
    self.catm_r = Reg("catm_dram")
    self.ng, self.ng_r = self.alloc(4 * 4 * KD, F32, "norm_g")
    self.dma(self.ng, dram["norm_g_t"], w=[self.ng_r])
    self.fcw, self.fcw_r = self.alloc(4 * 3 * NF, F32, "ffn_conv_w")
    self.dma(self.fcw, dram["ffn_conv_w_t"], w=[self.fcw_r])
    self.fcb, self.fcb_r = self.alloc(4 * NF, F32, "ffn_conv_b")
    self.dma(self.fcb, dram["ffn_conv_b_t"], w=[self.fcb_r])
    hT, _ = self.alloc(KD * T, F32, "hT")
    self.hT = _v3(hT, KD)
    self.hreg = [Reg(f"h{t}") for t in range(4)]
    self.wb = {}


def k_psum(self):
    i = self.ps_rr % 7
    self.ps_rr += 1
    return self.ps[i], self.ps_reg[i]


def k_wload(self, dst, dst_reg, src, nfree=None, shape3=None):
    self.dma(dst, src, w=[dst_reg])


def k_precast(self, names):
    self.mark()
    ring = Ring([self.alloc(2048, F32, f"pc_in{i}") for i in range(6)])
    ringo = Ring([self.alloc(2048, BF16, f"pc_out{i}") for i in range(6)])
    rr = 0
    for name in names:
        src = self.dram[name]
        shp = list(src.shape)
        dst_t = self.nc.dram_tensor(name + "_bf", shp, BF16)
        dst = dst_t.ap()
        self.wb[name] = dst
        if len(shp) == 3:
            src2 = src.rearrange("l r c -> (l r) c")
            dst2 = dst.rearrange("l r c -> (l r) c")
        else:
            src2, dst2 = src, dst
        R, C = src2.shape
        assert R % 128 == 0
        for rb in range(R // 128):
            for c0 in range(0, C, 2048):
                cw = min(2048, C - c0)
                a, a_r = ring.next()
                b, b_r = ringo.next()
                self.dma(a[:, 0:cw], src2[rb * 128:(rb + 1) * 128, c0:c0 + cw], w=[a_r])
                eng = ("act", "dve", "act", "dve", "pool")[rr % 5]
                rr += 1
                self.cp(eng, b[:, 0:cw], a[:, 0:cw], r=[a_r], w=[b_r])
                self.dma(dst2[rb * 128:(rb + 1) * 128, c0:c0 + cw], b[:, 0:cw], r=[b_r])
    self.release()


def k_load_seq(self, s):
    x = self.dram["x"]
    ring = Ring([self.alloc(1024, F32, f"xin{i}") for i in range(2)])
    for tb in range(16):
        xt, xr = ring.next()
        self.dma(xt, x[s, tb * 128:(tb + 1) * 128, :], w=[xr])
        for half in range(2):
            ps, pr = self.psum()
            for j in range(4):
                k = half * 4 + j
                self.tr(ps[:, j * 128:(j + 1) * 128], xt[:, k * 128:(k + 1) * 128], self.identf,
                        r=[xr, self.identf_r], w=[pr])
            dst = self.hT[:, half * 4:half * 4 + 4, tb * 128:(tb + 1) * 128]
            src = _v3(ps[:, 0:512], 4)
            self.cp("act" if half == 0 else "dve", dst, src, r=[pr], w=[self.hreg[tb // 4]])


def k_store_seq(self, s):
    y = self.dram["y"]
    ring = Ring([self.alloc(1024, F32, f"xout{i}") for i in range(2)])
    for tb in range(16):
        xt, xr = ring.next()
        for half in range(2):
            ps, pr = self.psum()
            for j in range(4):
                k = half * 4 + j
                self.tr(ps[:, j * 128:(j + 1) * 128], self.hT[:, k, tb * 128:(tb + 1) * 128], self.identf,
                        r=[self.hreg[tb // 4], self.identf_r], w=[pr])
            self.cp("act" if half == 0 else "dve", xt[:, half * 512:(half + 1) * 512], ps[:, 0:512], r=[pr], w=[xr])
        self.dma(y[s, tb * 128:(tb + 1) * 128, :], xt, r=[xr])


def k_rms_stats(self, sq, sq_r, n):
    ps, pr = self.psum()
    for k in range(KD):
        self.mm(ps[:, 0:n], self.onesm, sq[:, k, :], k == 0, k == KD - 1, r=[sq_r, self.onesm_r], w=[pr])
    rs, rs_r = self.rs_ring.next()
    self.act(rs[:, 0:n], ps[:, 0:n], AF.Ln, r=[pr, self.epsc_r], w=[rs_r], bias=self.epsc[:, 0:1])
    self.act(rs[:, 0:n], rs[:, 0:n], AF.Exp, r=[rs_r], w=[rs_r], scale=-0.5)
    return rs, rs_r


def _hregs(self, lo, hi):
    lo = max(lo, 0)
    hi = min(hi, T)
    return [self.hreg[b] for b in range(lo // 512, (hi - 1) // 512 + 1)]


def k_norm_u(self, t_lo, t_hi, gofs, U, U_r):
    n = t_hi - t_lo
    a = max(t_lo, 0)
    b = min(t_hi, T)
    ja, jb = a - t_lo, b - t_lo
    hr = _hregs(self, a, b)
    if ja > 0:
        self.ms("pool", U[:, :, 0:ja], 0.0, w=[U_r])
    if jb < n:
        self.ms("pool", U[:, :, jb:n], 0.0, w=[U_r])
    sq, sq_r = self.sq_ring.next()
    m = b - a
    self.act(sq[:, :, 0:m], self.hT[:, :, a:b], AF.Square, r=hr, w=[sq_r])
    rs, rs_r = self.rms_stats(sq[:, :, 0:m], sq_r, m)
    for k in range(KD):
        self.stt(U[:, k, ja:jb], self.hT[:, k, a:b], self.ng[:, gofs + k:gofs + k + 1], rs[:, 0:m],
                 ALU.mult, ALU.mult, r=hr + [rs_r, self.ng_r], w=[U_r])


def k_resid_norm(self, t0, n, ybuf, y_r, sq, sq_r, gofs):
    sl = slice(t0, t0 + n)
    hr = _hregs(self, t0, t0 + n)
    rs, rs_r = self.rms_stats(sq[:, :, 0:n], sq_r, n)
    for k in range(KD):
        self.stt(ybuf[:, k, 0:n], ybuf[:, k, 0:n], self.ng[:, gofs + k:gofs + k + 1], rs[:, 0:n],
                 ALU.mult, ALU.mult, r=[y_r, rs_r, self.ng_r], w=[y_r])
        self.tt("pool", self.hT[:, k, sl], self.hT[:, k, sl], ybuf[:, k, 0:n], ALU.add, r=[y_r] + hr, w=hr)


GELU_MODE = "tanh_act"
FFN_TILES = [(0, 510), (510, 510), (1020, 510), (1530, 510), (2040, 8)]


def k_ffn(self, l):
    self.mark()
    self.sq_ring = Ring([(_v3(a, KD), r) for a, r in [self.alloc(KD * 512, BF16, "sq")]])
    self.rs_ring = Ring([self.alloc(512, F32, f"rs{i}") for i in range(2)])
    Us = []
    for i in range(2):
        a, r = self.alloc(KD * 512, BF16, f"U{i}")
        Us.append((_v3(a, KD), r))
    hid_a, hid_r = self.alloc(NF * 512, BF16, "hid")
    hid = _v3(hid_a, NF)
    y_a, y_r = self.alloc(KD * 512, F32, "yffn")
    ybuf = _v3(y_a, KD)
    acc_ring = Ring([self.alloc(512, F32, f"acc{i}") for i in range(2)])
    gl_ring = Ring([self.alloc(512, F32, f"gl{i}") for i in range(2)])
    ub_ring = Ring([self.alloc(512, F32, f"ub{i}") for i in range(2)])
    wg_ring = Ring([(lambda ar: (_v3(ar[0], KD), ar[1]))(self.alloc(KD * 256, BF16, f"wg{i}")) for i in range(2)])
    wu_ring = Ring([(lambda ar: (_v3(ar[0], KD), ar[1]))(self.alloc(KD * 256, BF16, f"wu{i}")) for i in range(2)])
    wo_ring = Ring([(lambda ar: (_v3(ar[0], 2), ar[1]))(self.alloc(2 * 512, BF16, f"wo{i}")) for i in range(3)])
    w_in = self.wb["ffn_w_in"]
    w_out = self.wb["ffn_w_out"]
    g2 = (l * 4 + 2) * KD
    g3 = (l * 4 + 3) * KD
    cw0 = (l * 3 + 0) * NF
    cw1 = (l * 3 + 1) * NF
    cw2 = (l * 3 + 2) * NF
    cb = l * NF
    tiles = self.cfg.get("ffn_tiles", FFN_TILES)

    def make_u(i):
        t0_, n_ = tiles[i]
        U_, U_r_ = Us[i % 2]
        self.norm_u(t0_ - 1, t0_ + n_ + 1, g2, U_, U_r_)
        return U_, U_r_

    cur = make_u(0)
    for ti, (t0, n) in enumerate(tiles):
        U, U_r = cur
        nxt = make_u(ti + 1) if ti + 1 < len(tiles) else None
        for fb in range(NF // 2):
            c0 = fb * 256
            wg, wg_r = wg_ring.next()
            wu, wu_r = wu_ring.next()
            self.wload(wg, wg_r, w_in[l, :, c0:c0 + 256].rearrange("(k p) n -> p k n", p=128), KD * 256, KD)
            self.wload(wu, wu_r, w_in[l, :, DFF + c0:DFF + c0 + 256].rearrange("(k p) n -> p k n", p=128),
                       KD * 256, KD)
            for j in range(2):
                f = fb * 2 + j
                psg, psg_r = self.psum()
                psu, psu_r = self.psum()
                for k in range(KD):
                    self.mm(psg[:, 0:n + 2], wg[:, k, j * 128:(j + 1) * 128], U[:, k, 0:n + 2], k == 0, k == KD - 1,
                            r=[wg_r, U_r], w=[psg_r])
                for k in range(KD):
                    self.mm(psu[:, 0:n], wu[:, k, j * 128:(j + 1) * 128], U[:, k, 1:n + 1], k == 0, k == KD - 1,
                            r=[wu_r, U_r], w=[psu_r])
                acc, acc_r = acc_ring.next()
                gl, gl_r = gl_ring.next()
                ub, ub_r = ub_ring.next()
                A = acc[:, 0:n]
                G = gl[:, 0:n]
                self.act(A, psg[:, 1:n + 1], AF.Identity, r=[psg_r, self.fcw_r, self.fcb_r], w=[acc_r],
                         bias=self.fcb[:, cb + f:cb + f + 1], scale=self.fcw[:, cw1 + f:cw1 + f + 1])
                self.stt(A, psg[:, 0:n], self.fcw[:, cw0 + f:cw0 + f + 1], A, ALU.mult, ALU.add,
                         r=[psg_r, acc_r, self.fcw_r], w=[acc_r])
                self.stt(A, psg[:, 2:n + 2], self.fcw[:, cw2 + f:cw2 + f + 1], A, ALU.mult, ALU.add,
                         r=[psg_r, acc_r, self.fcw_r], w=[acc_r])
                self.act(G, A, AF.Gelu_apprx_tanh, r=[acc_r], w=[gl_r])
                self.cp("act", ub[:, 0:n], psu[:, 0:n], r=[psu_r], w=[ub_r])
                self.tt("pool", hid[:, f, 0:n], G, ub[:, 0:n], ALU.mult, r=[gl_r, ub_r], w=[hid_r])
        sq, sq_r = self.sq_ring.next()
        for half in range(2):
            banks = [self.psum() for _ in range(4)]
            for fb in range(NF // 2):
                wo, wo_r = wo_ring.next()
                self.wload(wo, wo_r,
                           w_out[l, fb * 256:(fb + 1) * 256, half * 512:(half + 1) * 512].rearrange(
                               "(j p) n -> p j n", p=128), 2 * 512, 2)
                for j in range(2):
                    f = fb * 2 + j
                    for q in range(4):
                        self.mm(banks[q][0][:, 0:n], wo[:, j, q * 128:(q + 1) * 128], hid[:, f, 0:n],
                                f == 0, f == NF - 1, r=[wo_r, hid_r], w=[banks[q][1]])
            for q in range(4):
                dk = half * 4 + q
                self.cp("act", ybuf[:, dk, 0:n], banks[q][0][:, 0:n], r=[banks[q][1]], w=[y_r])
                self.act(sq[:, dk, 0:n], banks[q][0][:, 0:n], AF.Square, r=[banks[q][1]], w=[sq_r])
        self.resid_norm(t0, n, ybuf, y_r, sq, sq_r, g3)
        cur = nxt
    self.release()


K.setup = k_setup
K.psum = k_psum
K.wload = k_wload
K.precast = k_precast
K.load_seq = k_load_seq
K.store_seq = k_store_seq
K.rms_stats = k_rms_stats
K.norm_u = k_norm_u
K.resid_norm = k_resid_norm
K.ffn = k_ffn


XW = 512
XSCALE = 128.0 ** -0.5
ASCALE = 64.0 ** -0.5


def k_setup_seq_consts(self):
    d = self.dram
    self.memn_a, self.memn_r = self.alloc(KD * MEM, BF16, "memn")
    self.memn = _v3(self.memn_a, KD)
    self.mng, self.mng_r = self.alloc(KD, F32, "mem_norm_g")
    self.dma(self.mng, d["mem_norm_g_t"], w=[self.mng_r])


def k_mem_norm(self, s):
    self.mark()
    self.sq_ring = Ring([(_v3(a, KD), r) for a, r in [self.alloc(KD * 512, BF16, "sq")]])
    self.rs_ring = Ring([self.alloc(512, F32, f"rs{i}") for i in range(2)])
    mT_a, mT_r = self.alloc(KD * MEM, F32, "memT")
    mT = _v3(mT_a, KD)
    ring = Ring([self.alloc(1024, F32, f"min{i}") for i in range(2)])
    for mb in range(2):
        xt, xr = ring.next()
        self.dma(xt, self.dram["mem"][s, mb * 128:(mb + 1) * 128, :], w=[xr])
        for half in range(2):
            ps, pr = self.psum()
            for j in range(4):
                k = half * 4 + j
                self.tr(ps[:, j * 128:(j + 1) * 128], xt[:, k * 128:(k + 1) * 128], self.identf,
                        r=[xr, self.identf_r], w=[pr])
            self.cp("act" if half == 0 else "dve", mT[:, half * 4:half * 4 + 4, mb * 128:(mb + 1) * 128],
                    _v3(ps[:, 0:512], 4), r=[pr], w=[mT_r])
    sq, sq_r = self.sq_ring.next()
    self.act(sq[:, :, 0:MEM], mT, AF.Square, r=[mT_r], w=[sq_r])
    rs, rs_r = self.rms_stats(sq[:, :, 0:MEM], sq_r, MEM)
    for k in range(KD):
        self.stt(self.memn[:, k, :], mT[:, k, :], self.mng[:, k:k + 1], rs[:, 0:MEM], ALU.mult, ALU.mult,
                 r=[mT_r, rs_r, self.mng_r], w=[self.memn_r])
    self.release()


def k_make_u_full(self, gofs, a=None):
    if a is None:
        a, _ = self.alloc(KD * T, BF16, "Ufull")
    U = _v3(a, KD)
    regs = [Reg(f"U{t}") for t in range(4)]
    for tt in range(4):
        self.norm_u(tt * 512, (tt + 1) * 512, gofs, U[:, :, tt * 512:(tt + 1) * 512], regs[tt])
    return U, regs


def k_xattn(self, l, w_in, col0, CATX, catx_r):
    self.mark()
    self.sq_ring = Ring([(_v3(a, KD), r) for a, r in [self.alloc(KD * 512, BF16, "sq")]])
    self.rs_ring = Ring([self.alloc(512, F32, f"rs{i}") for i in range(2)])
    U, U_r = self.make_u_full((l * 4 + 0) * KD)
    wmk = self.wb["w_mem_kv"]
    wk_ring = Ring([(lambda ar: (_v3(ar[0], KD), ar[1]))(self.alloc(KD * 256, BF16, f"xwk{i}")) for i in range(2)])
    Kmem_a, Kmem_r = self.alloc(4 * MEM, BF16, "Kmem")
    Kmem = _v3(Kmem_a, 4)
    Vmem_a, Vmem_r = self.alloc(2 * XW, BF16, "Vmem")
    Vmem = _v3(Vmem_a, 2)
    for hp in range(2):
        wk, wk_r = wk_ring.next()
        self.wload(wk, wk_r, wmk[l, :, hp * 256:(hp + 1) * 256].rearrange("(k p) n -> p k n", p=128))
        for hh in range(2):
            h = hp * 2 + hh
            ps, pr = self.psum()
            for k in range(KD):
                self.mm(ps[:, 0:MEM], wk[:, k, hh * 128:(hh + 1) * 128], self.memn[:, k, :], k == 0, k == KD - 1,
                        r=[wk_r, self.memn_r], w=[pr])
            self.cp("act", Kmem[:, h, :], ps[:, 0:MEM], r=[pr], w=[Kmem_r])
    for vp in range(2):
        wk, wk_r = wk_ring.next()
        self.wload(wk, wk_r, wmk[l, :, XW + vp * 256:XW + (vp + 1) * 256].rearrange("(k p) n -> p k n", p=128))
        for mb in range(2):
            ps, pr = self.psum()
            for k in range(KD):
                self.mm(ps[:, 0:256], self.memn[:, k, mb * 128:(mb + 1) * 128], wk[:, k, :], k == 0, k == KD - 1,
                        r=[wk_r, self.memn_r], w=[pr])
            self.cp("act", Vmem[:, mb, vp * 256:(vp + 1) * 256], ps[:, 0:256], r=[pr], w=[Vmem_r])
    xq_ring = Ring([self.alloc(512, BF16, f"xq{i}") for i in range(2)])
    e_ring = Ring([self.alloc(512, BF16, f"xe{i}") for i in range(4)])
    rd_ring = Ring([self.alloc(512, F32, f"xrd{i}") for i in range(2)])
    for hp in range(2):
        wq, wq_r = wk_ring.next()
        self.wload(wq, wq_r, w_in[:, col0 + hp * 256:col0 + (hp + 1) * 256].rearrange("(k p) n -> p k n", p=128))
        for hh in range(2):
            h = hp * 2 + hh
            for tt in range(4):
                sl = slice(tt * 512, (tt + 1) * 512)
                ps, pr = self.psum()
                for k in range(KD):
                    self.mm(ps[:, 0:512], wq[:, k, hh * 128:(hh + 1) * 128], U[:, k, sl], k == 0, k == KD - 1,
                            r=[wq_r, U_r[tt]], w=[pr])
                xq, xq_r = xq_ring.next()
                self.cp("act", xq, ps[:, 0:512], r=[pr], w=[xq_r])
                es = []
                for mb in range(2):
                    pss, pss_r = self.psum()
                    self.mm(pss[:, 0:512], Kmem[:, h, mb * 128:(mb + 1) * 128], xq, True, True,
                            r=[Kmem_r, xq_r], w=[pss_r])
                    e, e_r = e_ring.next()
                    self.act(e, pss[:, 0:512], AF.Exp, r=[pss_r], w=[e_r], scale=XSCALE)
                    es.append((e, e_r))
                pv, pv_r = self.psum()
                den, den_r = self.psum()
                for mb in range(2):
                    self.mm(pv[:, 0:512], Vmem[:, mb, h * 128:(h + 1) * 128], es[mb][0], mb == 0, mb == 1,
                            r=[Vmem_r, es[mb][1]], w=[pv_r])
                for mb in range(2):
                    self.mm(den[:, 0:512], self.onesb, es[mb][0], mb == 0, mb == 1,
                            r=[self.onesb_r, es[mb][1]], w=[den_r])
                rd, rd_r = rd_ring.next()
                self.rcp(rd, den[:, 0:512], r=[den_r], w=[rd_r])
                self.tt("dve", CATX[:, h, sl], pv[:, 0:512], rd, ALU.mult, r=[pv_r, rd_r], w=[catx_r[h][tt]])
    self.release()


def k_out_proj(self, l, chunks, w_out, gofs):
    self.mark()
    self.sq_ring = Ring([(_v3(a, KD), r) for a, r in [self.alloc(KD * 512, BF16, "sq")]])
    self.rs_ring = Ring([self.alloc(512, F32, f"rs{i}") for i in range(2)])
    nch = len(chunks)
    wo_a, _ = self.alloc(nch * D, BF16, "wo_res")
    wo = _v3(wo_a, nch)
    wo_regs = []
    for c, (apf, rf, kp, row0) in enumerate(chunks):
        r_ = Reg(f"wo{c}")
        wo_regs.append(r_)
        self.wload(wo[0:kp, c, :], r_, w_out[row0:row0 + kp, :])
    y_a, y_r = self.alloc(KD * 512, F32, "ymix")
    ybuf = _v3(y_a, KD)
    for tt in range(4):
        sq, sq_r = self.sq_ring.next()
        for half in range(2):
            banks = [self.psum() for _ in range(4)]
            for c, (apf, rf, kp, row0) in enumerate(chunks):
                if rf is None:
                    cap, cregs = apf(tt)
                else:
                    cap, cregs = apf(tt), rf(tt)
                for q in range(4):
                    dk = half * 4 + q
                    self.mm(banks[q][0][:, 0:512], wo[0:kp, c, dk * 128:(dk + 1) * 128], cap,
                            c == 0, c == nch - 1, r=[wo_regs[c]] + cregs, w=[banks[q][1]])
            for q in range(4):
                dk = half * 4 + q
                self.cp("act", ybuf[:, dk, :], banks[q][0][:, 0:512], r=[banks[q][1]], w=[y_r])
                self.act(sq[:, dk, :], banks[q][0][:, 0:512], AF.Square, r=[banks[q][1]], w=[sq_r])
        self.resid_norm(tt * 512, 512, ybuf, y_r, sq, sq_r, gofs)
    self.release()


def k_band_attn(self, QT, qt_r, KLO, KUP, k2_r, V2, v2_r, pairs_fn, tables, scale, esk=None, group_prologue=None):
    e_ring = Ring([self.alloc(512, BF16, f"ae{i}") for i in range(6)])
    rd_ring = Ring([self.alloc(512, F32, f"ard{i}") for i in range(2)])
    for g in range(4):
        if group_prologue is not None:
            group_prologue(g)
        for i in range(16):
            qs = slice(i * 128, (i + 1) * 128)
            prs = pairs_fn(g, i)
            es = []
            for (j, tk) in prs:
                ks = slice(j * 128, (j + 1) * 128)
                ps, pr = self.psum()
                self.mm(ps[:, 0:256], KLO[:, g, ks], QT[:, 2 * g:2 * g + 2, qs], True, True,
                        r=[k2_r[g][j // 4], qt_r[g][i]], w=[pr])
                self.mm(ps[:, 256:512], KUP[:, g, ks], QT[:, 2 * g:2 * g + 2, qs], True, True,
                        r=[k2_r[g][j // 4], qt_r[g][i]], w=[pr])
                e, e_r = e_ring.next()
                self.act(e, ps[:, 0:512], AF.Exp, r=[pr], w=[e_r], scale=scale)
                if tk is not None:
                    tap, t_r = tables[tk]
                    self.tt("pool", e, e, tap, ALU.mult, r=[e_r, t_r], w=[e_r])
                es.append((j, e, e_r))
            pv, pv_r = self.psum()
            den, den_r = self.psum()
            n = len(es)
            for s_ in range(4):
                for a, (j, e, e_r) in enumerate(es):
                    self.mm(pv[:, s_ * 128:(s_ + 1) * 128], V2[:, j, g, :], e[:, s_ * 128:(s_ + 1) * 128],
                            a == 0, a == n - 1, r=[v2_r[j], e_r], w=[pv_r])
            for a, (j, e, e_r) in enumerate(es):
                self.mm(den[:, 0:512], self.onesb, e, a == 0, a == n - 1,
                        r=[self.onesb_r, e_r], w=[den_r])
            rd, rd_r = rd_ring.next()
            if esk is not None:
                for s_, h in enumerate([4 * g, 4 * g + 2, 4 * g + 1, 4 * g + 3]):
                    self.ts("dve", rd[:, s_ * 128:(s_ + 1) * 128], den[:, s_ * 128:(s_ + 1) * 128],
                            esk[0][:, h:h + 1], None, ALU.add, None, r=[den_r, esk[1]], w=[rd_r])
                self.rcp(rd, rd, r=[rd_r], w=[rd_r])
            else:
                self.rcp(rd, den[:, 0:512], r=[den_r], w=[rd_r])
            self.tt("dve", QT[0:64, 2 * g:2 * g + 2, qs], _v3(pv[0:64, 0:256], 2), _v3(rd[0:64, 0:256], 2),
                    ALU.mult, r=[pv_r, rd_r], w=[qt_r[g][i]])
            self.tt("dve", QT[64:128, 2 * g:2 * g + 2, qs], _v3(pv[64:128, 256:512], 2),
                    _v3(rd[64:128, 256:512], 2), ALU.mult, r=[pv_r, rd_r], w=[qt_r[g][i]])


def k_attn_proj(self, U, U_r, w_in, w_sw, QT, qt_r, KLO, KUP, k2_r, V2, v2_r, rope):
    w_ring = Ring([(lambda ar: (_v3(ar[0], KD), ar[1]))(self.alloc(KD * 128, BF16, f"aw{i}")) for i in range(4)])
    if rope:
        cs_ring = Ring([self.alloc(512, F32, f"cos{i}") for i in range(1)])
        sn_ring = Ring([self.alloc(512, F32, f"sin{i}") for i in range(1)])
        t1_ring = Ring([self.alloc(512, F32, f"rt1{i}") for i in range(2)])
        t2_ring = Ring([self.alloc(512, F32, f"rt2{i}") for i in range(1)])

    def wfill(wt, wt_r, wsrc, wcols, place):
        if place is None:
            self.wload(wt, wt_r, wsrc[:, wcols].rearrange("(k p) n -> p k n", p=128))
        else:
            lo = 0 if place == "lo" else 64
            self.ms("pool", wt[:, :, 64 - lo:128 - lo], 0.0, w=[wt_r])
            self.wload(wt[:, :, lo:lo + 64], wt_r, wsrc[:, wcols].rearrange("(k p) n -> p k n", p=128))

    def proj(dst_fn, dst_regs_fn, wcols, place):
        w1, w1_r = w_ring.next()
        wfill(w1, w1_r, w_in, wcols, place)
        if rope:
            w2, w2_r = w_ring.next()
            wfill(w2, w2_r, w_sw, wcols, place)
        for tt in range(4):
            sl = slice(tt * 512, (tt + 1) * 512)
            ps1, p1_r = self.psum()
            for k in range(KD):
                self.mm(ps1[:, 0:512], w1[:, k, :], U[:, k, sl], k == 0, k == KD - 1, r=[w1_r, U_r[tt]], w=[p1_r])
            if not rope:
                self.cp("act", dst_fn(tt), ps1[:, 0:512], r=[p1_r], w=dst_regs_fn(tt))
                continue
            ps2, p2_r = self.psum()
            for k in range(KD):
                self.mm(ps2[:, 0:512], w2[:, k, :], U[:, k, sl], k == 0, k == KD - 1, r=[w2_r, U_r[tt]], w=[p2_r])
            cs, cs_r = cs_ring.next()
            sn, sn_r = sn_ring.next()
            self.dma(cs, self.dram["rope_cos"][:, sl], w=[cs_r])
            self.dma(sn, self.dram["rope_sin"][:, sl], w=[sn_r])
            t1, t1_r = t1_ring.next()
            t2, t2_r = t2_ring.next()
            self.tt("dve", t1, ps1[:, 0:512], cs, ALU.mult, r=[p1_r, cs_r], w=[t1_r])
            self.tt("dve", t2, ps2[:, 0:512], sn, ALU.mult, r=[p2_r, sn_r], w=[t2_r])
            self.tt("pool", dst_fn(tt), t1, t2, ALU.add, r=[t1_r, t2_r], w=dst_regs_fn(tt))

    for p_ in range(8):
        g = p_ // 2
        proj(lambda tt, p_=p_: QT[:, p_, tt * 512:(tt + 1) * 512],
             lambda tt, g=g: [qt_r[g][i] for i in range(tt * 4, tt * 4 + 4)],
             slice(p_ * 128, (p_ + 1) * 128), None)
    for g in range(4):
        for KX, place in ((KLO, "lo"), (KUP, "up")):
            proj(lambda tt, g=g, KX=KX: KX[:, g, tt * 512:(tt + 1) * 512],
                 lambda tt, g=g: [k2_r[g][tt]],
                 slice(1024 + g * 64, 1024 + (g + 1) * 64), place)
    wv_a, wv_r = self.alloc(KD * 256, BF16, "awv")
    wv = _v3(wv_a, KD)
    self.wload(wv, wv_r, w_in[:, 1280:1536].rearrange("(k p) n -> p k n", p=128))
    for blk in range(16):
        ps, pr = self.psum()
        for k in range(KD):
            self.mm(ps[:, 0:256], U[:, k, blk * 128:(blk + 1) * 128], wv[:, k, :], k == 0, k == KD - 1,
                    r=[wv_r, U_r[blk // 4]], w=[pr])
        src = _v3(ps[:, 0:256], 4)
        self.cp("act", V2[:, blk, :, 0:64], src, r=[pr], w=[v2_r[blk]])
        self.cp("dve", V2[:, blk, :, 64:128], src, r=[pr], w=[v2_r[blk]])


def k_mixer_attn(self, l, kind):
    rope = (kind == 0)
    skip = self.cfg.get("skip", ())
    w_in = self.wb["a_w_in" if rope else "d_w_in"][0]
    w_sw = self.wb["a_w_in_sw"] if rope else None
    QT_a, _ = self.alloc(8 * T, BF16, "QT")
    QT = _v3(QT_a, 8)
    qt_r = [[Reg(f"qt{g}_{i}") for i in range(16)] for g in range(4)]
    self.mark()
    KL_a, _ = self.alloc(4 * T, BF16, "KLO")
    KLO = _v3(KL_a, 4)
    KU_a, _ = self.alloc(4 * T, BF16, "KUP")
    KUP = _v3(KU_a, 4)
    k2_r = [[Reg(f"k2{g}_{t}") for t in range(4)] for g in range(4)]
    V2_a, _ = self.alloc(16 * 4 * 128, BF16, "V2")
    V2 = V2_a.rearrange("p (b g d) -> p b g d", b=16, g=4)
    v2_r = [Reg(f"v2{b}") for b in range(16)]
    tables = {}
    esk = None
    if rope:
        for nm in ("mask_prev4", "mask_next4"):
            a, r_ = self.alloc(512, BF16, nm)
            tables[nm] = (a, r_)
        esk = self.alloc(16, F32, "esk")
    self.mark()
    Ua, _ = self.alloc(KD * T, BF16, "Ufull")
    self.mark()
    self.sq_ring = Ring([(_v3(a, KD), r) for a, r in [self.alloc(KD * 512, BF16, "sq")]])
    self.rs_ring = Ring([self.alloc(512, F32, f"rs{i}") for i in range(2)])
    if rope:
        stg_a, stg_r = self.alloc(512, F32, "stg")
        for nm in ("mask_prev4", "mask_next4"):
            self.dma(stg_a[:, 0:512], self.dram[nm], w=[stg_r])
            self.cp("dve", tables[nm][0], stg_a[:, 0:512], r=[stg_r], w=[tables[nm][1]])
        self.dma(esk[0], self.dram["a_sink_b"], w=[esk[1]])
        self.act(esk[0], esk[0], AF.Exp, r=[esk[1]], w=[esk[1]])
    U, U_r = self.make_u_full((l * 4 + 0) * KD, Ua)
    self.release()
    self.p.barrier()
    self.attn_proj(U, U_r, w_in, w_sw, QT, qt_r, KLO, KUP, k2_r, V2, v2_r, rope)
    self.release()
    self.p.barrier()
    self.mark()
    if rope:
        def pairs_fn(g, i):
            out = []
            if i > 0:
                out.append((i - 1, "mask_prev4"))
            out.append((i, None))
            if i < 15:
                out.append((i + 1, "mask_next4"))
            return out
        self.band_attn(QT, qt_r, KLO, KUP, k2_r, V2, v2_r, pairs_fn, tables, ASCALE, esk)
    else:
        keys = na_table_keys()
        stg_ring = Ring([self.alloc(512, F32, f"nastg{i}") for i in range(2)])
        for kx in keys:
            a, r_ = self.alloc(512, BF16, "natab")
            tables[kx] = (a, r_)

        def prologue(g):
            for ti, kx in enumerate(keys):
                st, st_r = stg_ring.next()
                self.dma(st, self.dram["na_tab"][g, ti], w=[st_r])
                self.act(tables[kx][0], st, AF.Exp, r=[st_r], w=[tables[kx][1]])

        def pairs_fn(g, i):
            if 2 <= i <= 13:
                return [(i + d_, ("i", d_)) for d_ in (-2, -1, 0, 1, 2)]
            js = range(0, 4) if i < 2 else range(12, 16)
            return [(j, ("e", i, j)) for j in js]
        self.band_attn(QT, qt_r, KLO, KUP, k2_r, V2, v2_r, pairs_fn, tables, ASCALE, None, prologue)
    self.release()
    self.release()
    chunks = []
    for p_ in range(8):
        g = p_ // 2
        chunks.append((lambda tt, p_=p_: QT[:, p_, tt * 512:(tt + 1) * 512],
                       lambda tt, g=g: [qt_r[g][i] for i in range(tt * 4, tt * 4 + 4)], 128, p_ * 128))
    return chunks


MAMBA_TILES = [(0, 508), (508, 508), (1016, 508), (1524, 508), (2032, 16)]


def k_mixer_mamba(self, l):
    d = self.dram
    w_in = self.wb["c_w_in"][0]
    if not hasattr(self, "c_catm"):
        self.c_catm = self.nc.dram_tensor("c_catm", [32, 64, T], BF16).ap()
    catm = self.c_catm
    g0 = (l * 4 + 0) * KD
    self.mark()
    C = {}
    C["trif"] = self.alloc(128, F32, "trif")
    C["trib"] = self.alloc(128, F32, "trib")
    self.dma(C["trif"][0], d["tri_f"], w=[C["trif"][1]])
    self.dma(C["trib"][0], d["tri_b"], w=[C["trib"][1]])
    C["cw"] = self.alloc(5 * 32, F32, "c_cw")
    self.dma(C["cw"][0], d["c_conv_w_t"], w=[C["cw"][1]])
    C["cb"] = self.alloc(32, F32, "c_cb")
    self.dma(C["cb"][0], d["c_conv_b_t"], w=[C["cb"][1]])
    C["dcol"] = self.alloc(32, F32, "c_d")
    self.dma(C["dcol"][0], d["c_d_b"], w=[C["dcol"][1]])
    C["gcol"] = self.alloc(32, F32, "c_ng")
    self.dma(C["gcol"][0][0:64, :], d["c_norm_g_t"], w=[C["gcol"][1]])
    C["ones64"] = self.alloc(64, BF16, "ones64")
    self.ms("pool", C["ones64"][0], 1.0 / 256.0, w=[C["ones64"][1]])
    aneg, aneg_r = self.alloc(64, F32, "aneg")
    self.dma(aneg, d["c_a_log_b"], w=[aneg_r])
    self.act(aneg, aneg, AF.Exp, r=[aneg_r], w=[aneg_r])
    self.ts("dve", aneg, aneg, -1.0, None, ALU.mult, None, r=[aneg_r], w=[aneg_r])
    dtb, dtb_r = self.alloc(64, F32, "dtb")
    self.dma(dtb, d["c_dt_bias_b"], w=[dtb_r])
    dt_a, dt_r = self.alloc(16 * 64, F32, "dt")
    dt = _v3(dt_a, 16)
    dtA_a, dtA_r = self.alloc(16 * 64, F32, "dtA")
    dtA = _v3(dtA_a, 16)
    ncs_a, ncs_r = self.alloc(16 * 64, F32, "ncs")
    ncs = _v3(ncs_a, 16)
    C["dt"] = (dt, dt_r)
    C["dtA"] = (dtA, dtA_r)
    C["ncs"] = (ncs, ncs_r)
    trif, trif_r = C["trif"]
    trib, trib_r = C["trib"]
    self.mark()
    self.sq_ring = Ring([(_v3(a, KD), r) for a, r in [self.alloc(KD * 512, BF16, "sq")]])
    self.rs_ring = Ring([self.alloc(512, F32, f"rs{i}") for i in range(2)])
    Ua, U_r = self.alloc(KD * 512, BF16, "Utile")
    U = _v3(Ua, KD)
    wdt_a, wdt_r = self.alloc(KD * 64, BF16, "wdt")
    wdt = _v3(wdt_a, KD)
    self.wload(wdt, wdt_r, w_in[:, 6144:6208].rearrange("(k p) n -> p k n", p=128))
    for tt in range(4):
        self.norm_u(tt * 512, (tt + 1) * 512, g0, U, U_r)
        for b4 in range(4):
            blk = tt * 4 + b4
            ps, pr = self.psum()
            for k in range(KD):
                self.mm(ps[:, 0:64], U[:, k, b4 * 128:(b4 + 1) * 128], wdt[:, k, :], k == 0, k == KD - 1,
                        r=[U_r, wdt_r], w=[pr])
            self.tt("dve", dt[:, blk, :], ps[:, 0:64], dtb, ALU.add, r=[pr, dtb_r], w=[dt_r])
    self.act(dt_a, dt_a, AF.Exp, r=[dt_r], w=[dt_r])
    self.act(dt_a, dt_a, AF.Ln, r=[dt_r, self.onec_r], w=[dt_r], bias=self.onec[:, 0:1])
    self.tt("dve", dtA, dt, aneg.unsqueeze(1).broadcast_to([128, 16, 64]), ALU.mult, r=[dt_r, aneg_r], w=[dtA_r])
    for blk in range(16):
        ps, pr = self.psum()
        self.mm(ps[:, 0:32], trif, dtA[:, blk, 0:32], True, True, r=[trif_r, dtA_r], w=[pr])
        self.mm(ps[:, 32:64], trib, dtA[:, blk, 32:64], True, True, r=[trib_r, dtA_r], w=[pr])
        self.ts("dve", ncs[:, blk, :], ps[:, 0:64], -1.0, None, ALU.mult, None, r=[pr], w=[ncs_r])
    self.release()
    self.p.barrier()
    a_, r_ = self.alloc(4 * T, F32, "Yacc")
    C["Yacc"] = (_v3(a_, 4), r_)
    a_, r_ = self.alloc(4 * T, BF16, "zs")
    C["zs"] = (_v3(a_, 4), r_)
    a_, r_ = self.alloc(16 * 256, BF16, "xs_tok")
    C["xtok"] = (_v3(a_, 16), r_)
    a_, r_ = self.alloc(16 * 128, BF16, "B_tok")
    C["btok"] = (_v3(a_, 16), r_)
    C["BT"] = self.alloc(T, BF16, "BT")
    C["CT"] = self.alloc(T, BF16, "CT")
    for g in range(8):
        self.mark()
        self.mamba_p1(g, g0, w_in, C)
        self.release()
        self.p.barrier()
        self.mark()
        self.mamba_p2(g, C)
        self.release()
        self.p.barrier()
        self.mark()
        self.mamba_p3(g, C, catm)
        self.release()
        self.p.barrier()
    self.release()
    ring = Ring([self.alloc(512, BF16, f"cld{i}") for i in range(6)])
    chunks = []
    for hh in range(32):
        def get(tt, hh=hh):
            a, a_r = ring.next()
            self.dma(a[0:64, :], catm[hh, :, tt * 512:(tt + 1) * 512], r=[self.catm_r], w=[a_r])
            return a[0:64, :], [a_r]
        chunks.append((get, None, 64, hh * 64))
    return chunks


def k_mamba_p1(self, g, g0, w_in, C):
    cw, cw_r = C["cw"]
    cbias, cbias_r = C["cb"]
    zs, zs_r = C["zs"]
    xtok, xtok_r = C["xtok"]
    btok, btok_r = C["btok"]
    BT, BT_r = C["BT"]
    CT, CT_r = C["CT"]
    self.sq_ring = Ring([(_v3(a, KD), r) for a, r in [self.alloc(KD * 512, BF16, "sq")]])
    self.rs_ring = Ring([self.alloc(512, F32, f"rs{i}") for i in range(2)])
    Ua, U_r = self.alloc(KD * 512, BF16, "Utile")
    U = _v3(Ua, KD)
    xsT = [self.alloc(T, BF16, f"xsT{i}") for i in range(2)]
    wc_a, wc_r = self.alloc(KD * 512, BF16, "wc")
    wc = _v3(wc_a, KD)
    wz_a, wz_r = self.alloc(KD * 256, BF16, "wz")
    wz = _v3(wz_a, KD)
    acc_ring = Ring([self.alloc(512, F32, f"cacc{i}") for i in range(2)])
    self.wload(wc[:, :, 0:256], wc_r, w_in[:, 2048 + g * 256:2048 + (g + 1) * 256].rearrange("(k p) n -> p k n", p=128))
    self.wload(wc[:, :, 256:384], wc_r, w_in[:, 4096 + g * 128:4096 + (g + 1) * 128].rearrange("(k p) n -> p k n", p=128))
    self.wload(wc[:, :, 384:512], wc_r, w_in[:, 5120 + g * 128:5120 + (g + 1) * 128].rearrange("(k p) n -> p k n", p=128))
    self.wload(wz, wz_r, w_in[:, g * 256:(g + 1) * 256].rearrange("(k p) n -> p k n", p=128))
    dsts = [(xsT[0][0], xsT[0][1], 2 * g), (xsT[1][0], xsT[1][1], 2 * g + 1), (BT, BT_r, 16 + g), (CT, CT_r, 24 + g)]
    for (t0, n) in MAMBA_TILES:
        self.norm_u(t0 - 2, t0 + n + 2, g0, U, U_r)
        for ci, (dst, dst_r, cch) in enumerate(dsts):
            ps, pr = self.psum()
            for k in range(KD):
                self.mm(ps[:, 0:n + 4], wc[:, k, ci * 128:(ci + 1) * 128], U[:, k, 0:n + 4], k == 0, k == KD - 1,
                        r=[wc_r, U_r], w=[pr])
            acc, acc_r = acc_ring.next()
            A = acc[:, 0:n]
            self.ts("dve", A, ps[:, 2:n + 2], cw[:, 2 * 32 + cch:2 * 32 + cch + 1], cbias[:, cch:cch + 1],
                    ALU.mult, ALU.add, r=[pr, cw_r, cbias_r], w=[acc_r])
            for j in (0, 1, 3, 4):
                self.stt(A, ps[:, j:j + n], cw[:, j * 32 + cch:j * 32 + cch + 1], A, ALU.mult, ALU.add,
                         r=[pr, cw_r, acc_r], w=[acc_r])
            self.act(dst[:, t0:t0 + n], A, AF.Silu, r=[acc_r], w=[dst_r])
        for j in range(4):
            ps, pr = self.psum()
            for k in range(KD):
                self.mm(ps[0:64, 0:n], wz[:, k, j * 64:(j + 1) * 64], U[:, k, 2:n + 2], k == 0, k == KD - 1,
                        r=[wz_r, U_r], w=[pr])
            self.act(zs[0:64, j, t0:t0 + n], ps[0:64, 0:n], AF.Silu, r=[pr], w=[zs_r])
    for src, src_r, dstv, dst_r, c0 in ((xsT[0][0], xsT[0][1], xtok, xtok_r, 0),
                                        (xsT[1][0], xsT[1][1], xtok, xtok_r, 128),
                                        (BT, BT_r, btok, btok_r, 0)):
        for b4 in range(4):
            psf, pr = self.psum()
            psb = psf.bitcast(BF16)
            for q in range(4):
                blk = b4 * 4 + q
                self.tr(psb[:, q * 128:(q + 1) * 128], src[:, blk * 128:(blk + 1) * 128], self.identb,
                        r=[src_r, self.identb_r], w=[pr])
            self.cp("act", dstv[:, b4 * 4:b4 * 4 + 4, c0:c0 + 128], _v3(psb[:, 0:512], 4), r=[pr], w=[dst_r])


def k_mamba_p2(self, g, C):
    dcol, dcol_r = C["dcol"]
    dt, dt_r = C["dt"]
    dtA, dtA_r = C["dtA"]
    ncs, ncs_r = C["ncs"]
    Yacc, Y_r = C["Yacc"]
    xtok, xtok_r = C["xtok"]
    btok, btok_r = C["btok"]
    BT, BT_r = C["BT"]
    CT, CT_r = C["CT"]
    tri = [C["trif"], C["trib"]]
    dI_a, dI_r = self.alloc(4 * 128, BF16, "dI")
    dI = _v3(dI_a, 4)
    cbm, cbm_r = self.alloc(128, F32, "cbm")
    bc4_a, bc4_r = self.alloc(512, F32, "bc4")
    bc4 = _v3(bc4_a, 4)
    tmp_a, tmp_r = self.alloc(512, F32, "ctmp")
    tmp = _v3(tmp_a, 4)
    D_a, D_r = self.alloc(512, F32, "cD")
    Dm = _v3(D_a, 4)
    E_a, E_r = self.alloc(512, F32, "cE")
    Em = _v3(E_a, 4)
    MT_a, MT_r = self.alloc(512, BF16, "MT")
    MT = _v3(MT_a, 4)
    Cs_a, Cs_r = self.alloc(512, BF16, "CsT")
    CsT = _v3(Cs_a, 4)
    xdt_a, xdt_r = self.alloc(256, BF16, "xdt")
    xdt = _v3(xdt_a, 4)
    xdw_a, xdw_r = self.alloc(256, BF16, "xdtw")
    xdw = _v3(xdw_a, 4)
    st_a, st_r = self.alloc(256, F32, "state")
    st = _v3(st_a, 4)
    stb_a, stb_r = self.alloc(256, BF16, "stateb")
    stb = _v3(stb_a, 4)
    for j in range(4):
        hh = g * 4 + j
        self.ts("dve", dI[:, j, :], self.identf, dcol[:, hh:hh + 1], None, ALU.mult, None,
                r=[self.identf_r, dcol_r], w=[dI_r])
    for dr in range(2):
        c0 = dr * 32 + g * 4
        order = list(range(16)) if dr == 0 else list(range(15, -1, -1))
        tend = 127 if dr == 0 else 0
        trm, trm_r = tri[dr]
        for oi, blk in enumerate(order):
            bs = slice(blk * 128, (blk + 1) * 128)
            psc, psc_r = self.psum()
            self.mm(psc[:, 0:128], BT[:, bs], CT[:, bs], True, True, r=[BT_r, CT_r], w=[psc_r])
            self.tt("dve", cbm, psc[:, 0:128], trm, ALU.mult, r=[psc_r, trm_r], w=[cbm_r])
            self.cp("pool", bc4, dtA[:, blk, c0:c0 + 4].unsqueeze(2).broadcast_to([128, 4, 128]),
                    r=[dtA_r], w=[bc4_r])
            pcs, pcs_r = self.psum()
            pcs4 = _v3(pcs[:, 0:512], 4)
            for j in range(4):
                self.mm(pcs4[:, j, :], bc4[:, j, :], trm, True, True, r=[bc4_r, trm_r], w=[pcs_r])
            for j in range(4):
                self.ts("dve", tmp[:, j, :], pcs4[:, j, :], ncs[:, blk, c0 + j:c0 + j + 1], 0.0, ALU.add, ALU.min,
                        r=[pcs_r, ncs_r], w=[tmp_r])
            self.act(D_a, tmp_a, AF.Exp, r=[tmp_r], w=[D_r])
            self.act(E_a, pcs[:, 0:512], AF.Exp, r=[pcs_r], w=[E_r])
            self.tt("pool", MT, Dm, cbm.unsqueeze(1).broadcast_to([128, 4, 128]), ALU.mult,
                    r=[D_r, cbm_r], w=[MT_r])
            self.tt("pool", CsT, Em, CT[:, bs].unsqueeze(1).broadcast_to([128, 4, 128]), ALU.mult,
                    r=[E_r, CT_r], w=[Cs_r])
            self.tt("pool", xdt, xtok[:, blk, :].rearrange("p (j q) -> p j q", j=4),
                    dt[:, blk, c0:c0 + 4].unsqueeze(2).broadcast_to([128, 4, 64]), ALU.mult,
                    r=[xtok_r, dt_r], w=[xdt_r])
            py, py_r = self.psum()
            py4 = _v3(py[:, 0:512], 4)
            for j in range(4):
                seq = [(xdt[:, j, :], MT[:, j, :], [xdt_r, MT_r])]
                if oi > 0:
                    seq.append((stb[:, j, :], CsT[:, j, :], [stb_r, Cs_r]))
                if dr == 0:
                    seq.append((xtok[:, blk, j * 64:(j + 1) * 64], dI[:, j, :], [xtok_r, dI_r]))
                for si, (lt, rh, rg) in enumerate(seq):
                    self.mm(py4[0:64, j, :], lt, rh, si == 0, si == len(seq) - 1, r=rg, w=[py_r])
            if dr == 0:
                self.cp("act", Yacc[0:64, :, bs], py4[0:64, :, :], r=[py_r], w=[Y_r])
            else:
                self.tt("dve", Yacc[0:64, :, bs], Yacc[0:64, :, bs], py4[0:64, :, :], ALU.add, r=[py_r, Y_r], w=[Y_r])
            if oi < 15:
                self.tt("pool", xdw, xdt, Dm[:, :, tend:tend + 1].broadcast_to([128, 4, 64]), ALU.mult,
                        r=[xdt_r, D_r], w=[xdw_r])
                pu, pu_r = self.psum()
                self.mm(pu[:, 0:256], btok[:, blk, :], xdw_a, True, True, r=[btok_r, xdw_r], w=[pu_r])
                if oi == 0:
                    self.cp("act", st_a, pu[:, 0:256], r=[pu_r], w=[st_r])
                else:
                    self.tt("pool", st, st, Em[:, :, tend:tend + 1].broadcast_to([128, 4, 64]), ALU.mult,
                            r=[st_r, E_r], w=[st_r])
                    self.tt("dve", st_a, st_a, pu[:, 0:256], ALU.add, r=[st_r, pu_r], w=[st_r])
                self.cp("act", stb_a, st_a, r=[st_r], w=[stb_r])


def k_mamba_p3(self, g, C, catm):
    Yacc, Y_r = C["Yacc"]
    zs, zs_r = C["zs"]
    ones64, ones64_r = C["ones64"]
    gcol, gcol_r = C["gcol"]
    ysq_a, ysq_r = self.alloc(4 * 512, BF16, "ysq")
    ysq = _v3(ysq_a, 4)
    co_ring = Ring([(lambda ar: (_v3(ar[0], 4), ar[1]))(self.alloc(4 * 512, BF16, f"cato{i}")) for i in range(2)])
    frs, frs_r = self.alloc(512, F32, "frs")
    for tt in range(4):
        sl = slice(tt * 512, (tt + 1) * 512)
        self.tt("dve", Yacc[0:64, :, sl], Yacc[0:64, :, sl], zs[0:64, :, sl], ALU.mult, r=[Y_r, zs_r], w=[Y_r])
        self.act(ysq[0:64, :, :], Yacc[0:64, :, sl], AF.Square, r=[Y_r], w=[ysq_r])
        ps, pr = self.psum()
        for j in range(4):
            self.mm(ps[0:64, 0:512], ones64[0:64, :], ysq[0:64, j, :], j == 0, j == 3, r=[ones64_r, ysq_r], w=[pr])
        self.act(frs[0:64, :], ps[0:64, 0:512], AF.Ln, r=[pr, self.epsc_r], w=[frs_r], bias=self.epsc[0:64, 0:1])
        self.act(frs[0:64, :], frs[0:64, :], AF.Exp, r=[frs_r], w=[frs_r], scale=-0.5)
        co, co_r = co_ring.next()
        for j in range(4):
            hh = g * 4 + j
            self.stt(co[0:64, j, :], Yacc[0:64, j, sl], gcol[0:64, hh:hh + 1], frs[0:64, :], ALU.mult, ALU.mult,
                     r=[Y_r, gcol_r, frs_r], w=[co_r])
        self.dma(catm[g * 4:(g + 1) * 4, :, sl].rearrange("h p t -> p h t"), co[0:64, :, :],
                 r=[co_r], w=[self.catm_r])


HG_C = 32
HG_NC = T // HG_C


def k_mixer_hgrn(self, l):
    d = self.dram
    w_in = self.wb["b_w_in"][0]
    g0 = (l * 4 + 0) * KD
    cat_a, _ = self.alloc(8 * T, BF16, "hgcat")
    CAT = _v3(cat_a, 8)
    cat_r = [[Reg(f"hc{h}_{t}") for t in range(4)] for h in range(8)]
    self.mark()
    C = {}
    C["mk"] = [self.alloc(128, F32, "hgmf"), self.alloc(128, F32, "hgmb")]
    self.dma(C["mk"][0][0], d["hg_mask_f"], w=[C["mk"][0][1]])
    self.dma(C["mk"][1][0], d["hg_mask_b"], w=[C["mk"][1][1]])
    C["cmask"] = self.alloc(4, F32, "hgcm")
    self.dma(C["cmask"][0], d["hg_cmask"], w=[C["cmask"][1]])
    C["bng"] = self.alloc(8, F32, "b_ng")
    self.dma(C["bng"][0], d["b_norm_g_t"], w=[C["bng"][1]])
    C["ones128"] = self.alloc(128, BF16, "ones128")
    self.ms("pool", C["ones128"][0], 1.0 / 128.0, w=[C["ones128"][1]])
    lg, lg_r = self.alloc(64, F32, "lblog")
    self.dma(lg, d["b_lb_logits_t"], w=[lg_r])
    self.act(lg, lg, AF.Exp, r=[lg_r], w=[lg_r])
    lg4 = lg.rearrange("p (a b k) -> p a b k", a=2, b=4)
    den, den_r = self.alloc(16, F32, "lbden")
    den3 = den.rearrange("p (a k) -> p a k", a=2)
    num, num_r = self.alloc(16, F32, "lbnum")
    num3 = num.rearrange("p (a k) -> p a k", a=2)
    self.tt("dve", den3, lg4[:, :, 0, :], lg4[:, :, 1, :], ALU.add, r=[lg_r], w=[den_r])
    self.tt("dve", den3, den3, lg4[:, :, 2, :], ALU.add, r=[lg_r, den_r], w=[den_r])
    self.tt("dve", den3, den3, lg4[:, :, 3, :], ALU.add, r=[lg_r, den_r], w=[den_r])
    self.ms("dve", num, 0.0, w=[num_r])
    for dd in range(1, l + 1):
        self.tt("dve", num3, num3, lg4[:, :, dd, :], ALU.add, r=[lg_r, num_r], w=[num_r])
    self.rcp(den, den, r=[den_r], w=[den_r])
    lb, lb_r = self.alloc(16, F32, "lb")
    oml, oml_r = self.alloc(16, F32, "oml")
    self.tt("dve", lb, num, den, ALU.mult, r=[num_r, den_r], w=[lb_r])
    self.ts("dve", oml, lb, -1.0, 1.0, ALU.mult, ALU.add, r=[lb_r], w=[oml_r])
    C["lb"] = (lb, lb_r)
    C["oml"] = (oml, oml_r)
    C["qT"] = self.alloc(T, BF16, "hg_q")
    C["sg"] = [self.alloc(T, F32, "hg_sgf"), self.alloc(T, F32, "hg_sgb")]
    C["gs"] = self.alloc(T, BF16, "hg_gs")
    a_, r_ = self.alloc(16 * 128, BF16, "hg_vtok")
    C["vtok"] = (_v3(a_, 16), r_)
    C["Oacc"] = self.alloc(T, F32, "hg_O")
    for h in range(8):
        self.mark()
        self.hgrn_p1(h, g0, w_in, C)
        self.release()
        self.p.barrier()
        for dr in range(2):
            self.mark()
            self.hgrn_p2(h, dr, C)
            self.release()
            self.p.barrier()
        self.mark()
        self.hgrn_p3(h, C, CAT, cat_r)
        self.release()
        self.p.barrier()
    self.release()
    chunks = []
    for h in range(8):
        chunks.append((lambda tt, h=h: CAT[:, h, tt * 512:(tt + 1) * 512],
                       lambda tt, h=h: [cat_r[h][tt]], 128, h * 128))
    return chunks


def k_hgrn_p1(self, h, g0, w_in, C):
    qT, qT_r = C["qT"]
    gs, gs_r = C["gs"]
    vtok, vtok_r = C["vtok"]
    self.sq_ring = Ring([(_v3(a, KD), r) for a, r in [self.alloc(KD * 512, BF16, "sq")]])
    self.rs_ring = Ring([self.alloc(512, F32, f"rs{i}") for i in range(2)])
    Ua, U_r = self.alloc(KD * 512, BF16, "Utile")
    U = _v3(Ua, KD)
    w5_a, w5_r = self.alloc(KD * 5 * 128, BF16, "hgw")
    w5 = w5_a.rearrange("p (k c n) -> p k c n", k=KD, c=5)
    for c in range(5):
        self.wload(w5[:, :, c, :], w5_r, w_in[:, c * 1024 + h * 128:c * 1024 + (h + 1) * 128].rearrange(
            "(k p) n -> p k n", p=128))
    for tt in range(4):
        sl = slice(tt * 512, (tt + 1) * 512)
        self.norm_u(tt * 512, (tt + 1) * 512, g0, U, U_r)
        for c, (dst, dst_r, fn) in ((0, (qT, qT_r, AF.Silu)), (2, (C["sg"][0][0], C["sg"][0][1], AF.Sigmoid)),
                                    (3, (C["sg"][1][0], C["sg"][1][1], AF.Sigmoid)), (4, (gs, gs_r, AF.Silu))):
            ps, pr = self.psum()
            for k in range(KD):
                self.mm(ps[:, 0:512], w5[:, k, c, :], U[:, k, :], k == 0, k == KD - 1, r=[w5_r, U_r], w=[pr])
            self.act(dst[:, sl], ps[:, 0:512], fn, r=[pr], w=[dst_r])
        for b4 in range(4):
            blk = tt * 4 + b4
            ps, pr = self.psum()
            for k in range(KD):
                self.mm(ps[:, 0:128], U[:, k, b4 * 128:(b4 + 1) * 128], w5[:, k, 1, :], k == 0, k == KD - 1,
                        r=[w5_r, U_r], w=[pr])
            self.cp("dve", vtok[:, blk, :], ps[:, 0:128], r=[pr], w=[vtok_r])


def k_hgrn_p2(self, h, dr, C):
    qT, qT_r = C["qT"]
    sg, sg_r = C["sg"][dr]
    vtok, vtok_r = C["vtok"]
    Oacc, O_r = C["Oacc"]
    lb, lb_r = C["lb"]
    oml, oml_r = C["oml"]
    mk, mk_r = C["mk"][dr]
    cmask, cmask_r = C["cmask"]
    col = dr * KD + h
    NC_, CL = HG_NC, HG_C
    f_, f_r = self.alloc(T, F32, "hg_f")
    b0, b0_r = self.alloc(T, F32, "hg_b0")
    b1, b1_r = self.alloc(T, F32, "hg_b1")
    qd, qd_r = self.alloc(T, BF16, "hg_qd")
    kd, kd_r = self.alloc(T, BF16, "hg_kd")
    ke, ke_r = self.alloc(T, BF16, "hg_ke")
    kt_a, kt_r = self.alloc(16 * 128, BF16, "hg_ketok")
    ketok = _v3(kt_a, 16)
    dec, dec_r = self.alloc(NC_, F32, "hg_dec")
    self.ts("dve", f_, sg, oml[:, col:col + 1], lb[:, col:col + 1], ALU.mult, ALU.add, r=[sg_r, oml_r, lb_r], w=[f_r])
    self.act(b0, f_, AF.Ln, r=[f_r], w=[b0_r])
    self.ts("pool", f_, f_, -1.0, 1.0, ALU.mult, ALU.add, r=[f_r], w=[f_r])
    cur, cur_r, oth, oth_r = b0, b0_r, b1, b1_r
    dd = 1
    while dd < CL:
        cv = cur.rearrange("p (c i) -> p c i", i=CL)
        ov = oth.rearrange("p (c i) -> p c i", i=CL)
        if dr == 0:
            self.tt("dve", ov[:, :, dd:CL], cv[:, :, dd:CL], cv[:, :, 0:CL - dd], ALU.add, r=[cur_r], w=[oth_r])
            self.cp("pool", ov[:, :, 0:dd], cv[:, :, 0:dd], r=[cur_r], w=[oth_r])
        else:
            self.tt("dve", ov[:, :, 0:CL - dd], cv[:, :, 0:CL - dd], cv[:, :, dd:CL], ALU.add, r=[cur_r], w=[oth_r])
            self.cp("pool", ov[:, :, CL - dd:CL], cv[:, :, CL - dd:CL], r=[cur_r], w=[oth_r])
        cur, cur_r, oth, oth_r = oth, oth_r, cur, cur_r
        dd *= 2
    b, b_r, tmp, tmp_r = cur, cur_r, oth, oth_r
    bv = b.rearrange("p (c i) -> p c i", i=CL)
    e_idx = CL - 1 if dr == 0 else 0
    self.act(dec.unsqueeze(2), bv[:, :, e_idx:e_idx + 1], AF.Exp, r=[b_r], w=[dec_r])
    self.act(tmp, b, AF.Exp, r=[b_r], w=[tmp_r])
    self.tt("dve", qd, qT, tmp, ALU.mult, r=[qT_r, tmp_r], w=[qd_r])
    self.act(tmp, b, AF.Exp, r=[b_r], w=[tmp_r], scale=-1.0)
    self.tt("dve", tmp, tmp, f_, ALU.mult, r=[tmp_r, f_r], w=[tmp_r])
    self.cp("pool", kd, tmp, r=[tmp_r], w=[kd_r])
    self.tt("dve", ke.rearrange("p (c i) -> p c i", i=CL), tmp.rearrange("p (c i) -> p c i", i=CL),
            dec.unsqueeze(2).broadcast_to([128, NC_, CL]), ALU.mult, r=[tmp_r, dec_r], w=[ke_r])
    for b4 in range(4):
        psf, pr = self.psum()
        psb = psf.bitcast(BF16)
        for q in range(4):
            blk = b4 * 4 + q
            self.tr(psb[:, q * 128:(q + 1) * 128], ke[:, blk * 128:(blk + 1) * 128], self.identb,
                    r=[ke_r, self.identb_r], w=[pr])
        self.cp("act", ketok[:, b4 * 4:b4 * 4 + 4, :], _v3(psb[:, 0:512], 4), r=[pr], w=[kt_r])
    am_ring = Ring([(lambda ar: (_v3(ar[0], 4), ar[1]))(self.alloc(512, BF16, f"hg_am{i}")) for i in range(2)])
    vm_ring = Ring([(lambda ar: (_v3(ar[0], 4), ar[1]))(self.alloc(512, BF16, f"hg_vm{i}")) for i in range(2)])
    S_ring = Ring([self.alloc(128, F32, f"hg_S{i}") for i in range(6)])
    Sb_ring = Ring([self.alloc(128, BF16, f"hg_Sb{i}") for i in range(6)])
    blocks = list(range(16)) if dr == 0 else list(range(15, -1, -1))
    S_prev = None
    Sb_prev = None
    nper = 128 // CL
    for g4 in range(4):
        grp = blocks[g4 * 4:(g4 + 1) * 4]
        pa, pa_r = self.psum()
        pa4 = _v3(pa[:, 0:512], 4)
        for qi, blk in enumerate(grp):
            bs = slice(blk * 128, (blk + 1) * 128)
            self.mm(pa4[:, qi, :], kd[:, bs], qd[:, bs], True, True, r=[kd_r, qd_r], w=[pa_r])
        am, am_r = am_ring.next()
        self.tt("dve", am, pa4, mk.unsqueeze(1).broadcast_to([128, 4, 128]), ALU.mult, r=[pa_r, mk_r], w=[am_r])
        for qi, blk in enumerate(grp):
            bs = slice(blk * 128, (blk + 1) * 128)
            vm, vm_r = vm_ring.next()
            self.tt("pool", vm, vtok[:, blk, :].unsqueeze(1).broadcast_to([128, nper, 128]),
                    cmask.unsqueeze(2).broadcast_to([128, nper, 128]), ALU.mult, r=[vtok_r, cmask_r], w=[vm_r])
            pu, pu_r = self.psum()
            self.mm(pu[:, 0:512], ketok[:, blk, :], vm.rearrange("p c v -> p (c v)"), True, True,
                    r=[kt_r, vm_r], w=[pu_r])
            pu4 = _v3(pu[:, 0:512], 4)
            po, po_r = self.psum()
            chunks_in_blk = list(range(nper)) if dr == 0 else list(range(nper - 1, -1, -1))
            first_mm = True
            self.mm(po[:, 0:128], vtok[:, blk, :], am[:, qi, :], True, False, r=[vtok_r, am_r], w=[po_r])
            for ci, c in enumerate(chunks_in_blk):
                cg = blk * nper + c
                cs_ = slice(c * CL, (c + 1) * CL)
                if Sb_prev is not None:
                    self.mm(po[:, cs_], Sb_prev[0], qd[:, blk * 128 + c * CL:blk * 128 + (c + 1) * CL], False,
                            ci == nper - 1, r=[Sb_prev[1], qd_r], w=[po_r])
                S_new = S_ring.next()
                if S_prev is None:
                    self.cp("dve", S_new[0], pu4[:, c, :], r=[pu_r], w=[S_new[1]])
                else:
                    self.stt(S_new[0], S_prev[0], dec[:, cg:cg + 1], pu4[:, c, :], ALU.mult, ALU.add,
                             r=[S_prev[1], dec_r, pu_r], w=[S_new[1]])
                Sb_new = Sb_ring.next()
                self.cp("act", Sb_new[0], S_new[0], r=[S_new[1]], w=[Sb_new[1]])
                S_prev, Sb_prev = S_new, Sb_new
            if dr == 0:
                self.cp("act", Oacc[:, bs], po[:, 0:128], r=[po_r], w=[O_r])
            else:
                self.tt("dve", Oacc[:, bs], Oacc[:, bs], po[:, 0:128], ALU.add, r=[po_r, O_r], w=[O_r])


def k_hgrn_p3(self, h, C, CAT, cat_r):
    Oacc, O_r = C["Oacc"]
    gs, gs_r = C["gs"]
    bng, bng_r = C["bng"]
    ones128, ones128_r = C["ones128"]
    osq, osq_r = self.alloc(512, BF16, "hg_osq")
    frs, frs_r = self.alloc(512, F32, "hg_frs")
    ot, ot_r = self.alloc(512, F32, "hg_ot")
    for tt in range(4):
        sl = slice(tt * 512, (tt + 1) * 512)
        self.act(osq, Oacc[:, sl], AF.Square, r=[O_r], w=[osq_r])
        ps, pr = self.psum()
        self.mm(ps[:, 0:512], ones128, osq, True, True, r=[ones128_r, osq_r], w=[pr])
        self.act(frs, ps[:, 0:512], AF.Ln, r=[pr, self.epsc_r], w=[frs_r], bias=self.epsc[:, 0:1])
        self.act(frs, frs, AF.Exp, r=[frs_r], w=[frs_r], scale=-0.5)
        self.stt(ot, Oacc[:, sl], bng[:, h:h + 1], frs, ALU.mult, ALU.mult, r=[O_r, bng_r, frs_r], w=[ot_r])
        self.tt("dve", CAT[:, h, sl], ot, gs[:, sl], ALU.mult, r=[ot_r, gs_r], w=[cat_r[h][tt]])


def k_layer(self, l):
    kind = l % 4
    self.mark()
    names = {0: ("a_w_in", "a_w_out", 1536), 1: ("b_w_in", "b_w_out", 5120), 2: ("c_w_in", "c_w_out", 6208),
             3: ("d_w_in", "d_w_out", 1536)}[kind]
    w_in = self.wb[names[0]][0]
    w_out = self.wb[names[1]][0]
    if kind in (0, 3):
        chunks = self.mixer_attn(l, kind)
    elif kind == 2:
        chunks = self.mixer_mamba(l)
    else:
        chunks = self.mixer_hgrn(l)
    self.p.barrier()
    cx_a, _ = self.alloc(4 * T, BF16, "CATX")
    CATX = _v3(cx_a, 4)
    catx_r = [[Reg(f"cx{h}_{t}") for t in range(4)] for h in range(4)]
    skip = self.cfg.get("skip", ())
    if "xattn" in skip:
        self.ms("pool", cx_a, 0.0, w=[r_ for rr_ in catx_r for r_ in rr_])
    else:
        self.xattn(l, w_in, names[2], CATX, catx_r)
    base = max(c[3] + c[2] for c in chunks)
    for h in range(4):
        chunks.append((lambda tt, h=h: CATX[:, h, tt * 512:(tt + 1) * 512],
                       lambda tt, h=h: [catx_r[h][tt]], 128, base + h * 128))
    self.p.barrier()
    self.out_proj(l, chunks, w_out, (l * 4 + 1) * KD)
    self.release()
    self.p.barrier()


K.setup_seq_consts = k_setup_seq_consts
K.mem_norm = k_mem_norm
K.make_u_full = k_make_u_full
K.xattn = k_xattn
K.out_proj = k_out_proj
K.band_attn = k_band_attn
K.attn_proj = k_attn_proj
K.mixer_attn = k_mixer_attn
K.layer = k_layer
K.mixer_mamba = k_mixer_mamba
K.mixer_hgrn = k_mixer_hgrn
K.hgrn_p1 = k_hgrn_p1
K.hgrn_p2 = k_hgrn_p2
K.hgrn_p3 = k_hgrn_p3
K.mamba_p1 = k_mamba_p1
K.mamba_p2 = k_mamba_p2
K.mamba_p3 = k_mamba_p3


def _swap_halves_cols(w, ncols):
    w = np.asarray(w, np.float32)
    r = w[:, :ncols].reshape(w.shape[0], ncols // 64, 2, 32)
    return np.ascontiguousarray(r[:, :, ::-1, :].reshape(w.shape[0], ncols))


def na_table_keys():
    keys = [("i", d_) for d_ in (-2, -1, 0, 1, 2)]
    for i in (0, 1, 14, 15):
        js = range(0, 4) if i < 2 else range(12, 16)
        keys += [("e", i, j) for j in js]
    return keys


def na_tables(rpb):
    rpb = np.asarray(rpb, np.float32)
    out = np.empty((4, 21, 128, 512), np.float32)
    b = np.arange(128)[:, None]
    a = np.arange(128)[None, :]
    for ti, kx in enumerate(na_table_keys()):
        if kx[0] == "i":
            i, j = 6, 6 + kx[1]
        else:
            i, j = kx[1], kx[2]
        krow = 2 * j + b // 64
        kcol = b % 64
        r = 2 * i + a // 64
        qcol = a % 64
        rs = np.clip(r - 4, 0, 24)
        cs = np.clip(qcol - 8, 0, 48)
        valid = (krow >= rs) & (krow < rs + 8) & (kcol >= cs) & (kcol < cs + 16)
        dr = np.clip(krow - r + 7, 0, 14)
        dc = np.clip(kcol - qcol + 15, 0, 30)
        for g in range(4):
            for s_, h in enumerate([4 * g, 4 * g + 2, 4 * g + 1, 4 * g + 3]):
                tab = rpb[h][dr, dc]
                out[g, ti, :, s_ * 128:(s_ + 1) * 128] = np.where(valid, tab, np.float32(-30000.0))
    return out


def host_consts(inputs):
    c = {}
    c["c_ident"] = np.eye(128, dtype=np.float32)
    ng = np.asarray(inputs["norm_g"], np.float32)
    c["norm_g_t"] = np.ascontiguousarray(ng.reshape(4, 4, KD, 128).transpose(3, 0, 1, 2).reshape(128, 4 * 4 * KD))
    cw = np.asarray(inputs["ffn_conv_w"], np.float32)
    c["ffn_conv_w_t"] = np.ascontiguousarray(cw.reshape(4, 3, NF, 128).transpose(3, 0, 1, 2).reshape(128, 4 * 3 * NF))
    cb = np.asarray(inputs["ffn_conv_b"], np.float32)
    c["ffn_conv_b_t"] = np.ascontiguousarray(cb.reshape(4, NF, 128).transpose(2, 0, 1).reshape(128, 4 * NF))
    mg = np.asarray(inputs["mem_norm_g"], np.float32)
    c["mem_norm_g_t"] = np.ascontiguousarray(mg.reshape(KD, 128).T)
    half = 32
    inv = (10000.0 ** (-np.arange(half, dtype=np.float32) / half)).astype(np.float32)
    ang = np.arange(T, dtype=np.float32)[None, :] * inv[:, None]
    cos = np.cos(ang).astype(np.float32)
    sin = np.sin(ang).astype(np.float32)
    c["rope_cos"] = np.ascontiguousarray(np.concatenate([cos, cos, cos, cos], 0))
    c["rope_sin"] = np.ascontiguousarray(np.concatenate([-sin, sin, -sin, sin], 0))
    b = np.arange(128)[:, None]
    a = np.arange(128)[None, :]
    c["mask_prev4"] = np.ascontiguousarray(np.tile((a <= b).astype(np.float32), (1, 4)))
    c["mask_next4"] = np.ascontiguousarray(np.tile((b <= a).astype(np.float32), (1, 4)))
    sk = np.asarray(inputs["a_sink"], np.float32)[0]
    c["a_sink_b"] = np.ascontiguousarray(np.broadcast_to(sk[None, :], (128, 16)))
    c["a_w_in_sw"] = _swap_halves_cols(np.asarray(inputs["a_w_in"], np.float32)[0], 1280)
    c["na_tab"] = na_tables(np.asarray(inputs["d_rpb"], np.float32)[0])
    r_ = np.arange(128)[:, None]
    t_ = np.arange(128)[None, :]
    c["tri_f"] = (r_ <= t_).astype(np.float32)
    c["tri_b"] = (r_ >= t_).astype(np.float32)
    ccw = np.asarray(inputs["c_conv_w"], np.float32)[0]
    c["c_conv_w_t"] = np.ascontiguousarray(ccw.reshape(5, 32, 128).transpose(2, 0, 1).reshape(128, 5 * 32))
    ccb = np.asarray(inputs["c_conv_b"], np.float32)[0]
    c["c_conv_b_t"] = np.ascontiguousarray(ccb.reshape(32, 128).T)
    c["c_dt_bias_b"] = np.ascontiguousarray(np.broadcast_to(np.asarray(inputs["c_dt_bias"], np.float32)[0].reshape(1, 64), (128, 64)))
    c["c_a_log_b"] = np.ascontiguousarray(np.broadcast_to(np.asarray(inputs["c_a_log"], np.float32)[0].reshape(1, 64), (128, 64)))
    c["c_d_b"] = np.ascontiguousarray(np.broadcast_to(np.asarray(inputs["c_d"], np.float32)[0].reshape(1, 32), (128, 32)))
    c["c_norm_g_t"] = np.ascontiguousarray(np.asarray(inputs["c_norm_g"], np.float32)[0].reshape(32, 64).T)
    same = (r_ // HG_C) == (t_ // HG_C)
    c["hg_mask_f"] = (same & (r_ <= t_)).astype(np.float32)
    c["hg_mask_b"] = (same & (r_ >= t_)).astype(np.float32)
    c["hg_cmask"] = ((np.arange(128)[:, None] // HG_C) == np.arange(128 // HG_C)[None, :]).astype(np.float32)
    lbl = np.asarray(inputs["b_lb_logits"], np.float32)
    c["b_lb_logits_t"] = np.ascontiguousarray(lbl.reshape(2, 4, KD, 128).transpose(3, 0, 1, 2).reshape(128, 64))
    c["b_norm_g_t"] = np.ascontiguousarray(np.asarray(inputs["b_norm_g"], np.float32)[0].reshape(KD, 128).T)
    return c


IN_SHAPES = {
    "c_ident": (128, 128), "norm_g_t": (128, 4 * 4 * KD), "ffn_conv_w_t": (128, 4 * 3 * NF),
    "ffn_conv_b_t": (128, 4 * NF), "mem_norm_g_t": (128, KD), "rope_cos": (128, T), "rope_sin": (128, T),
    "mask_prev4": (128, 512), "mask_next4": (128, 512), "a_sink_b": (128, 16), "a_w_in_sw": (D, 1280),
    "na_tab": (4, 21, 128, 512),
    "tri_f": (128, 128), "tri_b": (128, 128), "c_conv_w_t": (128, 160), "c_conv_b_t": (128, 32),
    "hg_mask_f": (128, 128), "hg_mask_b": (128, 128), "hg_cmask": (128, 4), "b_lb_logits_t": (128, 64),
    "b_norm_g_t": (128, KD),
    "c_dt_bias_b": (128, 64), "c_a_log_b": (128, 64), "c_d_b": (128, 32), "c_norm_g_t": (64, 32),
    "ffn_w_in": (4, D, 2 * DFF), "ffn_w_out": (4, DFF, D), "w_mem_kv": (4, D, 1024),
    "a_w_in": (1, D, 2048), "a_w_out": (1, 1536, D),
    "b_w_in": (1, D, 5632), "b_w_out": (1, 1536, D),
    "c_w_in": (1, D, 6720), "c_w_out": (1, 2560, D),
    "d_w_in": (1, D, 2048), "d_w_out": (1, 1536, D),
}
WEIGHTS = ["ffn_w_in", "ffn_w_out", "w_mem_kv", "a_w_in", "a_w_in_sw", "a_w_out", "b_w_in", "b_w_out",
           "c_w_in", "c_w_out", "d_w_in", "d_w_out"]


def build(cfg):
    nc = bass.Bass("TRN2", target_bir_lowering=False)
    stack = contextlib.ExitStack()
    dram = {}
    nseq = cfg.get("nseq", NSEQ)
    used = cfg.get("inputs", list(IN_SHAPES))
    dram["x"] = nc.dram_tensor("x", [nseq, T, D], F32, kind="ExternalInput").ap()
    dram["mem"] = nc.dram_tensor("mem", [nseq, MEM, D], F32, kind="ExternalInput").ap()
    for name in used:
        dram[name] = nc.dram_tensor(name, list(IN_SHAPES[name]), F32, kind="ExternalInput").ap()
    dram["y"] = nc.dram_tensor("y", [nseq, T, D], F32, kind="ExternalOutput").ap()
    with stack:
        k = K(nc, stack, dram)
        k.cfg = cfg
        k.setup()
        k.setup_seq_consts()
        k.precast([w for w in WEIGHTS if w in used])
        k.p.barrier()
        for s in range(nseq):
            k.mark()
            k.load_seq(s)
            k.release()
            k.p.barrier()
            if cfg.get("mixers", True):
                k.mem_norm(s)
                k.p.barrier()
            for l in cfg.get("layers", range(4)):
                if cfg.get("mixers", True):
                    k.layer(l)
                if cfg.get("ffn", True):
                    k.ffn(l)
                    k.p.barrier()
            k.mark()
            k.store_seq(s)
            k.release()
            k.p.barrier()
        k.p.finish()
        k.p.emit(stack)
    return nc, k


N_CORES = 8


def kernel(**inputs):
    x = np.ascontiguousarray(np.asarray(inputs["x"], np.float32))
    mem = np.ascontiguousarray(np.asarray(inputs["mem"], np.float32))
    consts = host_consts(inputs)
    shared = {}
    for name, shp in IN_SHAPES.items():
        src = consts[name] if name in consts else inputs[name]
        shared[name] = np.ascontiguousarray(np.asarray(src, np.float32).reshape(shp))
    nc, _ = build({})
    in_maps = []
    for c in range(N_CORES):
        m = dict(shared)
        m["x"] = x[c * NSEQ:(c + 1) * NSEQ]
        m["mem"] = mem[c * NSEQ:(c + 1) * NSEQ]
        in_maps.append(m)
    res = run_bass_kernel_spmd(nc, in_maps, core_ids=list(range(N_CORES)))
    return np.concatenate([r["y"] for r in res.results], axis=0)
```

```python
import contextlib
import numpy as np
import concourse.bass as bass
import concourse.mybir as mybir
from concourse.bass_utils import run_bass_kernel_spmd

F32 = mybir.dt.float32
BF16 = mybir.dt.bfloat16
AF = mybir.ActivationFunctionType
ALU = mybir.AluOpType
AX = mybir.AxisListType

ENGS = ("pe", "act", "dve", "pool", "sp")
SEM_EPOCH = 30000
N_DMA_SEMS = 24


class Reg:
    __slots__ = ("name", "lw", "rd", "rd_dma")

    def __init__(self, name=""):
        self.name = name
        self.lw = None
        self.rd = {}
        self.rd_dma = []


class Op:
    __slots__ = ("eng", "fn", "deps", "signal", "sem", "val", "is_dma", "epoch", "prev")

    def __init__(self, eng, fn, is_dma, epoch):
        self.eng = eng
        self.fn = fn
        self.is_dma = is_dma
        self.epoch = epoch
        self.deps = []
        self.signal = False
        self.sem = None
        self.val = 0
        self.prev = 0


class Prog:
    def __init__(self, nc):
        self.nc = nc
        self.ops = {e: [] for e in ENGS}
        self.last = {e: None for e in ENGS}
        self.dma_since = []
        self.epoch = 0
        self.nops = 0

    def add(self, eng, fn, r=(), w=(), dma=False, extra_deps=()):
        op = Op(eng, fn, dma, self.epoch)
        deps = []
        ep = self.epoch

        def need(d, kind):
            if d is None or d.epoch < ep:
                return
            if d.is_dma or dma:
                deps.append(d)
                return
            if d.eng == eng and eng == "pe":
                return
            deps.append(d)

        for g in r:
            need(g.lw, "raw")
        for g in w:
            need(g.lw, "waw")
            for o in g.rd.values():
                need(o, "war")
            for o in g.rd_dma:
                need(o, "war")
        deps.extend(extra_deps)
        seen = set()
        for d in deps:
            if d is op or id(d) in seen:
                continue
            seen.add(id(d))
            op.deps.append(d)
            d.signal = True
        for g in w:
            g.lw = op
            g.rd = {}
            g.rd_dma = []
        for g in r:
            if dma:
                g.rd_dma.append(op)
            else:
                g.rd[eng] = op
        self.ops[eng].append(op)
        if dma:
            self.dma_since.append(op)
        elif fn is not None:
            self.last[eng] = op
        self.nops += 1
        return op

    def barrier(self):
        deps = [self.last[e] for e in ("pe", "act", "dve", "pool")
                if self.last[e] is not None and self.last[e].epoch == self.epoch]
        deps += self.dma_since
        b = self.add("sp", lambda e: e.nop(), extra_deps=deps)
        b.signal = True
        self.epoch += 1
        self.dma_since = []
        b.epoch = self.epoch
        for e in ("pe", "act", "dve", "pool"):
            self.add(e, None, extra_deps=[b])

    def finish(self):
        deps = [self.last[e] for e in ("pe", "act", "dve", "pool") if self.last[e] is not None]
        deps += self.dma_since
        self.add("sp", None, extra_deps=deps)

    def emit(self, stack):
        nc = self.nc
        for e in ENGS:
            pool = None
            cnts = None
            rr = 0
            cur = None
            cnt = 0
            for op in self.ops[e]:
                if op.is_dma:
                    if pool is None:
                        pool = [stack.enter_context(nc.semaphore(f"dq_{e}_{i}")) for i in range(N_DMA_SEMS)]
                        cnts = [0] * N_DMA_SEMS
                    i = rr % N_DMA_SEMS
                    rr += 1
                    op.sem = pool[i]
                    op.prev = cnts[i]
                    cnts[i] += 16
                    op.val = cnts[i]
                elif op.signal:
                    if cur is None or cnt >= SEM_EPOCH:
                        cur = stack.enter_context(nc.semaphore(f"s_{e}_{len(self.ops[e])}_{cnt}_{id(op) % 9973}"))
                        cnt = 0
                    cnt += 1
                    op.sem = cur
                    op.val = cnt

        def run(ename, eh):
            known = {}
            for op in self.ops[ename]:
                waits = {}
                for d in op.deps:
                    if known.get(d.sem, 0) >= d.val:
                        continue
                    if waits.get(d.sem, 0) < d.val:
                        waits[d.sem] = d.val
                if op.is_dma and op.prev > 0 and known.get(op.sem, 0) < op.prev:
                    if waits.get(op.sem, 0) < op.prev:
                        waits[op.sem] = op.prev
                for sem, val in waits.items():
                    eh.wait_ge(sem, val)
                    known[sem] = val
                if op.fn is not None:
                    ins = op.fn(eh)
                    if op.is_dma:
                        ins.then_inc(op.sem, 16)
                    elif op.signal:
                        ins.then_inc(op.sem, 1)

        with nc.Block() as block:
            @block.sync
            def _(e):
                run("sp", e)

            @block.tensor
            def _(e):
                run("pe", e)

            @block.scalar
            def _(e):
                run("act", e)

            @block.vector
            def _(e):
                run("dve", e)

            @block.gpsimd
            def _(e):
                run("pool", e)


D = 1024
T = 2048
KD = D // 128
NSEQ = 2
MEM = 256
DFF = 2816
NF = DFF // 128
EPS = 1e-6


class K:
    def __init__(self, nc, stack, dram):
        self.nc = nc
        self.p = Prog(nc)
        self.stack = stack
        self.dram = dram
        p = self.p
        self.ps = []
        self.ps_reg = []
        for i in range(8):
            t = stack.enter_context(nc.psum_tensor(f"ps{i}", [128, 512], F32))
            self.ps.append(t)
            self.ps_reg.append(Reg(f"ps{i}"))
        self.ps_rr = 0
        self.ARENA_W = 53200
        self.arena = stack.enter_context(nc.sbuf_tensor("arena", [128, self.ARENA_W], F32))
        self.arena_bf = self.arena.bitcast(BF16)
        self.bump = 0
        self.marks = []

    def alloc(self, n, dt=F32, name=""):
        words = (n + 1) // 2 if dt == BF16 else n
        words = (words + 15) // 16 * 16
        off = self.bump
        self.bump += words
        assert self.bump <= self.ARENA_W, f"arena overflow at {name}: {self.bump}"
        if dt == BF16:
            ap = self.arena_bf[:, 2 * off: 2 * off + n]
        else:
            ap = self.arena[:, off: off + n]
        return ap, Reg(name)

    def mark(self):
        self.marks.append(self.bump)

    def release(self):
        self.bump = self.marks.pop()

    def psum(self):
        i = self.ps_rr % 8
        self.ps_rr += 1
        return self.ps[i], self.ps_reg[i]

    def dma(self, out, in_, r=(), w=(), eng="sp"):
        return self.p.add(eng, lambda e: e.dma_start(out=out, in_=in_), r=r, w=w, dma=True)

    def mm(self, out, lhsT, rhs, start, stop, r=(), w=()):
        return self.p.add("pe", lambda e: e.matmul(out, lhsT, rhs, start=start, stop=stop), r=r, w=w)

    def tr(self, out, in_, ident, r=(), w=()):
        return self.p.add("pe", lambda e: e.transpose(out, in_, ident), r=r, w=w)

    def act(self, out, in_, func, r=(), w=(), bias=None, scale=None, eng="act"):
        kw = {}
        if bias is not None:
            kw["bias"] = bias
        if scale is not None:
            kw["scale"] = scale
        return self.p.add("act", lambda e: e.activation(out, in_, func, **kw), r=r, w=w)

    def v(self, eng, fn, r=(), w=()):
        return self.p.add(eng, fn, r=r, w=w)

    def ts(self, eng, out, in0, s1, s2, op0, op1, r=(), w=()):
        if s2 is None:
            return self.p.add(eng, lambda e: e.tensor_scalar(out, in0, s1, None, op0), r=r, w=w)
        return self.p.add(eng, lambda e: e.tensor_scalar(out, in0, s1, s2, op0, op1), r=r, w=w)

    def stt(self, out, in0, scalar, in1, op0, op1, r=(), w=()):
        return self.p.add("dve", lambda e: e.scalar_tensor_tensor(out=out, in0=in0, scalar=scalar, in1=in1,
                                                                  op0=op0, op1=op1), r=r, w=w)

    def tt(self, eng, out, in0, in1, op, r=(), w=()):
        return self.p.add(eng, lambda e: e.tensor_tensor(out=out, in0=in0, in1=in1, op=op), r=r, w=w)

    def cp(self, eng, out, in_, r=(), w=()):
        if eng == "act":
            return self.p.add("act", lambda e: e.copy(out, in_), r=r, w=w)
        return self.p.add(eng, lambda e: e.tensor_copy(out, in_), r=r, w=w)

    def ms(self, eng, out, val, w=()):
        return self.p.add(eng, lambda e: e.memset(out, val), w=w)

    def rcp(self, out, in_, r=(), w=()):
        return self.p.add("dve", lambda e: e.reciprocal(out, in_), r=r, w=w)


def _v3(ap, a):
    return ap.rearrange("p (a b) -> p a b", a=a)


class Ring:
    def __init__(self, items):
        self.items = items
        self.i = 0

    def next(self):
        it = self.items[self.i % len(self.items)]
        self.i += 1
        return it


def k_setup(self):
    dram = self.dram
    self.identf, self.identf_r = self.alloc(128, F32, "identf")
    self.identb, self.identb_r = self.alloc(128, BF16, "identb")
    self.onesm, self.onesm_r = self.alloc(128, BF16, "onesm")
    self.onesb, self.onesb_r = self.alloc(128, BF16, "onesb")
    self.dma(self.identf, dram["c_ident"], w=[self.identf_r])
    self.cp("dve", self.identb, self.identf, r=[self.identf_r], w=[self.identb_r])
    self.ms("pool", self.onesm, 1.0 / 1024.0, w=[self.onesm_r])
    self.ms("pool", self.onesb, 1.0, w=[self.onesb_r])
    self.epsc, self.epsc_r = self.alloc(16, F32, "epsc")
    self.ms("pool", self.epsc, EPS, w=[self.epsc_r])
    self.onec, self.onec_r = self.alloc(16, F32, "onec")
    self.ms("pool", self.onec, 1.0, w=[self.onec_r])
    self.catm_r = Reg("catm_dram")
    self.ng, self.ng_r = self.alloc(4 * 4 * KD, F32, "norm_g")
    self.dma(self.ng, dram["norm_g_t"], w=[self.ng_r])
    self.fcw, self.fcw_r = self.alloc(4 * 3 * NF, F32, "ffn_conv_w")
    self.dma(self.fcw, dram["ffn_conv_w_t"], w=[self.fcw_r])
    self.fcb, self.fcb_r = self.alloc(4 * NF, F32, "ffn_conv_b")
    self.dma(self.fcb, dram["ffn_conv_b_t"], w=[self.fcb_r])
    hT, _ = self.alloc(KD * T, F32, "hT")
    self.hT = _v3(hT, KD)
    self.hreg = [Reg(f"h{t}") for t in range(4)]
    self.wb = {}


def k_psum(self):
    i = self.ps_rr % 7
    self.ps_rr += 1
    return self.ps[i], self.ps_reg[i]


def k_wload(self, dst, dst_reg, src, nfree=None, shape3=None):
    self.dma(dst, src, w=[dst_reg])


def k_precast(self, names):
    self.mark()
    ring = Ring([self.alloc(2048, F32, f"pc_in{i}") for i in range(6)])
    ringo = Ring([self.alloc(2048, BF16, f"pc_out{i}") for i in range(6)])
    rr = 0
    for name in names:
        src = self.dram[name]
        shp = list(src.shape)
        dst_t = self.nc.dram_tensor(name + "_bf", shp, BF16)
        dst = dst_t.ap()
        self.wb[name] = dst
        if len(shp) == 3:
            src2 = src.rearrange("l r c -> (l r) c")
            dst2 = dst.rearrange("l r c -> (l r) c")
        else:
            src2, dst2 = src, dst
        R, C = src2.shape
        assert R % 128 == 0
        for rb in range(R // 128):
            for c0 in range(0, C, 2048):
                cw = min(2048, C - c0)
                a, a_r = ring.next()
                b, b_r = ringo.next()
                self.dma(a[:, 0:cw], src2[rb * 128:(rb + 1) * 128, c0:c0 + cw], w=[a_r])
                eng = ("act", "dve", "act", "dve", "pool")[rr % 5]
                rr += 1
                self.cp(eng, b[:, 0:cw], a[:, 0:cw], r=[a_r], w=[b_r])
                self.dma(dst2[rb * 128:(rb + 1) * 128, c0:c0 + cw], b[:, 0:cw], r=[b_r],
                         eng=self.cfg.get("pc_store_eng", "act"))
    self.release()


def k_load_seq(self, s):
    x = self.dram["x"]
    ring = Ring([self.alloc(1024, F32, f"xin{i}") for i in range(2)])
    for tb in range(16):
        xt, xr = ring.next()
        self.dma(xt, x[s, tb * 128:(tb + 1) * 128, :], w=[xr])
        for half in range(2):
            ps, pr = self.psum()
            for j in range(4):
                k = half * 4 + j
                self.tr(ps[:, j * 128:(j + 1) * 128], xt[:, k * 128:(k + 1) * 128], self.identf,
                        r=[xr, self.identf_r], w=[pr])
            dst = self.hT[:, half * 4:half * 4 + 4, tb * 128:(tb + 1) * 128]
            src = _v3(ps[:, 0:512], 4)
            self.cp("act" if half == 0 else "dve", dst, src, r=[pr], w=[self.hreg[tb // 4]])


def k_store_seq(self, s):
    y = self.dram["y"]
    ring = Ring([self.alloc(1024, F32, f"xout{i}") for i in range(2)])
    for tb in range(16):
        xt, xr = ring.next()
        for half in range(2):
            ps, pr = self.psum()
            for j in range(4):
                k = half * 4 + j
                self.tr(ps[:, j * 128:(j + 1) * 128], self.hT[:, k, tb * 128:(tb + 1) * 128], self.identf,
                        r=[self.hreg[tb // 4], self.identf_r], w=[pr])
            self.cp("act" if half == 0 else "dve", xt[:, half * 512:(half + 1) * 512], ps[:, 0:512], r=[pr], w=[xr])
        self.dma(y[s, tb * 128:(tb + 1) * 128, :], xt, r=[xr])


def k_rms_stats(self, sq, sq_r, n):
    ps, pr = self.psum()
    for k in range(KD):
        self.mm(ps[:, 0:n], self.onesm, sq[:, k, :], k == 0, k == KD - 1, r=[sq_r, self.onesm_r], w=[pr])
    rs, rs_r = self.rs_ring.next()
    self.act(rs[:, 0:n], ps[:, 0:n], AF.Ln, r=[pr, self.epsc_r], w=[rs_r], bias=self.epsc[:, 0:1])
    self.act(rs[:, 0:n], rs[:, 0:n], AF.Exp, r=[rs_r], w=[rs_r], scale=-0.5)
    return rs, rs_r


def _hregs(self, lo, hi):
    lo = max(lo, 0)
    hi = min(hi, T)
    return [self.hreg[b] for b in range(lo // 512, (hi - 1) // 512 + 1)]


def k_norm_u(self, t_lo, t_hi, gofs, U, U_r):
    n = t_hi - t_lo
    a = max(t_lo, 0)
    b = min(t_hi, T)
    ja, jb = a - t_lo, b - t_lo
    hr = _hregs(self, a, b)
    if ja > 0:
        self.ms("pool", U[:, :, 0:ja], 0.0, w=[U_r])
    if jb < n:
        self.ms("pool", U[:, :, jb:n], 0.0, w=[U_r])
    sq, sq_r = self.sq_ring.next()
    m = b - a
    self.act(sq[:, :, 0:m], self.hT[:, :, a:b], AF.Square, r=hr, w=[sq_r])
    rs, rs_r = self.rms_stats(sq[:, :, 0:m], sq_r, m)
    for k in range(KD):
        self.stt(U[:, k, ja:jb], self.hT[:, k, a:b], self.ng[:, gofs + k:gofs + k + 1], rs[:, 0:m],
                 ALU.mult, ALU.mult, r=hr + [rs_r, self.ng_r], w=[U_r])


def k_resid_norm(self, t0, n, ybuf, y_r, sq, sq_r, gofs):
    sl = slice(t0, t0 + n)
    hr = _hregs(self, t0, t0 + n)
    rs, rs_r = self.rms_stats(sq[:, :, 0:n], sq_r, n)
    for k in range(KD):
        self.stt(ybuf[:, k, 0:n], ybuf[:, k, 0:n], self.ng[:, gofs + k:gofs + k + 1], rs[:, 0:n],
                 ALU.mult, ALU.mult, r=[y_r, rs_r, self.ng_r], w=[y_r])
        self.tt("pool", self.hT[:, k, sl], self.hT[:, k, sl], ybuf[:, k, 0:n], ALU.add, r=[y_r] + hr, w=hr)


GELU_MODE = "tanh_act"
FFN_TILES = [(0, 510), (510, 510), (1020, 510), (1530, 510), (2040, 8)]


def k_ffn(self, l):
    self.mark()
    self.sq_ring = Ring([(_v3(a, KD), r) for a, r in [self.alloc(KD * 512, BF16, "sq")]])
    self.rs_ring = Ring([self.alloc(512, F32, f"rs{i}") for i in range(2)])
    Us = []
    for i in range(2):
        a, r = self.alloc(KD * 512, BF16, f"U{i}")
        Us.append((_v3(a, KD), r))
    hid_a, hid_r = self.alloc(NF * 512, BF16, "hid")
    hid = _v3(hid_a, NF)
    y_a, y_r = self.alloc(KD * 512, F32, "yffn")
    ybuf = _v3(y_a, KD)
    acc_ring = Ring([self.alloc(512, F32, f"acc{i}") for i in range(2)])
    gl_ring = Ring([self.alloc(512, F32, f"gl{i}") for i in range(2)])
    ub_ring = Ring([self.alloc(512, F32, f"ub{i}") for i in range(2)])
    wg_ring = Ring([(lambda ar: (_v3(ar[0], KD), ar[1]))(self.alloc(KD * 256, BF16, f"wg{i}")) for i in range(2)])
    wu_ring = Ring([(lambda ar: (_v3(ar[0], KD), ar[1]))(self.alloc(KD * 256, BF16, f"wu{i}")) for i in range(2)])
    wo_ring = Ring([(lambda ar: (_v3(ar[0], 2), ar[1]))(self.alloc(2 * 512, BF16, f"wo{i}")) for i in range(3)])
    w_in = self.wb["ffn_w_in"]
    w_out = self.wb["ffn_w_out"]
    g2 = (l * 4 + 2) * KD
    g3 = (l * 4 + 3) * KD
    cw0 = (l * 3 + 0) * NF
    cw1 = (l * 3 + 1) * NF
    cw2 = (l * 3 + 2) * NF
    cb = l * NF
    tiles = self.cfg.get("ffn_tiles", FFN_TILES)

    def make_u(i):
        t0_, n_ = tiles[i]
        U_, U_r_ = Us[i % 2]
        self.norm_u(t0_ - 1, t0_ + n_ + 1, g2, U_, U_r_)
        return U_, U_r_

    cur = make_u(0)
    for ti, (t0, n) in enumerate(tiles):
        U, U_r = cur
        nxt = make_u(ti + 1) if ti + 1 < len(tiles) else None
        for fb in range(NF // 2):
            c0 = fb * 256
            wg, wg_r = wg_ring.next()
            wu, wu_r = wu_ring.next()
            self.wload(wg, wg_r, w_in[l, :, c0:c0 + 256].rearrange("(k p) n -> p k n", p=128), KD * 256, KD)
            self.wload(wu, wu_r, w_in[l, :, DFF + c0:DFF + c0 + 256].rearrange("(k p) n -> p k n", p=128),
                       KD * 256, KD)
            for j in range(2):
                f = fb * 2 + j
                psg, psg_r = self.psum()
                psu, psu_r = self.psum()
                for k in range(KD):
                    self.mm(psg[:, 0:n + 2], wg[:, k, j * 128:(j + 1) * 128], U[:, k, 0:n + 2], k == 0, k == KD - 1,
                            r=[wg_r, U_r], w=[psg_r])
                for k in range(KD):
                    self.mm(psu[:, 0:n], wu[:, k, j * 128:(j + 1) * 128], U[:, k, 1:n + 1], k == 0, k == KD - 1,
                            r=[wu_r, U_r], w=[psu_r])
                acc, acc_r = acc_ring.next()
                gl, gl_r = gl_ring.next()
                ub, ub_r = ub_ring.next()
                A = acc[:, 0:n]
                G = gl[:, 0:n]
                self.act(A, psg[:, 1:n + 1], AF.Identity, r=[psg_r, self.fcw_r, self.fcb_r], w=[acc_r],
                         bias=self.fcb[:, cb + f:cb + f + 1], scale=self.fcw[:, cw1 + f:cw1 + f + 1])
                self.stt(A, psg[:, 0:n], self.fcw[:, cw0 + f:cw0 + f + 1], A, ALU.mult, ALU.add,
                         r=[psg_r, acc_r, self.fcw_r], w=[acc_r])
                self.stt(A, psg[:, 2:n + 2], self.fcw[:, cw2 + f:cw2 + f + 1], A, ALU.mult, ALU.add,
                         r=[psg_r, acc_r, self.fcw_r], w=[acc_r])
                self.act(G, A, AF.Gelu_apprx_tanh, r=[acc_r], w=[gl_r])
                self.cp("act", ub[:, 0:n], psu[:, 0:n], r=[psu_r], w=[ub_r])
                self.tt("pool", hid[:, f, 0:n], G, ub[:, 0:n], ALU.mult, r=[gl_r, ub_r], w=[hid_r])
        sq, sq_r = self.sq_ring.next()
        for half in range(2):
            banks = [self.psum() for _ in range(4)]
            for fb in range(NF // 2):
                wo, wo_r = wo_ring.next()
                self.wload(wo, wo_r,
                           w_out[l, fb * 256:(fb + 1) * 256, half * 512:(half + 1) * 512].rearrange(
                               "(j p) n -> p j n", p=128), 2 * 512, 2)
                for j in range(2):
                    f = fb * 2 + j
                    for q in range(4):
                        self.mm(banks[q][0][:, 0:n], wo[:, j, q * 128:(q + 1) * 128], hid[:, f, 0:n],
                                f == 0, f == NF - 1, r=[wo_r, hid_r], w=[banks[q][1]])
            for q in range(4):
                dk = half * 4 + q
                self.cp("act", ybuf[:, dk, 0:n], banks[q][0][:, 0:n], r=[banks[q][1]], w=[y_r])
                self.act(sq[:, dk, 0:n], banks[q][0][:, 0:n], AF.Square, r=[banks[q][1]], w=[sq_r])
        self.resid_norm(t0, n, ybuf, y_r, sq, sq_r, g3)
        cur = nxt
    self.release()


K.setup = k_setup
K.psum = k_psum
K.wload = k_wload
K.precast = k_precast
K.load_seq = k_load_seq
K.store_seq = k_store_seq
K.rms_stats = k_rms_stats
K.norm_u = k_norm_u
K.resid_norm = k_resid_norm
K.ffn = k_ffn


XW = 512
XSCALE = 128.0 ** -0.5
ASCALE = 64.0 ** -0.5


def k_setup_seq_consts(self):
    d = self.dram
    self.memn_a, self.memn_r = self.alloc(KD * MEM, BF16, "memn")
    self.memn = _v3(self.memn_a, KD)
    self.mng, self.mng_r = self.alloc(KD, F32, "mem_norm_g")
    self.dma(self.mng, d["mem_norm_g_t"], w=[self.mng_r])


def k_mem_norm(self, s):
    self.mark()
    self.sq_ring = Ring([(_v3(a, KD), r) for a, r in [self.alloc(KD * 512, BF16, "sq")]])
    self.rs_ring = Ring([self.alloc(512, F32, f"rs{i}") for i in range(2)])
    mT_a, mT_r = self.alloc(KD * MEM, F32, "memT")
    mT = _v3(mT_a, KD)
    ring = Ring([self.alloc(1024, F32, f"min{i}") for i in range(2)])
    for mb in range(2):
        xt, xr = ring.next()
        self.dma(xt, self.dram["mem"][s, mb * 128:(mb + 1) * 128, :], w=[xr])
        for half in range(2):
            ps, pr = self.psum()
            for j in range(4):
                k = half * 4 + j
                self.tr(ps[:, j * 128:(j + 1) * 128], xt[:, k * 128:(k + 1) * 128], self.identf,
                        r=[xr, self.identf_r], w=[pr])
            self.cp("act" if half == 0 else "dve", mT[:, half * 4:half * 4 + 4, mb * 128:(mb + 1) * 128],
                    _v3(ps[:, 0:512], 4), r=[pr], w=[mT_r])
    sq, sq_r = self.sq_ring.next()
    self.act(sq[:, :, 0:MEM], mT, AF.Square, r=[mT_r], w=[sq_r])
    rs, rs_r = self.rms_stats(sq[:, :, 0:MEM], sq_r, MEM)
    for k in range(KD):
        self.stt(self.memn[:, k, :], mT[:, k, :], self.mng[:, k:k + 1], rs[:, 0:MEM], ALU.mult, ALU.mult,
                 r=[mT_r, rs_r, self.mng_r], w=[self.memn_r])
    self.release()


def k_make_u_full(self, gofs, a=None):
    if a is None:
        a, _ = self.alloc(KD * T, BF16, "Ufull")
    U = _v3(a, KD)
    regs = [Reg(f"U{t}") for t in range(4)]
    for tt in range(4):
        self.norm_u(tt * 512, (tt + 1) * 512, gofs, U[:, :, tt * 512:(tt + 1) * 512], regs[tt])
    return U, regs


def k_xattn(self, l, w_in, col0, CATX, catx_r):
    self.mark()
    self.sq_ring = Ring([(_v3(a, KD), r) for a, r in [self.alloc(KD * 512, BF16, "sq")]])
    self.rs_ring = Ring([self.alloc(512, F32, f"rs{i}") for i in range(2)])
    U, U_r = self.make_u_full((l * 4 + 0) * KD)
    wmk = self.wb["w_mem_kv"]
    wk_ring = Ring([(lambda ar: (_v3(ar[0], KD), ar[1]))(self.alloc(KD * 256, BF16, f"xwk{i}")) for i in range(2)])
    Kmem_a, Kmem_r = self.alloc(4 * MEM, BF16, "Kmem")
    Kmem = _v3(Kmem_a, 4)
    Vmem_a, Vmem_r = self.alloc(2 * XW, BF16, "Vmem")
    Vmem = _v3(Vmem_a, 2)
    for hp in range(2):
        wk, wk_r = wk_ring.next()
        self.wload(wk, wk_r, wmk[l, :, hp * 256:(hp + 1) * 256].rearrange("(k p) n -> p k n", p=128))
        for hh in range(2):
            h = hp * 2 + hh
            ps, pr = self.psum()
            for k in range(KD):
                self.mm(ps[:, 0:MEM], wk[:, k, hh * 128:(hh + 1) * 128], self.memn[:, k, :], k == 0, k == KD - 1,
                        r=[wk_r, self.memn_r], w=[pr])
            self.cp("act", Kmem[:, h, :], ps[:, 0:MEM], r=[pr], w=[Kmem_r])
    for vp in range(2):
        wk, wk_r = wk_ring.next()
        self.wload(wk, wk_r, wmk[l, :, XW + vp * 256:XW + (vp + 1) * 256].rearrange("(k p) n -> p k n", p=128))
        for mb in range(2):
            ps, pr = self.psum()
            for k in range(KD):
                self.mm(ps[:, 0:256], self.memn[:, k, mb * 128:(mb + 1) * 128], wk[:, k, :], k == 0, k == KD - 1,
                        r=[wk_r, self.memn_r], w=[pr])
            self.cp("act", Vmem[:, mb, vp * 256:(vp + 1) * 256], ps[:, 0:256], r=[pr], w=[Vmem_r])
    xq_ring = Ring([self.alloc(512, BF16, f"xq{i}") for i in range(2)])
    e_ring = Ring([self.alloc(512, BF16, f"xe{i}") for i in range(4)])
    rd_ring = Ring([self.alloc(512, F32, f"xrd{i}") for i in range(2)])
    for hp in range(2):
        wq, wq_r = wk_ring.next()
        self.wload(wq, wq_r, w_in[:, col0 + hp * 256:col0 + (hp + 1) * 256].rearrange("(k p) n -> p k n", p=128))
        for hh in range(2):
            h = hp * 2 + hh
            for tt in range(4):
                sl = slice(tt * 512, (tt + 1) * 512)
                ps, pr = self.psum()
                for k in range(KD):
                    self.mm(ps[:, 0:512], wq[:, k, hh * 128:(hh + 1) * 128], U[:, k, sl], k == 0, k == KD - 1,
                            r=[wq_r, U_r[tt]], w=[pr])
                xq, xq_r = xq_ring.next()
                self.cp("act", xq, ps[:, 0:512], r=[pr], w=[xq_r])
                es = []
                for mb in range(2):
                    pss, pss_r = self.psum()
                    self.mm(pss[:, 0:512], Kmem[:, h, mb * 128:(mb + 1) * 128], xq, True, True,
                            r=[Kmem_r, xq_r], w=[pss_r])
                    e, e_r = e_ring.next()
                    self.act(e, pss[:, 0:512], AF.Exp, r=[pss_r], w=[e_r], scale=XSCALE)
                    es.append((e, e_r))
                pv, pv_r = self.psum()
                den, den_r = self.psum()
                for mb in range(2):
                    self.mm(pv[:, 0:512], Vmem[:, mb, h * 128:(h + 1) * 128], es[mb][0], mb == 0, mb == 1,
                            r=[Vmem_r, es[mb][1]], w=[pv_r])
                for mb in range(2):
                    self.mm(den[:, 0:512], self.onesb, es[mb][0], mb == 0, mb == 1,
                            r=[self.onesb_r, es[mb][1]], w=[den_r])
                rd, rd_r = rd_ring.next()
                self.rcp(rd, den[:, 0:512], r=[den_r], w=[rd_r])
                self.tt("dve", CATX[:, h, sl], pv[:, 0:512], rd, ALU.mult, r=[pv_r, rd_r], w=[catx_r[h][tt]])
    self.release()


def k_out_proj(self, l, chunks, w_out, gofs):
    self.mark()
    self.sq_ring = Ring([(_v3(a, KD), r) for a, r in [self.alloc(KD * 512, BF16, "sq")]])
    self.rs_ring = Ring([self.alloc(512, F32, f"rs{i}") for i in range(2)])
    nch = len(chunks)
    wo_a, _ = self.alloc(nch * D, BF16, "wo_res")
    wo = _v3(wo_a, nch)
    wo_regs = []
    for c, (apf, rf, kp, row0) in enumerate(chunks):
        r_ = Reg(f"wo{c}")
        wo_regs.append(r_)
        self.wload(wo[0:kp, c, :], r_, w_out[row0:row0 + kp, :])
    y_a, y_r = self.alloc(KD * 512, F32, "ymix")
    ybuf = _v3(y_a, KD)
    for tt in range(4):
        sq, sq_r = self.sq_ring.next()
        for half in range(2):
            banks = [self.psum() for _ in range(4)]
            for c, (apf, rf, kp, row0) in enumerate(chunks):
                if rf is None:
                    cap, cregs = apf(tt)
                else:
                    cap, cregs = apf(tt), rf(tt)
                for q in range(4):
                    dk = half * 4 + q
                    self.mm(banks[q][0][:, 0:512], wo[0:kp, c, dk * 128:(dk + 1) * 128], cap,
                            c == 0, c == nch - 1, r=[wo_regs[c]] + cregs, w=[banks[q][1]])
            for q in range(4):
                dk = half * 4 + q
                self.cp("act", ybuf[:, dk, :], banks[q][0][:, 0:512], r=[banks[q][1]], w=[y_r])
                self.act(sq[:, dk, :], banks[q][0][:, 0:512], AF.Square, r=[banks[q][1]], w=[sq_r])
        self.resid_norm(tt * 512, 512, ybuf, y_r, sq, sq_r, gofs)
    self.release()


def k_band_attn(self, QT, qt_r, KLO, KUP, k2_r, V2, v2_r, pairs_fn, tables, scale, esk=None, group_prologue=None):
    e_ring = Ring([self.alloc(512, BF16, f"ae{i}") for i in range(6)])
    rd_ring = Ring([self.alloc(512, F32, f"ard{i}") for i in range(2)])
    for g in range(4):
        if group_prologue is not None:
            group_prologue(g)
        for i in range(16):
            qs = slice(i * 128, (i + 1) * 128)
            prs = pairs_fn(g, i)
            es = []
            for (j, tk) in prs:
                ks = slice(j * 128, (j + 1) * 128)
                ps, pr = self.psum()
                self.mm(ps[:, 0:256], KLO[:, g, ks], QT[:, 2 * g:2 * g + 2, qs], True, True,
                        r=[k2_r[g][j // 4], qt_r[g][i]], w=[pr])
                self.mm(ps[:, 256:512], KUP[:, g, ks], QT[:, 2 * g:2 * g + 2, qs], True, True,
                        r=[k2_r[g][j // 4], qt_r[g][i]], w=[pr])
                e, e_r = e_ring.next()
                self.act(e, ps[:, 0:512], AF.Exp, r=[pr], w=[e_r], scale=scale)
                if tk is not None:
                    tap, t_r = tables[tk]
                    self.tt("pool", e, e, tap, ALU.mult, r=[e_r, t_r], w=[e_r])
                es.append((j, e, e_r))
            pv, pv_r = self.psum()
            den, den_r = self.psum()
            n = len(es)
            for s_ in range(4):
                for a, (j, e, e_r) in enumerate(es):
                    self.mm(pv[:, s_ * 128:(s_ + 1) * 128], V2[:, j, g, :], e[:, s_ * 128:(s_ + 1) * 128],
                            a == 0, a == n - 1, r=[v2_r[j], e_r], w=[pv_r])
            for a, (j, e, e_r) in enumerate(es):
                self.mm(den[:, 0:512], self.onesb, e, a == 0, a == n - 1,
                        r=[self.onesb_r, e_r], w=[den_r])
            rd, rd_r = rd_ring.next()
            if esk is not None:
                for s_, h in enumerate([4 * g, 4 * g + 2, 4 * g + 1, 4 * g + 3]):
                    self.ts("dve", rd[:, s_ * 128:(s_ + 1) * 128], den[:, s_ * 128:(s_ + 1) * 128],
                            esk[0][:, h:h + 1], None, ALU.add, None, r=[den_r, esk[1]], w=[rd_r])
                self.rcp(rd, rd, r=[rd_r], w=[rd_r])
            else:
                self.rcp(rd, den[:, 0:512], r=[den_r], w=[rd_r])
            self.tt("dve", QT[0:64, 2 * g:2 * g + 2, qs], _v3(pv[0:64, 0:256], 2), _v3(rd[0:64, 0:256], 2),
                    ALU.mult, r=[pv_r, rd_r], w=[qt_r[g][i]])
            self.tt("dve", QT[64:128, 2 * g:2 * g + 2, qs], _v3(pv[64:128, 256:512], 2),
                    _v3(rd[64:128, 256:512], 2), ALU.mult, r=[pv_r, rd_r], w=[qt_r[g][i]])


def k_attn_proj(self, U, U_r, w_in, w_sw, QT, qt_r, KLO, KUP, k2_r, V2, v2_r, rope):
    w_ring = Ring([(lambda ar: (_v3(ar[0], KD), ar[1]))(self.alloc(KD * 128, BF16, f"aw{i}")) for i in range(4)])
    if rope:
        cs_ring = Ring([self.alloc(512, F32, f"cos{i}") for i in range(1)])
        sn_ring = Ring([self.alloc(512, F32, f"sin{i}") for i in range(1)])
        t1_ring = Ring([self.alloc(512, F32, f"rt1{i}") for i in range(2)])
        t2_ring = Ring([self.alloc(512, F32, f"rt2{i}") for i in range(1)])

    def wfill(wt, wt_r, wsrc, wcols, place):
        if place is None:
            self.wload(wt, wt_r, wsrc[:, wcols].rearrange("(k p) n -> p k n", p=128))
        else:
            lo = 0 if place == "lo" else 64
            self.ms("pool", wt[:, :, 64 - lo:128 - lo], 0.0, w=[wt_r])
            self.wload(wt[:, :, lo:lo + 64], wt_r, wsrc[:, wcols].rearrange("(k p) n -> p k n", p=128))

    def proj(dst_fn, dst_regs_fn, wcols, place):
        w1, w1_r = w_ring.next()
        wfill(w1, w1_r, w_in, wcols, place)
        if rope:
            w2, w2_r = w_ring.next()
            wfill(w2, w2_r, w_sw, wcols, place)
        for tt in range(4):
            sl = slice(tt * 512, (tt + 1) * 512)
            ps1, p1_r = self.psum()
            for k in range(KD):
                self.mm(ps1[:, 0:512], w1[:, k, :], U[:, k, sl], k == 0, k == KD - 1, r=[w1_r, U_r[tt]], w=[p1_r])
            if not rope:
                self.cp("act", dst_fn(tt), ps1[:, 0:512], r=[p1_r], w=dst_regs_fn(tt))
                continue
            ps2, p2_r = self.psum()
            for k in range(KD):
                self.mm(ps2[:, 0:512], w2[:, k, :], U[:, k, sl], k == 0, k == KD - 1, r=[w2_r, U_r[tt]], w=[p2_r])
            cs, cs_r = cs_ring.next()
            sn, sn_r = sn_ring.next()
            self.dma(cs, self.dram["rope_cos"][:, sl], w=[cs_r])
            self.dma(sn, self.dram["rope_sin"][:, sl], w=[sn_r])
            t1, t1_r = t1_ring.next()
            t2, t2_r = t2_ring.next()
            self.tt("dve", t1, ps1[:, 0:512], cs, ALU.mult, r=[p1_r, cs_r], w=[t1_r])
            self.tt("dve", t2, ps2[:, 0:512], sn, ALU.mult, r=[p2_r, sn_r], w=[t2_r])
            self.tt("pool", dst_fn(tt), t1, t2, ALU.add, r=[t1_r, t2_r], w=dst_regs_fn(tt))

    for p_ in range(8):
        g = p_ // 2
        proj(lambda tt, p_=p_: QT[:, p_, tt * 512:(tt + 1) * 512],
             lambda tt, g=g: [qt_r[g][i] for i in range(tt * 4, tt * 4 + 4)],
             slice(p_ * 128, (p_ + 1) * 128), None)
    for g in range(4):
        for KX, place in ((KLO, "lo"), (KUP, "up")):
            proj(lambda tt, g=g, KX=KX: KX[:, g, tt * 512:(tt + 1) * 512],
                 lambda tt, g=g: [k2_r[g][tt]],
                 slice(1024 + g * 64, 1024 + (g + 1) * 64), place)
    wv_a, wv_r = self.alloc(KD * 256, BF16, "awv")
    wv = _v3(wv_a, KD)
    self.wload(wv, wv_r, w_in[:, 1280:1536].rearrange("(k p) n -> p k n", p=128))
    for blk in range(16):
        ps, pr = self.psum()
        for k in range(KD):
            self.mm(ps[:, 0:256], U[:, k, blk * 128:(blk + 1) * 128], wv[:, k, :], k == 0, k == KD - 1,
                    r=[wv_r, U_r[blk // 4]], w=[pr])
        src = _v3(ps[:, 0:256], 4)
        self.cp("act", V2[:, blk, :, 0:64], src, r=[pr], w=[v2_r[blk]])
        self.cp("dve", V2[:, blk, :, 64:128], src, r=[pr], w=[v2_r[blk]])


def k_mixer_attn(self, l, kind):
    rope = (kind == 0)
    skip = self.cfg.get("skip", ())
    w_in = self.wb["a_w_in" if rope else "d_w_in"][0]
    w_sw = self.wb["a_w_in_sw"] if rope else None
    QT_a, _ = self.alloc(8 * T, BF16, "QT")
    QT = _v3(QT_a, 8)
    qt_r = [[Reg(f"qt{g}_{i}") for i in range(16)] for g in range(4)]
    self.mark()
    KL_a, _ = self.alloc(4 * T, BF16, "KLO")
    KLO = _v3(KL_a, 4)
    KU_a, _ = self.alloc(4 * T, BF16, "KUP")
    KUP = _v3(KU_a, 4)
    k2_r = [[Reg(f"k2{g}_{t}") for t in range(4)] for g in range(4)]
    V2_a, _ = self.alloc(16 * 4 * 128, BF16, "V2")
    V2 = V2_a.rearrange("p (b g d) -> p b g d", b=16, g=4)
    v2_r = [Reg(f"v2{b}") for b in range(16)]
    tables = {}
    esk = None
    if rope:
        for nm in ("mask_prev4", "mask_next4"):
            a, r_ = self.alloc(512, BF16, nm)
            tables[nm] = (a, r_)
        esk = self.alloc(16, F32, "esk")
    self.mark()
    Ua, _ = self.alloc(KD * T, BF16, "Ufull")
    self.mark()
    self.sq_ring = Ring([(_v3(a, KD), r) for a, r in [self.alloc(KD * 512, BF16, "sq")]])
    self.rs_ring = Ring([self.alloc(512, F32, f"rs{i}") for i in range(2)])
    if rope:
        stg_a, stg_r = self.alloc(512, F32, "stg")
        for nm in ("mask_prev4", "mask_next4"):
            self.dma(stg_a[:, 0:512], self.dram[nm], w=[stg_r])
            self.cp("dve", tables[nm][0], stg_a[:, 0:512], r=[stg_r], w=[tables[nm][1]])
        self.dma(esk[0], self.dram["a_sink_b"], w=[esk[1]])
        self.act(esk[0], esk[0], AF.Exp, r=[esk[1]], w=[esk[1]])
    U, U_r = self.make_u_full((l * 4 + 0) * KD, Ua)
    self.release()
    self.p.barrier()
    self.attn_proj(U, U_r, w_in, w_sw, QT, qt_r, KLO, KUP, k2_r, V2, v2_r, rope)
    self.release()
    self.p.barrier()
    self.mark()
    if rope:
        def pairs_fn(g, i):
            out = []
            if i > 0:
                out.append((i - 1, "mask_prev4"))
            out.append((i, None))
            if i < 15:
                out.append((i + 1, "mask_next4"))
            return out
        self.band_attn(QT, qt_r, KLO, KUP, k2_r, V2, v2_r, pairs_fn, tables, ASCALE, esk)
    else:
        keys = na_table_keys()
        stg_ring = Ring([self.alloc(512, F32, f"nastg{i}") for i in range(2)])
        for kx in keys:
            a, r_ = self.alloc(512, BF16, "natab")
            tables[kx] = (a, r_)

        def prologue(g):
            for ti, kx in enumerate(keys):
                st, st_r = stg_ring.next()
                self.dma(st, self.dram["na_tab"][g, ti], w=[st_r])
                self.act(tables[kx][0], st, AF.Exp, r=[st_r], w=[tables[kx][1]])

        def pairs_fn(g, i):
            if 2 <= i <= 13:
                return [(i + d_, ("i", d_)) for d_ in (-2, -1, 0, 1, 2)]
            js = range(0, 4) if i < 2 else range(12, 16)
            return [(j, ("e", i, j)) for j in js]
        self.band_attn(QT, qt_r, KLO, KUP, k2_r, V2, v2_r, pairs_fn, tables, ASCALE, None, prologue)
    self.release()
    self.release()
    chunks = []
    for p_ in range(8):
        g = p_ // 2
        chunks.append((lambda tt, p_=p_: QT[:, p_, tt * 512:(tt + 1) * 512],
                       lambda tt, g=g: [qt_r[g][i] for i in range(tt * 4, tt * 4 + 4)], 128, p_ * 128))
    return chunks


MAMBA_TILES = [(0, 508), (508, 508), (1016, 508), (1524, 508), (2032, 16)]


def k_mixer_mamba(self, l):
    d = self.dram
    w_in = self.wb["c_w_in"][0]
    if not hasattr(self, "c_catm"):
        self.c_catm = self.nc.dram_tensor("c_catm", [32, 64, T], BF16).ap()
    catm = self.c_catm
    g0 = (l * 4 + 0) * KD
    self.mark()
    C = {}
    C["trif"] = self.alloc(128, F32, "trif")
    C["trib"] = self.alloc(128, F32, "trib")
    self.dma(C["trif"][0], d["tri_f"], w=[C["trif"][1]])
    self.dma(C["trib"][0], d["tri_b"], w=[C["trib"][1]])
    C["cw"] = self.alloc(5 * 32, F32, "c_cw")
    self.dma(C["cw"][0], d["c_conv_w_t"], w=[C["cw"][1]])
    C["cb"] = self.alloc(32, F32, "c_cb")
    self.dma(C["cb"][0], d["c_conv_b_t"], w=[C["cb"][1]])
    C["dcol"] = self.alloc(32, F32, "c_d")
    self.dma(C["dcol"][0], d["c_d_b"], w=[C["dcol"][1]])
    C["gcol"] = self.alloc(32, F32, "c_ng")
    self.dma(C["gcol"][0][0:64, :], d["c_norm_g_t"], w=[C["gcol"][1]])
    C["ones64"] = self.alloc(64, BF16, "ones64")
    self.ms("pool", C["ones64"][0], 1.0 / 256.0, w=[C["ones64"][1]])
    aneg, aneg_r = self.alloc(64, F32, "aneg")
    self.dma(aneg, d["c_a_log_b"], w=[aneg_r])
    self.act(aneg, aneg, AF.Exp, r=[aneg_r], w=[aneg_r])
    self.ts("dve", aneg, aneg, -1.0, None, ALU.mult, None, r=[aneg_r], w=[aneg_r])
    dtb, dtb_r = self.alloc(64, F32, "dtb")
    self.dma(dtb, d["c_dt_bias_b"], w=[dtb_r])
    dt_a, dt_r = self.alloc(16 * 64, F32, "dt")
    dt = _v3(dt_a, 16)
    dtA_a, dtA_r = self.alloc(16 * 64, F32, "dtA")
    dtA = _v3(dtA_a, 16)
    ncs_a, ncs_r = self.alloc(16 * 64, F32, "ncs")
    ncs = _v3(ncs_a, 16)
    C["dt"] = (dt, dt_r)
    C["dtA"] = (dtA, dtA_r)
    C["ncs"] = (ncs, ncs_r)
    trif, trif_r = C["trif"]
    trib, trib_r = C["trib"]
    self.mark()
    self.sq_ring = Ring([(_v3(a, KD), r) for a, r in [self.alloc(KD * 512, BF16, "sq")]])
    self.rs_ring = Ring([self.alloc(512, F32, f"rs{i}") for i in range(2)])
    Ua, U_r = self.alloc(KD * 512, BF16, "Utile")
    U = _v3(Ua, KD)
    wdt_a, wdt_r = self.alloc(KD * 64, BF16, "wdt")
    wdt = _v3(wdt_a, KD)
    self.wload(wdt, wdt_r, w_in[:, 6144:6208].rearrange("(k p) n -> p k n", p=128))
    if not hasattr(self, "c_ud"):
        self.c_ud = self.nc.dram_tensor("c_ud", [128, KD, T + 4], BF16).ap()
        self.c_ud_r = Reg("c_ud")
    Ud = self.c_ud
    C["Ud"] = (Ud, self.c_ud_r)
    zt_a, zt_r = self.alloc(KD * 2, BF16, "zpad")
    zt = _v3(zt_a, KD)
    self.ms("pool", zt_a, 0.0, w=[zt_r])
    self.dma(Ud[:, :, 0:2], zt, r=[zt_r], w=[self.c_ud_r])
    self.dma(Ud[:, :, T + 2:T + 4], zt, r=[zt_r], w=[self.c_ud_r])
    for tt in range(4):
        self.norm_u(tt * 512, (tt + 1) * 512, g0, U, U_r)
        self.dma(Ud[:, :, 2 + tt * 512:2 + (tt + 1) * 512], U, r=[U_r], w=[self.c_ud_r])
        for b4 in range(4):
            blk = tt * 4 + b4
            ps, pr = self.psum()
            for k in range(KD):
                self.mm(ps[:, 0:64], U[:, k, b4 * 128:(b4 + 1) * 128], wdt[:, k, :], k == 0, k == KD - 1,
                        r=[U_r, wdt_r], w=[pr])
            self.tt("dve", dt[:, blk, :], ps[:, 0:64], dtb, ALU.add, r=[pr, dtb_r], w=[dt_r])
    self.act(dt_a, dt_a, AF.Exp, r=[dt_r], w=[dt_r])
    self.act(dt_a, dt_a, AF.Ln, r=[dt_r, self.onec_r], w=[dt_r], bias=self.onec[:, 0:1])
    self.tt("dve", dtA, dt, aneg.unsqueeze(1).broadcast_to([128, 16, 64]), ALU.mult, r=[dt_r, aneg_r], w=[dtA_r])
    for blk in range(16):
        ps, pr = self.psum()
        self.mm(ps[:, 0:32], trif, dtA[:, blk, 0:32], True, True, r=[trif_r, dtA_r], w=[pr])
        self.mm(ps[:, 32:64], trib, dtA[:, blk, 32:64], True, True, r=[trib_r, dtA_r], w=[pr])
        self.ts("dve", ncs[:, blk, :], ps[:, 0:64], -1.0, None, ALU.mult, None, r=[pr], w=[ncs_r])
    self.release()
    self.p.barrier()
    a_, r_ = self.alloc(4 * T, F32, "Yacc")
    C["Yacc"] = (_v3(a_, 4), r_)
    a_, r_ = self.alloc(4 * T, BF16, "zs")
    C["zs"] = (_v3(a_, 4), r_)
    a_, r_ = self.alloc(16 * 256, BF16, "xs_tok")
    C["xtok"] = (_v3(a_, 16), r_)
    a_, r_ = self.alloc(16 * 128, BF16, "B_tok")
    C["btok"] = (_v3(a_, 16), r_)
    C["BT"] = self.alloc(T, BF16, "BT")
    C["CT"] = self.alloc(T, BF16, "CT")
    for g in range(8):
        self.mark()
        self.mamba_p1(g, g0, w_in, C)
        self.release()
        self.p.barrier()
        self.mark()
        self.mamba_p2(g, C)
        self.release()
        self.p.barrier()
        self.mark()
        self.mamba_p3(g, C, catm)
        self.release()
        self.p.barrier()
    self.release()
    ring = Ring([self.alloc(512, BF16, f"cld{i}") for i in range(6)])
    chunks = []
    for hh in range(32):
        def get(tt, hh=hh):
            a, a_r = ring.next()
            self.dma(a[0:64, :], catm[hh, :, tt * 512:(tt + 1) * 512], r=[self.catm_r], w=[a_r])
            return a[0:64, :], [a_r]
        chunks.append((get, None, 64, hh * 64))
    return chunks


def k_mamba_p1(self, g, g0, w_in, C):
    cw, cw_r = C["cw"]
    cbias, cbias_r = C["cb"]
    zs, zs_r = C["zs"]
    xtok, xtok_r = C["xtok"]
    btok, btok_r = C["btok"]
    BT, BT_r = C["BT"]
    CT, CT_r = C["CT"]
    Ud, ud_r = C["Ud"]
    u_ring = Ring([(lambda ar: (_v3(ar[0], KD), ar[1]))(self.alloc(KD * 512, BF16, f"Utile{i}")) for i in range(2)])
    xsT = [self.alloc(T, BF16, f"xsT{i}") for i in range(2)]
    wc_a, wc_r = self.alloc(KD * 512, BF16, "wc")
    wc = _v3(wc_a, KD)
    wz_a, wz_r = self.alloc(KD * 256, BF16, "wz")
    wz = _v3(wz_a, KD)
    acc_ring = Ring([self.alloc(512, F32, f"cacc{i}") for i in range(2)])
    self.wload(wc[:, :, 0:256], wc_r, w_in[:, 2048 + g * 256:2048 + (g + 1) * 256].rearrange("(k p) n -> p k n", p=128))
    self.wload(wc[:, :, 256:384], wc_r, w_in[:, 4096 + g * 128:4096 + (g + 1) * 128].rearrange("(k p) n -> p k n", p=128))
    self.wload(wc[:, :, 384:512], wc_r, w_in[:, 5120 + g * 128:5120 + (g + 1) * 128].rearrange("(k p) n -> p k n", p=128))
    self.wload(wz, wz_r, w_in[:, g * 256:(g + 1) * 256].rearrange("(k p) n -> p k n", p=128))
    dsts = [(xsT[0][0], xsT[0][1], 2 * g), (xsT[1][0], xsT[1][1], 2 * g + 1), (BT, BT_r, 16 + g), (CT, CT_r, 24 + g)]
    for (t0, n) in MAMBA_TILES:
        U, U_r = u_ring.next()
        self.dma(U[:, :, 0:n + 4], Ud[:, :, t0:t0 + n + 4], r=[ud_r], w=[U_r])
        for ci, (dst, dst_r, cch) in enumerate(dsts):
            ps, pr = self.psum()
            for k in range(KD):
                self.mm(ps[:, 0:n + 4], wc[:, k, ci * 128:(ci + 1) * 128], U[:, k, 0:n + 4], k == 0, k == KD - 1,
                        r=[wc_r, U_r], w=[pr])
            acc, acc_r = acc_ring.next()
            A = acc[:, 0:n]
            self.act(A, ps[:, 2:n + 2], AF.Identity, r=[pr, cw_r, cbias_r], w=[acc_r],
                     bias=cbias[:, cch:cch + 1], scale=cw[:, 2 * 32 + cch:2 * 32 + cch + 1])
            for j in (0, 1, 3, 4):
                self.stt(A, ps[:, j:j + n], cw[:, j * 32 + cch:j * 32 + cch + 1], A, ALU.mult, ALU.add,
                         r=[pr, cw_r, acc_r], w=[acc_r])
            self.act(dst[:, t0:t0 + n], A, AF.Silu, r=[acc_r], w=[dst_r])
        for j in range(4):
            ps, pr = self.psum()
            for k in range(KD):
                self.mm(ps[0:64, 0:n], wz[:, k, j * 64:(j + 1) * 64], U[:, k, 2:n + 2], k == 0, k == KD - 1,
                        r=[wz_r, U_r], w=[pr])
            self.act(zs[0:64, j, t0:t0 + n], ps[0:64, 0:n], AF.Silu, r=[pr], w=[zs_r])
    for src, src_r, dstv, dst_r, c0 in ((xsT[0][0], xsT[0][1], xtok, xtok_r, 0),
                                        (xsT[1][0], xsT[1][1], xtok, xtok_r, 128),
                                        (BT, BT_r, btok, btok_r, 0)):
        for b4 in range(4):
            psf, pr = self.psum()
            psb = psf.bitcast(BF16)
            for q in range(4):
                blk = b4 * 4 + q
                self.tr(psb[:, q * 128:(q + 1) * 128], src[:, blk * 128:(blk + 1) * 128], self.identb,
                        r=[src_r, self.identb_r], w=[pr])
            self.cp("act", dstv[:, b4 * 4:b4 * 4 + 4, c0:c0 + 128], _v3(psb[:, 0:512], 4), r=[pr], w=[dst_r])


def k_mamba_p2(self, g, C):
    dcol, dcol_r = C["dcol"]
    dt, dt_r = C["dt"]
    dtA, dtA_r = C["dtA"]
    ncs, ncs_r = C["ncs"]
    Yacc, Y_r = C["Yacc"]
    xtok, xtok_r = C["xtok"]
    btok, btok_r = C["btok"]
    BT, BT_r = C["BT"]
    CT, CT_r = C["CT"]
    tri = [C["trif"], C["trib"]]

    def ring4(n, dtp, name, cnt=2):
        return Ring([(lambda ar: (_v3(ar[0], 4), ar[0], ar[1]))(self.alloc(n, dtp, f"{name}{i}")) for i in range(cnt)])

    dI_a, dI_r = self.alloc(4 * 128, BF16, "dI")
    dI = _v3(dI_a, 4)
    cbm_ring = Ring([self.alloc(128, F32, f"cbm{i}") for i in range(2)])
    bc4_ring = ring4(512, F32, "bc4")
    tmp_ring = ring4(512, F32, "ctmp")
    D_ring = ring4(512, F32, "cD")
    E_ring = ring4(512, F32, "cE")
    MT_ring = ring4(512, BF16, "MT")
    Cs_ring = ring4(512, BF16, "CsT")
    xdt_ring = ring4(256, BF16, "xdt")
    xdw_ring = ring4(256, BF16, "xdw")
    st_ring = ring4(256, F32, "state")
    stb_ring = ring4(256, BF16, "stateb")
    for j in range(4):
        hh = g * 4 + j
        self.ts("dve", dI[:, j, :], self.identf, dcol[:, hh:hh + 1], None, ALU.mult, None,
                r=[self.identf_r, dcol_r], w=[dI_r])
    for dr in range(2):
        c0 = dr * 32 + g * 4
        order = list(range(16)) if dr == 0 else list(range(15, -1, -1))
        tend = 127 if dr == 0 else 0
        trm, trm_r = tri[dr]
        st_prev = None
        stb_prev = None
        for oi, blk in enumerate(order):
            bs = slice(blk * 128, (blk + 1) * 128)
            psc, psc_r = self.psum()
            self.mm(psc[:, 0:128], BT[:, bs], CT[:, bs], True, True, r=[BT_r, CT_r], w=[psc_r])
            cbm, cbm_r = cbm_ring.next()
            self.tt("dve", cbm, psc[:, 0:128], trm, ALU.mult, r=[psc_r, trm_r], w=[cbm_r])
            bc4, bc4_a, bc4_r = bc4_ring.next()
            self.cp("pool", bc4, dtA[:, blk, c0:c0 + 4].unsqueeze(2).broadcast_to([128, 4, 128]),
                    r=[dtA_r], w=[bc4_r])
            pcs, pcs_r = self.psum()
            pcs4 = _v3(pcs[:, 0:512], 4)
            for j in range(4):
                self.mm(pcs4[:, j, :], bc4[:, j, :], trm, True, True, r=[bc4_r, trm_r], w=[pcs_r])
            tmp, tmp_a, tmp_r = tmp_ring.next()
            for j in range(4):
                self.ts("dve", tmp[:, j, :], pcs4[:, j, :], ncs[:, blk, c0 + j:c0 + j + 1], 0.0, ALU.add, ALU.min,
                        r=[pcs_r, ncs_r], w=[tmp_r])
            Dm, D_a, D_r = D_ring.next()
            Em, E_a, E_r = E_ring.next()
            self.act(D_a, tmp_a, AF.Exp, r=[tmp_r], w=[D_r])
            self.act(E_a, pcs[:, 0:512], AF.Exp, r=[pcs_r], w=[E_r])
            MT, MT_a, MT_r = MT_ring.next()
            self.tt("dve", MT, Dm, cbm.unsqueeze(1).broadcast_to([128, 4, 128]), ALU.mult,
                    r=[D_r, cbm_r], w=[MT_r])
            xdt, xdt_a, xdt_r = xdt_ring.next()
            self.tt("pool", xdt, xtok[:, blk, :].rearrange("p (j q) -> p j q", j=4),
                    dt[:, blk, c0:c0 + 4].unsqueeze(2).broadcast_to([128, 4, 64]), ALU.mult,
                    r=[xtok_r, dt_r], w=[xdt_r])
            if oi > 0:
                CsT, Cs_a, Cs_r = Cs_ring.next()
                self.tt("pool", CsT, Em, CT[:, bs].unsqueeze(1).broadcast_to([128, 4, 128]), ALU.mult,
                        r=[E_r, CT_r], w=[Cs_r])
            py, py_r = self.psum()
            py4 = _v3(py[:, 0:512], 4)
            for j in range(4):
                seq = [(xdt[:, j, :], MT[:, j, :], [xdt_r, MT_r])]
                if oi > 0:
                    seq.append((stb_prev[0][:, j, :], CsT[:, j, :], [stb_prev[2], Cs_r]))
                if dr == 0:
                    seq.append((xtok[:, blk, j * 64:(j + 1) * 64], dI[:, j, :], [xtok_r, dI_r]))
                for si, (lt, rh, rg) in enumerate(seq):
                    self.mm(py4[0:64, j, :], lt, rh, si == 0, si == len(seq) - 1, r=rg, w=[py_r])
            if dr == 0:
                self.cp("act", Yacc[0:64, :, bs], py4[0:64, :, :], r=[py_r], w=[Y_r])
            else:
                self.tt("dve", Yacc[0:64, :, bs], Yacc[0:64, :, bs], py4[0:64, :, :], ALU.add, r=[py_r, Y_r], w=[Y_r])
            if oi < 15:
                xdw, xdw_a, xdw_r = xdw_ring.next()
                self.tt("pool", xdw, xdt, Dm[:, :, tend:tend + 1].broadcast_to([128, 4, 64]), ALU.mult,
                        r=[xdt_r, D_r], w=[xdw_r])
                pu, pu_r = self.psum()
                self.mm(pu[:, 0:256], btok[:, blk, :], xdw_a, True, True, r=[btok_r, xdw_r], w=[pu_r])
                st_new = st_ring.next()
                if oi == 0:
                    self.cp("act", st_new[1], pu[:, 0:256], r=[pu_r], w=[st_new[2]])
                else:
                    self.tt("pool", st_new[0], st_prev[0], Em[:, :, tend:tend + 1].broadcast_to([128, 4, 64]), ALU.mult,
                            r=[st_prev[2], E_r], w=[st_new[2]])
                    self.tt("dve", st_new[1], st_new[1], pu[:, 0:256], ALU.add, r=[st_new[2], pu_r], w=[st_new[2]])
                stb_new = stb_ring.next()
                self.cp("act", stb_new[1], st_new[1], r=[st_new[2]], w=[stb_new[2]])
                st_prev, stb_prev = st_new, stb_new


def k_mamba_p3(self, g, C, catm):
    Yacc, Y_r = C["Yacc"]
    zs, zs_r = C["zs"]
    ones64, ones64_r = C["ones64"]
    gcol, gcol_r = C["gcol"]
    ysq_a, ysq_r = self.alloc(4 * 512, BF16, "ysq")
    ysq = _v3(ysq_a, 4)
    co_ring = Ring([(lambda ar: (_v3(ar[0], 4), ar[1]))(self.alloc(4 * 512, BF16, f"cato{i}")) for i in range(2)])
    frs, frs_r = self.alloc(512, F32, "frs")
    for tt in range(4):
        sl = slice(tt * 512, (tt + 1) * 512)
        self.tt("dve", Yacc[0:64, :, sl], Yacc[0:64, :, sl], zs[0:64, :, sl], ALU.mult, r=[Y_r, zs_r], w=[Y_r])
        self.act(ysq[0:64, :, :], Yacc[0:64, :, sl], AF.Square, r=[Y_r], w=[ysq_r])
        ps, pr = self.psum()
        for j in range(4):
            self.mm(ps[0:64, 0:512], ones64[0:64, :], ysq[0:64, j, :], j == 0, j == 3, r=[ones64_r, ysq_r], w=[pr])
        self.act(frs[0:64, :], ps[0:64, 0:512], AF.Ln, r=[pr, self.epsc_r], w=[frs_r], bias=self.epsc[0:64, 0:1])
        self.act(frs[0:64, :], frs[0:64, :], AF.Exp, r=[frs_r], w=[frs_r], scale=-0.5)
        co, co_r = co_ring.next()
        for j in range(4):
            hh = g * 4 + j
            self.stt(co[0:64, j, :], Yacc[0:64, j, sl], gcol[0:64, hh:hh + 1], frs[0:64, :], ALU.mult, ALU.mult,
                     r=[Y_r, gcol_r, frs_r], w=[co_r])
        self.dma(catm[g * 4:(g + 1) * 4, :, sl].rearrange("h p t -> p h t"), co[0:64, :, :],
                 r=[co_r], w=[self.catm_r])


HG_C = 32
HG_NC = T // HG_C


def k_mixer_hgrn(self, l):
    d = self.dram
    w_in = self.wb["b_w_in"][0]
    g0 = (l * 4 + 0) * KD
    cat_a, _ = self.alloc(8 * T, BF16, "hgcat")
    CAT = _v3(cat_a, 8)
    cat_r = [[Reg(f"hc{h}_{t}") for t in range(4)] for h in range(8)]
    self.mark()
    C = {}
    C["mk"] = [self.alloc(128, F32, "hgmf"), self.alloc(128, F32, "hgmb")]
    self.dma(C["mk"][0][0], d["hg_mask_f"], w=[C["mk"][0][1]])
    self.dma(C["mk"][1][0], d["hg_mask_b"], w=[C["mk"][1][1]])
    C["cmask"] = self.alloc(4, F32, "hgcm")
    self.dma(C["cmask"][0], d["hg_cmask"], w=[C["cmask"][1]])
    C["bng"] = self.alloc(8, F32, "b_ng")
    self.dma(C["bng"][0], d["b_norm_g_t"], w=[C["bng"][1]])
    C["ones128"] = self.alloc(128, BF16, "ones128")
    self.ms("pool", C["ones128"][0], 1.0 / 128.0, w=[C["ones128"][1]])
    lg, lg_r = self.alloc(64, F32, "lblog")
    self.dma(lg, d["b_lb_logits_t"], w=[lg_r])
    self.act(lg, lg, AF.Exp, r=[lg_r], w=[lg_r])
    lg4 = lg.rearrange("p (a b k) -> p a b k", a=2, b=4)
    den, den_r = self.alloc(16, F32, "lbden")
    den3 = den.rearrange("p (a k) -> p a k", a=2)
    num, num_r = self.alloc(16, F32, "lbnum")
    num3 = num.rearrange("p (a k) -> p a k", a=2)
    self.tt("dve", den3, lg4[:, :, 0, :], lg4[:, :, 1, :], ALU.add, r=[lg_r], w=[den_r])
    self.tt("dve", den3, den3, lg4[:, :, 2, :], ALU.add, r=[lg_r, den_r], w=[den_r])
    self.tt("dve", den3, den3, lg4[:, :, 3, :], ALU.add, r=[lg_r, den_r], w=[den_r])
    self.ms("dve", num, 0.0, w=[num_r])
    for dd in range(1, l + 1):
        self.tt("dve", num3, num3, lg4[:, :, dd, :], ALU.add, r=[lg_r, num_r], w=[num_r])
    self.rcp(den, den, r=[den_r], w=[den_r])
    lb, lb_r = self.alloc(16, F32, "lb")
    oml, oml_r = self.alloc(16, F32, "oml")
    self.tt("dve", lb, num, den, ALU.mult, r=[num_r, den_r], w=[lb_r])
    self.ts("dve", oml, lb, -1.0, 1.0, ALU.mult, ALU.add, r=[lb_r], w=[oml_r])
    C["lb"] = (lb, lb_r)
    C["oml"] = (oml, oml_r)
    C["qT"] = self.alloc(T, BF16, "hg_q")
    C["sg"] = [self.alloc(T, F32, "hg_sgf"), self.alloc(T, F32, "hg_sgb")]
    C["gs"] = self.alloc(T, BF16, "hg_gs")
    a_, r_ = self.alloc(16 * 128, BF16, "hg_vtok")
    C["vtok"] = (_v3(a_, 16), r_)
    C["Oacc"] = self.alloc(T, F32, "hg_O")
    if not hasattr(self, "b_ud"):
        self.b_ud = self.nc.dram_tensor("b_ud", [128, KD, T], BF16).ap()
        self.b_ud_r = Reg("b_ud")
    C["Ud"] = (self.b_ud, self.b_ud_r)
    self.mark()
    self.sq_ring = Ring([(_v3(a, KD), r) for a, r in [self.alloc(KD * 512, BF16, "sq")]])
    self.rs_ring = Ring([self.alloc(512, F32, f"rs{i}") for i in range(2)])
    up_ring = Ring([(lambda ar: (_v3(ar[0], KD), ar[1]))(self.alloc(KD * 512, BF16, f"Upre{i}")) for i in range(2)])
    for tt in range(4):
        Up, Up_r = up_ring.next()
        self.norm_u(tt * 512, (tt + 1) * 512, g0, Up, Up_r)
        self.dma(self.b_ud[:, :, tt * 512:(tt + 1) * 512], Up, r=[Up_r], w=[self.b_ud_r])
    self.release()
    self.p.barrier()
    for h in range(8):
        self.mark()
        self.hgrn_p1(h, g0, w_in, C)
        self.release()
        self.p.barrier()
        for dr in range(2):
            self.mark()
            self.hgrn_p2(h, dr, C)
            self.release()
            self.p.barrier()
        self.mark()
        self.hgrn_p3(h, C, CAT, cat_r)
        self.release()
        self.p.barrier()
    self.release()
    chunks = []
    for h in range(8):
        chunks.append((lambda tt, h=h: CAT[:, h, tt * 512:(tt + 1) * 512],
                       lambda tt, h=h: [cat_r[h][tt]], 128, h * 128))
    return chunks


def k_hgrn_p1(self, h, g0, w_in, C):
    qT, qT_r = C["qT"]
    gs, gs_r = C["gs"]
    vtok, vtok_r = C["vtok"]
    Ud, ud_r = C["Ud"]
    u_ring = Ring([(lambda ar: (_v3(ar[0], KD), ar[1]))(self.alloc(KD * 512, BF16, f"Utile{i}")) for i in range(2)])
    w5_a, w5_r = self.alloc(KD * 5 * 128, BF16, "hgw")
    w5 = w5_a.rearrange("p (k c n) -> p k c n", k=KD, c=5)
    for c in range(5):
        self.wload(w5[:, :, c, :], w5_r, w_in[:, c * 1024 + h * 128:c * 1024 + (h + 1) * 128].rearrange(
            "(k p) n -> p k n", p=128))
    for tt in range(4):
        sl = slice(tt * 512, (tt + 1) * 512)
        U, U_r = u_ring.next()
        self.dma(U, Ud[:, :, sl], r=[ud_r], w=[U_r])
        for c, (dst, dst_r, fn) in ((0, (qT, qT_r, AF.Silu)), (2, (C["sg"][0][0], C["sg"][0][1], AF.Sigmoid)),
                                    (3, (C["sg"][1][0], C["sg"][1][1], AF.Sigmoid)), (4, (gs, gs_r, AF.Silu))):
            ps, pr = self.psum()
            for k in range(KD):
                self.mm(ps[:, 0:512], w5[:, k, c, :], U[:, k, :], k == 0, k == KD - 1, r=[w5_r, U_r], w=[pr])
            self.act(dst[:, sl], ps[:, 0:512], fn, r=[pr], w=[dst_r])
        for b4 in range(4):
            blk = tt * 4 + b4
            ps, pr = self.psum()
            for k in range(KD):
                self.mm(ps[:, 0:128], U[:, k, b4 * 128:(b4 + 1) * 128], w5[:, k, 1, :], k == 0, k == KD - 1,
                        r=[w5_r, U_r], w=[pr])
            self.cp("dve", vtok[:, blk, :], ps[:, 0:128], r=[pr], w=[vtok_r])


def k_hgrn_p2(self, h, dr, C):
    qT, qT_r = C["qT"]
    sg, sg_r = C["sg"][dr]
    vtok, vtok_r = C["vtok"]
    Oacc, O_r = C["Oacc"]
    lb, lb_r = C["lb"]
    oml, oml_r = C["oml"]
    mk, mk_r = C["mk"][dr]
    cmask, cmask_r = C["cmask"]
    col = dr * KD + h
    NC_, CL = HG_NC, HG_C
    f_, f_r = self.alloc(T, F32, "hg_f")
    b0, b0_r = self.alloc(T, F32, "hg_b0")
    b1, b1_r = self.alloc(T, F32, "hg_b1")
    qd, qd_r = self.alloc(T, BF16, "hg_qd")
    kd, kd_r = self.alloc(T, BF16, "hg_kd")
    ke, ke_r = self.alloc(T, BF16, "hg_ke")
    kt_a, kt_r = self.alloc(16 * 128, BF16, "hg_ketok")
    ketok = _v3(kt_a, 16)
    dec, dec_r = self.alloc(NC_, F32, "hg_dec")
    self.ts("dve", f_, sg, oml[:, col:col + 1], lb[:, col:col + 1], ALU.mult, ALU.add, r=[sg_r, oml_r, lb_r], w=[f_r])
    self.act(b0, f_, AF.Ln, r=[f_r], w=[b0_r])
    self.ts("pool", f_, f_, -1.0, 1.0, ALU.mult, ALU.add, r=[f_r], w=[f_r])
    cur, cur_r, oth, oth_r = b0, b0_r, b1, b1_r
    dd = 1
    while dd < CL:
        cv = cur.rearrange("p (c i) -> p c i", i=CL)
        ov = oth.rearrange("p (c i) -> p c i", i=CL)
        if dr == 0:
            self.tt("dve", ov[:, :, dd:CL], cv[:, :, dd:CL], cv[:, :, 0:CL - dd], ALU.add, r=[cur_r], w=[oth_r])
            self.cp("pool", ov[:, :, 0:dd], cv[:, :, 0:dd], r=[cur_r], w=[oth_r])
        else:
            self.tt("dve", ov[:, :, 0:CL - dd], cv[:, :, 0:CL - dd], cv[:, :, dd:CL], ALU.add, r=[cur_r], w=[oth_r])
            self.cp("pool", ov[:, :, CL - dd:CL], cv[:, :, CL - dd:CL], r=[cur_r], w=[oth_r])
        cur, cur_r, oth, oth_r = oth, oth_r, cur, cur_r
        dd *= 2
    b, b_r, tmp, tmp_r = cur, cur_r, oth, oth_r
    bv = b.rearrange("p (c i) -> p c i", i=CL)
    e_idx = CL - 1 if dr == 0 else 0
    self.act(dec.unsqueeze(2), bv[:, :, e_idx:e_idx + 1], AF.Exp, r=[b_r], w=[dec_r])
    self.act(tmp, b, AF.Exp, r=[b_r], w=[tmp_r])
    self.tt("dve", qd, qT, tmp, ALU.mult, r=[qT_r, tmp_r], w=[qd_r])
    self.act(tmp, b, AF.Exp, r=[b_r], w=[tmp_r], scale=-1.0)
    self.tt("dve", tmp, tmp, f_, ALU.mult, r=[tmp_r, f_r], w=[tmp_r])
    self.cp("pool", kd, tmp, r=[tmp_r], w=[kd_r])
    self.tt("dve", ke.rearrange("p (c i) -> p c i", i=CL), tmp.rearrange("p (c i) -> p c i", i=CL),
            dec.unsqueeze(2).broadcast_to([128, NC_, CL]), ALU.mult, r=[tmp_r, dec_r], w=[ke_r])
    for b4 in range(4):
        psf, pr = self.psum()
        psb = psf.bitcast(BF16)
        for q in range(4):
            blk = b4 * 4 + q
            self.tr(psb[:, q * 128:(q + 1) * 128], ke[:, blk * 128:(blk + 1) * 128], self.identb,
                    r=[ke_r, self.identb_r], w=[pr])
        self.cp("act", ketok[:, b4 * 4:b4 * 4 + 4, :], _v3(psb[:, 0:512], 4), r=[pr], w=[kt_r])
    am_ring = Ring([(lambda ar: (_v3(ar[0], 4), ar[1]))(self.alloc(512, BF16, f"hg_am{i}")) for i in range(2)])
    vm_ring = Ring([(lambda ar: (_v3(ar[0], 4), ar[1]))(self.alloc(512, BF16, f"hg_vm{i}")) for i in range(2)])
    S_ring = Ring([self.alloc(128, F32, f"hg_S{i}") for i in range(6)])
    Sb_ring = Ring([self.alloc(128, BF16, f"hg_Sb{i}") for i in range(6)])
    blocks = list(range(16)) if dr == 0 else list(range(15, -1, -1))
    S_prev = None
    Sb_prev = None
    nper = 128 // CL
    for g4 in range(4):
        grp = blocks[g4 * 4:(g4 + 1) * 4]
        pa, pa_r = self.psum()
        pa4 = _v3(pa[:, 0:512], 4)
        for qi, blk in enumerate(grp):
            bs = slice(blk * 128, (blk + 1) * 128)
            self.mm(pa4[:, qi, :], kd[:, bs], qd[:, bs], True, True, r=[kd_r, qd_r], w=[pa_r])
        am, am_r = am_ring.next()
        self.tt("dve", am, pa4, mk.unsqueeze(1).broadcast_to([128, 4, 128]), ALU.mult, r=[pa_r, mk_r], w=[am_r])
        for qi, blk in enumerate(grp):
            bs = slice(blk * 128, (blk + 1) * 128)
            vm, vm_r = vm_ring.next()
            self.tt("pool", vm, vtok[:, blk, :].unsqueeze(1).broadcast_to([128, nper, 128]),
                    cmask.unsqueeze(2).broadcast_to([128, nper, 128]), ALU.mult, r=[vtok_r, cmask_r], w=[vm_r])
            pu, pu_r = self.psum()
            self.mm(pu[:, 0:512], ketok[:, blk, :], vm.rearrange("p c v -> p (c v)"), True, True,
                    r=[kt_r, vm_r], w=[pu_r])
            pu4 = _v3(pu[:, 0:512], 4)
            po, po_r = self.psum()
            chunks_in_blk = list(range(nper)) if dr == 0 else list(range(nper - 1, -1, -1))
            first_mm = True
            self.mm(po[:, 0:128], vtok[:, blk, :], am[:, qi, :], True, False, r=[vtok_r, am_r], w=[po_r])
            for ci, c in enumerate(chunks_in_blk):
                cg = blk * nper + c
                cs_ = slice(c * CL, (c + 1) * CL)
                if Sb_prev is not None:
                    self.mm(po[:, cs_], Sb_prev[0], qd[:, blk * 128 + c * CL:blk * 128 + (c + 1) * CL], False,
                            ci == nper - 1, r=[Sb_prev[1], qd_r], w=[po_r])
                S_new = S_ring.next()
                if S_prev is None:
                    self.cp("dve", S_new[0], pu4[:, c, :], r=[pu_r], w=[S_new[1]])
                else:
                    self.stt(S_new[0], S_prev[0], dec[:, cg:cg + 1], pu4[:, c, :], ALU.mult, ALU.add,
                             r=[S_prev[1], dec_r, pu_r], w=[S_new[1]])
                Sb_new = Sb_ring.next()
                self.cp("act", Sb_new[0], S_new[0], r=[S_new[1]], w=[Sb_new[1]])
                S_prev, Sb_prev = S_new, Sb_new
            if dr == 0:
                self.cp("act", Oacc[:, bs], po[:, 0:128], r=[po_r], w=[O_r])
            else:
                self.tt("dve", Oacc[:, bs], Oacc[:, bs], po[:, 0:128], ALU.add, r=[po_r, O_r], w=[O_r])


def k_hgrn_p3(self, h, C, CAT, cat_r):
    Oacc, O_r = C["Oacc"]
    gs, gs_r = C["gs"]
    bng, bng_r = C["bng"]
    ones128, ones128_r = C["ones128"]
    osq, osq_r = self.alloc(512, BF16, "hg_osq")
    frs, frs_r = self.alloc(512, F32, "hg_frs")
    ot, ot_r = self.alloc(512, F32, "hg_ot")
    for tt in range(4):
        sl = slice(tt * 512, (tt + 1) * 512)
        self.act(osq, Oacc[:, sl], AF.Square, r=[O_r], w=[osq_r])
        ps, pr = self.psum()
        self.mm(ps[:, 0:512], ones128, osq, True, True, r=[ones128_r, osq_r], w=[pr])
        self.act(frs, ps[:, 0:512], AF.Ln, r=[pr, self.epsc_r], w=[frs_r], bias=self.epsc[:, 0:1])
        self.act(frs, frs, AF.Exp, r=[frs_r], w=[frs_r], scale=-0.5)
        self.stt(ot, Oacc[:, sl], bng[:, h:h + 1], frs, ALU.mult, ALU.mult, r=[O_r, bng_r, frs_r], w=[ot_r])
        self.tt("dve", CAT[:, h, sl], ot, gs[:, sl], ALU.mult, r=[ot_r, gs_r], w=[cat_r[h][tt]])


def k_layer(self, l):
    kind = l % 4
    self.mark()
    names = {0: ("a_w_in", "a_w_out", 1536), 1: ("b_w_in", "b_w_out", 5120), 2: ("c_w_in", "c_w_out", 6208),
             3: ("d_w_in", "d_w_out", 1536)}[kind]
    w_in = self.wb[names[0]][0]
    w_out = self.wb[names[1]][0]
    if kind in (0, 3):
        chunks = self.mixer_attn(l, kind)
    elif kind == 2:
        chunks = self.mixer_mamba(l)
    else:
        chunks = self.mixer_hgrn(l)
    self.p.barrier()
    cx_a, _ = self.alloc(4 * T, BF16, "CATX")
    CATX = _v3(cx_a, 4)
    catx_r = [[Reg(f"cx{h}_{t}") for t in range(4)] for h in range(4)]
    skip = self.cfg.get("skip", ())
    if "xattn" in skip:
        self.ms("pool", cx_a, 0.0, w=[r_ for rr_ in catx_r for r_ in rr_])
    else:
        self.xattn(l, w_in, names[2], CATX, catx_r)
    base = max(c[3] + c[2] for c in chunks)
    for h in range(4):
        chunks.append((lambda tt, h=h: CATX[:, h, tt * 512:(tt + 1) * 512],
                       lambda tt, h=h: [catx_r[h][tt]], 128, base + h * 128))
    self.p.barrier()
    self.out_proj(l, chunks, w_out, (l * 4 + 1) * KD)
    self.release()
    self.p.barrier()


K.setup_seq_consts = k_setup_seq_consts
K.mem_norm = k_mem_norm
K.make_u_full = k_make_u_full
K.xattn = k_xattn
K.out_proj = k_out_proj
K.band_attn = k_band_attn
K.attn_proj = k_attn_proj
K.mixer_attn = k_mixer_attn
K.layer = k_layer
K.mixer_mamba = k_mixer_mamba
K.mixer_hgrn = k_mixer_hgrn
K.hgrn_p1 = k_hgrn_p1
K.hgrn_p2 = k_hgrn_p2
K.hgrn_p3 = k_hgrn_p3
K.mamba_p1 = k_mamba_p1
K.mamba_p2 = k_mamba_p2
K.mamba_p3 = k_mamba_p3


def _swap_halves_cols(w, ncols):
    w = np.asarray(w, np.float32)
    r = w[:, :ncols].reshape(w.shape[0], ncols // 64, 2, 32)
    return np.ascontiguousarray(r[:, :, ::-1, :].reshape(w.shape[0], ncols))


def na_table_keys():
    keys = [("i", d_) for d_ in (-2, -1, 0, 1, 2)]
    for i in (0, 1, 14, 15):
        js = range(0, 4) if i < 2 else range(12, 16)
        keys += [("e", i, j) for j in js]
    return keys


def na_tables(rpb):
    rpb = np.asarray(rpb, np.float32)
    out = np.empty((4, 21, 128, 512), np.float32)
    b = np.arange(128)[:, None]
    a = np.arange(128)[None, :]
    for ti, kx in enumerate(na_table_keys()):
        if kx[0] == "i":
            i, j = 6, 6 + kx[1]
        else:
            i, j = kx[1], kx[2]
        krow = 2 * j + b // 64
        kcol = b % 64
        r = 2 * i + a // 64
        qcol = a % 64
        rs = np.clip(r - 4, 0, 24)
        cs = np.clip(qcol - 8, 0, 48)
        valid = (krow >= rs) & (krow < rs + 8) & (kcol >= cs) & (kcol < cs + 16)
        dr = np.clip(krow - r + 7, 0, 14)
        dc = np.clip(kcol - qcol + 15, 0, 30)
        for g in range(4):
            for s_, h in enumerate([4 * g, 4 * g + 2, 4 * g + 1, 4 * g + 3]):
                tab = rpb[h][dr, dc]
                out[g, ti, :, s_ * 128:(s_ + 1) * 128] = np.where(valid, tab, np.float32(-30000.0))
    return out


def host_consts(inputs):
    c = {}
    c["c_ident"] = np.eye(128, dtype=np.float32)
    ng = np.asarray(inputs["norm_g"], np.float32)
    c["norm_g_t"] = np.ascontiguousarray(ng.reshape(4, 4, KD, 128).transpose(3, 0, 1, 2).reshape(128, 4 * 4 * KD))
    cw = np.asarray(inputs["ffn_conv_w"], np.float32)
    c["ffn_conv_w_t"] = np.ascontiguousarray(cw.reshape(4, 3, NF, 128).transpose(3, 0, 1, 2).reshape(128, 4 * 3 * NF))
    cb = np.asarray(inputs["ffn_conv_b"], np.float32)
    c["ffn_conv_b_t"] = np.ascontiguousarray(cb.reshape(4, NF, 128).transpose(2, 0, 1).reshape(128, 4 * NF))
    mg = np.asarray(inputs["mem_norm_g"], np.float32)
    c["mem_norm_g_t"] = np.ascontiguousarray(mg.reshape(KD, 128).T)
    half = 32
    inv = (10000.0 ** (-np.arange(half, dtype=np.float32) / half)).astype(np.float32)
    ang = np.arange(T, dtype=np.float32)[None, :] * inv[:, None]
    cos = np.cos(ang).astype(np.float32)
    sin = np.sin(ang).astype(np.float32)
    c["rope_cos"] = np.ascontiguousarray(np.concatenate([cos, cos, cos, cos], 0))
    c["rope_sin"] = np.ascontiguousarray(np.concatenate([-sin, sin, -sin, sin], 0))
    b = np.arange(128)[:, None]
    a = np.arange(128)[None, :]
    c["mask_prev4"] = np.ascontiguousarray(np.tile((a <= b).astype(np.float32), (1, 4)))
    c["mask_next4"] = np.ascontiguousarray(np.tile((b <= a).astype(np.float32), (1, 4)))
    sk = np.asarray(inputs["a_sink"], np.float32)[0]
    c["a_sink_b"] = np.ascontiguousarray(np.broadcast_to(sk[None, :], (128, 16)))
    c["a_w_in_sw"] = _swap_halves_cols(np.asarray(inputs["a_w_in"], np.float32)[0], 1280)
    c["na_tab"] = na_tables(np.asarray(inputs["d_rpb"], np.float32)[0])
    r_ = np.arange(128)[:, None]
    t_ = np.arange(128)[None, :]
    c["tri_f"] = (r_ <= t_).astype(np.float32)
    c["tri_b"] = (r_ >= t_).astype(np.float32)
    ccw = np.asarray(inputs["c_conv_w"], np.float32)[0]
    c["c_conv_w_t"] = np.ascontiguousarray(ccw.reshape(5, 32, 128).transpose(2, 0, 1).reshape(128, 5 * 32))
    ccb = np.asarray(inputs["c_conv_b"], np.float32)[0]
    c["c_conv_b_t"] = np.ascontiguousarray(ccb.reshape(32, 128).T)
    c["c_dt_bias_b"] = np.ascontiguousarray(np.broadcast_to(np.asarray(inputs["c_dt_bias"], np.float32)[0].reshape(1, 64), (128, 64)))
    c["c_a_log_b"] = np.ascontiguousarray(np.broadcast_to(np.asarray(inputs["c_a_log"], np.float32)[0].reshape(1, 64), (128, 64)))
    c["c_d_b"] = np.ascontiguousarray(np.broadcast_to(np.asarray(inputs["c_d"], np.float32)[0].reshape(1, 32), (128, 32)))
    c["c_norm_g_t"] = np.ascontiguousarray(np.asarray(inputs["c_norm_g"], np.float32)[0].reshape(32, 64).T)
    same = (r_ // HG_C) == (t_ // HG_C)
    c["hg_mask_f"] = (same & (r_ <= t_)).astype(np.float32)
    c["hg_mask_b"] = (same & (r_ >= t_)).astype(np.float32)
    c["hg_cmask"] = ((np.arange(128)[:, None] // HG_C) == np.arange(128 // HG_C)[None, :]).astype(np.float32)
    lbl = np.asarray(inputs["b_lb_logits"], np.float32)
    c["b_lb_logits_t"] = np.ascontiguousarray(lbl.reshape(2, 4, KD, 128).transpose(3, 0, 1, 2).reshape(128, 64))
    c["b_norm_g_t"] = np.ascontiguousarray(np.asarray(inputs["b_norm_g"], np.float32)[0].reshape(KD, 128).T)
    return c


IN_SHAPES = {
    "c_ident": (128, 128), "norm_g_t": (128, 4 * 4 * KD), "ffn_conv_w_t": (128, 4 * 3 * NF),
    "ffn_conv_b_t": (128, 4 * NF), "mem_norm_g_t": (128, KD), "rope_cos": (128, T), "rope_sin": (128, T),
    "mask_prev4": (128, 512), "mask_next4": (128, 512), "a_sink_b": (128, 16), "a_w_in_sw": (D, 1280),
    "na_tab": (4, 21, 128, 512),
    "tri_f": (128, 128), "tri_b": (128, 128), "c_conv_w_t": (128, 160), "c_conv_b_t": (128, 32),
    "hg_mask_f": (128, 128), "hg_mask_b": (128, 128), "hg_cmask": (128, 4), "b_lb_logits_t": (128, 64),
    "b_norm_g_t": (128, KD),
    "c_dt_bias_b": (128, 64), "c_a_log_b": (128, 64), "c_d_b": (128, 32), "c_norm_g_t": (64, 32),
    "ffn_w_in": (4, D, 2 * DFF), "ffn_w_out": (4, DFF, D), "w_mem_kv": (4, D, 1024),
    "a_w_in": (1, D, 2048), "a_w_out": (1, 1536, D),
    "b_w_in": (1, D, 5632), "b_w_out": (1, 1536, D),
    "c_w_in": (1, D, 6720), "c_w_out": (1, 2560, D),
    "d_w_in": (1, D, 2048), "d_w_out": (1, 1536, D),
}
WEIGHTS = ["ffn_w_in", "ffn_w_out", "w_mem_kv", "a_w_in", "a_w_in_sw", "a_w_out", "b_w_in", "b_w_out",
           "c_w_in", "c_w_out", "d_w_in", "d_w_out"]


def build(cfg):
    nc = bass.Bass("TRN2", target_bir_lowering=False)
    stack = contextlib.ExitStack()
    dram = {}
    nseq = cfg.get("nseq", NSEQ)
    used = cfg.get("inputs", list(IN_SHAPES))
    dram["x"] = nc.dram_tensor("x", [nseq, T, D], F32, kind="ExternalInput").ap()
    dram["mem"] = nc.dram_tensor("mem", [nseq, MEM, D], F32, kind="ExternalInput").ap()
    for name in used:
        dram[name] = nc.dram_tensor(name, list(IN_SHAPES[name]), F32, kind="ExternalInput").ap()
    dram["y"] = nc.dram_tensor("y", [nseq, T, D], F32, kind="ExternalOutput").ap()
    with stack:
        k = K(nc, stack, dram)
        k.cfg = cfg
        k.setup()
        k.setup_seq_consts()
        k.precast([w for w in WEIGHTS if w in used])
        k.p.barrier()
        for s in range(nseq):
            k.mark()
            k.load_seq(s)
            k.release()
            k.p.barrier()
            if cfg.get("mixers", True):
                k.mem_norm(s)
                k.p.barrier()
            for l in cfg.get("layers", range(4)):
                if cfg.get("mixers", True):
                    k.layer(l)
                if cfg.get("ffn", True):
                    k.ffn(l)
                    k.p.barrier()
            k.mark()
            k.store_seq(s)
            k.release()
            k.p.barrier()
        k.p.finish()
        k.p.emit(stack)
    return nc, k


N_CORES = 8


def kernel(**inputs):
    x = np.ascontiguousarray(np.asarray(inputs["x"], np.float32))
    mem = np.ascontiguousarray(np.asarray(inputs["mem"], np.float32))
    consts = host_consts(inputs)
    shared = {}
    for name, shp in IN_SHAPES.items():
        src = consts[name] if name in consts else inputs[name]
        shared[name] = np.ascontiguousarray(np.asarray(src, np.float32).reshape(shp))
    nc, _ = build({})
    in_maps = []
    for c in range(N_CORES):
        m = dict(shared)
        m["x"] = x[c * NSEQ:(c + 1) * NSEQ]
        m["mem"] = mem[c * NSEQ:(c + 1) * NSEQ]
        in_maps.append(m)
    res = run_bass_kernel_spmd(nc, in_maps, core_ids=list(range(N_CORES)))
    return np.concatenate([r["y"] for r in res.results], axis=0)
```
